# Optimizing a Trainium2 kernel written in Bass

```python
import jax, jax.numpy as jnp
from jax import lax
import numpy as np

D_MODEL = 1024
BATCH = 2
SEQ = 16384
DEPTH = 4

GRID_W = 64
Q_BLOCK = 128
ROPE_THETA = 10000.0
EPS = 1e-6
D_FF = 2816
GQA_HEADS = 8
GQA_KV_HEADS = 2
GQA_HEAD_DIM = 64
SC_WIDTH = 512
SC_KERNEL = 3
MLA_HEADS = 8
MLA_Q_LORA = 384
MLA_KV_LORA = 256
MLA_NOPE = 64
MLA_ROPE = 32
MLA_V = 64
LRU_WIDTH = 512
LRU_BLOCKS = 8
LRU_BLOCK_W = LRU_WIDTH // LRU_BLOCKS
LRU_CONV = 4
LRU_C = 8.0
N_BRANCH = 4
BRANCH_W = 512
IN_SIZES = (
    GQA_HEADS * GQA_HEAD_DIM,
    GQA_KV_HEADS * GQA_HEAD_DIM,
    GQA_KV_HEADS * GQA_HEAD_DIM,
    3 * SC_WIDTH,
    MLA_Q_LORA,
    MLA_KV_LORA,
    MLA_ROPE,
    2 * LRU_WIDTH,
    N_BRANCH * D_MODEL,
)
IN_TOTAL = sum(IN_SIZES)

kernel_name = "hybrid_gqa_shortconv_mla_rglru_macaron_encoder"


def rmsnorm(x, g):
    xf = x.astype(jnp.float32)
    y = xf * lax.rsqrt(jnp.mean(xf * xf, axis=-1, keepdims=True) + EPS)
    return (y * g.astype(jnp.float32)).astype(x.dtype)


def swiglu_ffn(x, g, wi, wo):
    gate, up = jnp.split(rmsnorm(x, g) @ wi, 2, axis=-1)
    return (jax.nn.silu(gate) * up) @ wo


def rope_angles(pos, dim):
    inv = ROPE_THETA ** (-jnp.arange(0, dim, 2, dtype=jnp.float32) / dim)
    return pos[:, None] * inv[None, :]


def apply_rope_1d(x, ang):
    cos = jnp.cos(ang)[None, :, None, :].astype(x.dtype)
    sin = jnp.sin(ang)[None, :, None, :].astype(x.dtype)
    x1, x2 = jnp.split(x, 2, axis=-1)
    return jnp.concatenate([x1 * cos - x2 * sin, x2 * cos + x1 * sin], axis=-1)


def apply_axial_rope(x, ang_row, ang_col):
    half = x.shape[-1] // 2
    return jnp.concatenate([apply_rope_1d(x[..., :half], ang_row),
                            apply_rope_1d(x[..., half:], ang_col)], axis=-1)


def block_attention(q, k, v):
    b, hk, g, s, d = q.shape
    nb = s // Q_BLOCK
    qb = jnp.moveaxis(q.reshape(b, hk, g, nb, Q_BLOCK, d), 3, 0)
    scale = d ** -0.5

    def one_block(qi):
        sc = jnp.einsum('bkgqd,bksd->bkgqs', qi, k).astype(jnp.float32) * scale
        p = jax.nn.softmax(sc, axis=-1).astype(v.dtype)
        return jnp.einsum('bkgqs,bksv->bkgqv', p, v)

    o = lax.map(one_block, qb)
    return jnp.moveaxis(o, 0, 3).reshape(b, hk, g, s, v.shape[-1])


def dwconv(x, w, bias, left):
    width = w.shape[0]
    s = x.shape[1]
    xp = jnp.pad(x, ((0, 0), (left, width - 1 - left), (0, 0)))
    out = bias
    for tap in range(width):
        out = out + w[tap] * xp[:, tap:tap + s]
    return out


def gqa_mixer(q, k, v, qn, kn, ang_r, ang_c):
    b, s, _ = q.shape
    grp = GQA_HEADS // GQA_KV_HEADS
    q = q.reshape(b, s, GQA_HEADS, GQA_HEAD_DIM)
    k = k.reshape(b, s, GQA_KV_HEADS, GQA_HEAD_DIM)
    v = v.reshape(b, s, GQA_KV_HEADS, GQA_HEAD_DIM)
    q = apply_axial_rope(rmsnorm(q, qn), ang_r, ang_c)
    k = apply_axial_rope(rmsnorm(k, kn), ang_r, ang_c)
    qh = q.reshape(b, s, GQA_KV_HEADS, grp, GQA_HEAD_DIM).transpose(0, 2, 3, 1, 4)
    o = block_attention(qh, k.transpose(0, 2, 1, 3), v.transpose(0, 2, 1, 3))
    return o.transpose(0, 3, 1, 2, 4).reshape(b, s, GQA_HEADS * GQA_HEAD_DIM)


def shortconv_mixer(u, w, bias):
    bg, cg, xs = jnp.split(u, 3, axis=-1)
    return bg * dwconv(cg * xs, w, bias, left=SC_KERNEL // 2)


def mla_mixer(q_lat, kv_lat, k_rope, qa_norm, wq_up, kva_norm, wkv_up, qn, kn, ang_r, ang_c):
    b, s, _ = q_lat.shape
    q = (rmsnorm(q_lat, qa_norm) @ wq_up).reshape(b, s, MLA_HEADS, MLA_NOPE + MLA_ROPE)
    kv = (rmsnorm(kv_lat, kva_norm) @ wkv_up).reshape(b, s, MLA_HEADS, MLA_NOPE + MLA_V)
    k_nope, v = kv[..., :MLA_NOPE], kv[..., MLA_NOPE:]
    k_r = jnp.broadcast_to(k_rope[:, :, None, :], (b, s, MLA_HEADS, MLA_ROPE))
    k = jnp.concatenate([k_nope, k_r], axis=-1)
    q = rmsnorm(q, qn)
    k = rmsnorm(k, kn)
    q = jnp.concatenate([q[..., :MLA_NOPE], apply_axial_rope(q[..., MLA_NOPE:], ang_r, ang_c)], axis=-1)
    k = jnp.concatenate([k[..., :MLA_NOPE], apply_axial_rope(k[..., MLA_NOPE:], ang_r, ang_c)], axis=-1)
    qh = q.transpose(0, 2, 1, 3)[:, :, None]
    o = block_attention(qh, k.transpose(0, 2, 1, 3), v.transpose(0, 2, 1, 3))
    return o[:, :, 0].transpose(0, 2, 1, 3).reshape(b, s, MLA_HEADS * MLA_V)


def linear_scan(a, bval):
    def combine(left, right):
        a1, b1 = left
        a2, b2 = right
        return a1 * a2, a2 * b1 + b2
    return lax.associative_scan(combine, (a, bval), axis=1)[1]


def rglru_direction(xc, wa, ba, wx, bx, lam, reverse):
    b, s, w = xc.shape
    xh = xc.reshape(b, s, LRU_BLOCKS, LRU_BLOCK_W)
    r = jax.nn.sigmoid(jnp.einsum('bsnc,ncd->bsnd', xh, wa).reshape(b, s, w) + ba)
    i = jax.nn.sigmoid(jnp.einsum('bsnc,ncd->bsnd', xh, wx).reshape(b, s, w) + bx)
    log_a = -LRU_C * r * jax.nn.softplus(-lam)
    a = jnp.exp(log_a)
    mult = jnp.sqrt(-jnp.expm1(2.0 * log_a))
    bval = mult * (i * xc)
    if reverse:
        return jnp.flip(linear_scan(jnp.flip(a, 1), jnp.flip(bval, 1)), 1)
    return linear_scan(a, bval)


def rglru_mixer(u, conv_w, conv_b, wa, ba, wx, bx, lam):
    gate, xb = jnp.split(u, 2, axis=-1)
    xc = dwconv(xb, conv_w, conv_b, left=LRU_CONV // 2)
    h = (rglru_direction(xc, wa[0], ba[0], wx[0], bx[0], lam[0], False)
         + rglru_direction(xc, wa[1], ba[1], wx[1], bx[1], lam[1], True))
    return jax.nn.gelu(gate) * h


def setup_inputs(seed: int = 0) -> dict:
    key = jax.random.key(seed)
    ks = iter(jax.random.split(key, 32))
    L, D = DEPTH, D_MODEL
    f32 = jnp.float32

    def nrm(shape, fan_in):
        return jax.random.normal(next(ks), shape, f32) * fan_in ** -0.5

    def gain(shape):
        return 1.0 + 0.05 * jax.random.normal(next(ks), shape, f32)

    def bias(shape, scale=0.02):
        return scale * jax.random.normal(next(ks), shape, f32)

    def lru_lambda(shape):
        a0 = jax.random.uniform(next(ks), shape, f32, minval=0.9, maxval=0.999)
        a_base = a0 ** (1.0 / LRU_C)
        return jnp.log(a_base) - jnp.log1p(-a_base)

    return {
        "x": jax.random.normal(next(ks), (BATCH, SEQ, D), f32),
        "ffn1_norm": gain((L, D)),
        "ffn1_wi": nrm((L, D, 2 * D_FF), D),
        "ffn1_wo": nrm((L, D_FF, D), D_FF),
        "mix_norm": gain((L, D)),
        "w_in": nrm((L, D, IN_TOTAL), D),
        "gqa_q_norm": gain((L, GQA_HEAD_DIM)),
        "gqa_k_norm": gain((L, GQA_HEAD_DIM)),
        "sc_conv_w": nrm((L, SC_KERNEL, SC_WIDTH), SC_KERNEL),
        "sc_conv_b": bias((L, SC_WIDTH)),
        "mla_qa_norm": gain((L, MLA_Q_LORA)),
        "mla_wq_up": nrm((L, MLA_Q_LORA, MLA_HEADS * (MLA_NOPE + MLA_ROPE)), MLA_Q_LORA),
        "mla_kva_norm": gain((L, MLA_KV_LORA)),
        "mla_wkv_up": nrm((L, MLA_KV_LORA, MLA_HEADS * (MLA_NOPE + MLA_V)), MLA_KV_LORA),
        "mla_q_norm": gain((L, MLA_NOPE + MLA_ROPE)),
        "mla_k_norm": gain((L, MLA_NOPE + MLA_ROPE)),
        "lru_conv_w": nrm((L, LRU_CONV, LRU_WIDTH), LRU_CONV),
        "lru_conv_b": bias((L, LRU_WIDTH)),
        "lru_wa": nrm((L, 2, LRU_BLOCKS, LRU_BLOCK_W, LRU_BLOCK_W), LRU_BLOCK_W),
        "lru_ba": bias((L, 2, LRU_WIDTH), 0.1),
        "lru_wx": nrm((L, 2, LRU_BLOCKS, LRU_BLOCK_W, LRU_BLOCK_W), LRU_BLOCK_W),
        "lru_bx": bias((L, 2, LRU_WIDTH), 0.1),
        "lru_lambda": lru_lambda((L, 2, LRU_WIDTH)),
        "w_branch": nrm((L, N_BRANCH, BRANCH_W, D), BRANCH_W),
        "w_out": nrm((L, D, D), D),
        "ffn2_norm": gain((L, D)),
        "ffn2_wi": nrm((L, D, 2 * D_FF), D),
        "ffn2_wo": nrm((L, D_FF, D), D_FF),
    }


def reference(x, ffn1_norm, ffn1_wi, ffn1_wo, mix_norm, w_in, gqa_q_norm, gqa_k_norm,
              sc_conv_w, sc_conv_b, mla_qa_norm, mla_wq_up, mla_kva_norm, mla_wkv_up,
              mla_q_norm, mla_k_norm, lru_conv_w, lru_conv_b, lru_wa, lru_ba, lru_wx,
              lru_bx, lru_lambda, w_branch, w_out, ffn2_norm, ffn2_wi, ffn2_wo):
    b, s, d = x.shape
    rows = s // GRID_W
    row_pos = jnp.repeat(jnp.arange(rows, dtype=jnp.float32), GRID_W)
    col_pos = jnp.tile(jnp.arange(GRID_W, dtype=jnp.float32), rows)
    gqa_ang_r = rope_angles(row_pos, GQA_HEAD_DIM // 2)
    gqa_ang_c = rope_angles(col_pos, GQA_HEAD_DIM // 2)
    mla_ang_r = rope_angles(row_pos, MLA_ROPE // 2)
    mla_ang_c = rope_angles(col_pos, MLA_ROPE // 2)
    split_idx = np.cumsum(IN_SIZES)[:-1].tolist()

    for l in range(DEPTH):
        x = x + 0.5 * swiglu_ffn(x, ffn1_norm[l], ffn1_wi[l], ffn1_wo[l])

        h = rmsnorm(x, mix_norm[l])
        (a_q, a_k, a_v, sc_u, c_qlat, c_kvlat, c_krope, lru_u, gate_lin) = jnp.split(
            h @ w_in[l], split_idx, axis=-1)
        o_a = gqa_mixer(a_q, a_k, a_v, gqa_q_norm[l], gqa_k_norm[l], gqa_ang_r, gqa_ang_c)
        o_b = shortconv_mixer(sc_u, sc_conv_w[l], sc_conv_b[l])
        o_c = mla_mixer(c_qlat, c_kvlat, c_krope, mla_qa_norm[l], mla_wq_up[l],
                        mla_kva_norm[l], mla_wkv_up[l], mla_q_norm[l], mla_k_norm[l],
                        mla_ang_r, mla_ang_c)
        o_d = rglru_mixer(lru_u, lru_conv_w[l], lru_conv_b[l], lru_wa[l], lru_ba[l],
                          lru_wx[l], lru_bx[l], lru_lambda[l])
        branches = jnp.stack([o_a, o_b, o_c, o_d], axis=2)
        y = jnp.einsum('bsnc,ncd->bsnd', branches, w_branch[l])
        g = jax.nn.sigmoid(gate_lin.reshape(b, s, N_BRANCH, d))
        merged = jnp.sum(g * y, axis=2)
        x = x + merged @ w_out[l]

        x = x + 0.5 * swiglu_ffn(x, ffn2_norm[l], ffn2_wi[l], ffn2_wo[l])
    return x
```

```python
import numpy as np
from contextlib import ExitStack
import concourse.bass as bass
import concourse.mybir as mybir
from concourse.bass_utils import run_bass_kernel_spmd

F32 = mybir.dt.float32
BF16 = mybir.dt.bfloat16
ALU = mybir.AluOpType
AF = mybir.ActivationFunctionType


class Buf:
    __slots__ = ("name", "w", "r")

    def __init__(self, name=""):
        self.name = name
        self.w = None
        self.r = {}


class Eng:
    def __init__(self, name, h, sem, same_sync=True):
        self.name = name
        self.h = h
        self.sem = sem
        self.cnt = 0
        self.seen = {}
        self.same_sync = same_sync


class Sch:
    def __init__(self, nc, es, n_dma_sems=40):
        self.nc = nc
        self.es = es
        mk = lambda n, h, ss=True: Eng(n, h, es.enter_context(nc.semaphore("p_" + n)), ss)
        self.pe = mk("pe", nc.tensor, False)
        self.act = mk("act", nc.scalar)
        self.dve = mk("dve", nc.vector)
        self.pool = mk("pool", nc.gpsimd)
        self.sp = mk("sp", nc.sync)
        self.dsems = [es.enter_context(nc.semaphore("d%d" % i)) for i in range(n_dma_sems)]
        self.dvals = [0] * n_dma_sems
        self.dnext = 0
        self.ninst = 0
        self.ccsem = None
        self.cccnt = 0

    def _wait(self, eng, ev):
        if ev is None:
            return
        key, sem, val = ev
        if key == eng.name and not eng.same_sync:
            return
        if eng.seen.get(key, 0) >= val:
            return
        eng.h.wait_ge(sem, val)
        eng.seen[key] = val

    def _deps(self, eng, reads, writes):
        for b in reads:
            self._wait(eng, b.w)
        for b in writes:
            self._wait(eng, b.w)
            for ev in b.r.values():
                self._wait(eng, ev)

    def _record(self, ev, reads, writes):
        for b in reads:
            b.r[ev[0]] = ev
        for b in writes:
            b.w = ev
            b.r = {}

    def op(self, eng, fn, reads=(), writes=()):
        self._deps(eng, reads, writes)
        ins = fn(eng.h)
        eng.cnt += 1
        ins.then_inc(eng.sem, 1)
        ev = (eng.name, eng.sem, eng.cnt)
        self._record(ev, reads, writes)
        self.ninst += 1
        return ev

    def dma(self, q, out, in_, reads=(), writes=(), **kw):
        i = self.dnext
        self.dnext = (self.dnext + 1) % len(self.dsems)
        sem = self.dsems[i]
        key = "d%d" % i
        if self.dvals[i] > 0:
            self._wait(q, (key, sem, self.dvals[i]))
        self._deps(q, reads, writes)
        q.h.dma_start(out=out, in_=in_, **kw).then_inc(sem, 16)
        self.dvals[i] += 16
        ev = (key, sem, self.dvals[i])
        self._record(ev, reads, writes)
        self.ninst += 1
        return ev

    def dma_barrier(self, q):
        for i, sem in enumerate(self.dsems):
            if self.dvals[i] > 0:
                self._wait(q, ("d%d" % i, sem, self.dvals[i]))

    def collective(self, src, dst, groups):
        q = self.pool
        if self.ccsem is None:
            self.ccsem = self.es.enter_context(self.nc.semaphore("ccsem"))
            self.cccnt = 0
        self.nc.gpsimd.collective_compute("AllGather", ALU.bypass, replica_groups=groups, ins=[src], outs=[dst]).then_inc(self.ccsem)
        self.cccnt += 1
        self.ninst += 1

    def barrier(self):
        engs = (self.pe, self.act, self.dve, self.pool, self.sp)
        for e in engs:
            for o in engs:
                if o is not e and o.cnt > 0:
                    self._wait(e, (o.name, o.sem, o.cnt))
            self.dma_barrier(e)
            if self.ccsem is not None and self.cccnt > 0:
                self._wait(e, ("cc", self.ccsem, self.cccnt))

    def cc_wait(self, eng):
        if self.ccsem is not None and self.cccnt > 0:
            self._wait(eng, ("cc", self.ccsem, self.cccnt))

    def finish(self):
        for e in (self.pe, self.act, self.dve, self.pool):
            if e.cnt > 0:
                self._wait(self.sp, (e.name, e.sem, e.cnt))
        self.dma_barrier(self.sp)
        self.cc_wait(self.sp)
D = 1024
DFF = 2816
KC = 8
FC = 22
EPS = 1e-6


def copy_on(S, eng, out, in_, reads, writes):
    if eng is S.act:
        return S.op(eng, lambda e: e.activation(out=out, in_=in_, func=AF.Copy), reads=reads, writes=writes)
    return S.op(eng, lambda e: e.tensor_copy(out=out, in_=in_), reads=reads, writes=writes)


def cast_weight(S, R, src, dst, runs):
    K, N = src.shape
    kcn = K // 128
    for kc in range(kcn):
        for (c0, nrun, s0, sst, width) in runs:
            i = 0
            while i < nrun:
                ns = min(8, nrun - i)
                w = ns * 128 if width == 128 else width
                cc = c0 + i * 128
                st, stb = R.cast_stage()
                S.dma(S.sp, st.ap[:, 0:w], src[kc * 128:(kc + 1) * 128, cc:cc + w], writes=[st.b])
                eng = R.next_cast_eng()
                copy_on(S, eng, stb.ap[:, 0:w], st.ap[:, 0:w], [st.b], [stb.b])
                sa = s0 + i * sst
                if width == 128:
                    o = dst[sa:sa + (ns - 1) * sst + 1:sst, :, kc, :].rearrange("s p c -> p s c")
                    i_ = stb.ap[:, 0:w].rearrange("p (s c) -> p s c", c=128)
                else:
                    o = dst[sa, :, kc, 0:w]
                    i_ = stb.ap[:, 0:w]
                S.dma(S.pool, o, i_, reads=[stb.b])
                i += ns
                yield


class T:
    def __init__(self, ap, name=""):
        self.ap = ap
        self.b = Buf(name)


class Res:
    def __init__(self, nc, es, S, TT, psum=None, tag="", nstage=2):
        self.nc, self.S, self.TT = nc, S, TT
        sb = lambda name, shape, dt: T(es.enter_context(nc.sbuf_tensor(tag + name, shape, dt)), name)
        self.sb = sb
        if psum is None:
            psum = [T(es.enter_context(nc.psum_tensor("ps%d" % i, [128, 512], F32)), "ps%d" % i) for i in range(8)]
        self.psum = psum
        self.pnext = 0
        self.cstage = [(sb("cst%d" % i, [128, 1024], F32), sb("cstb%d" % i, [128, 1024], BF16)) for i in range(nstage)]
        self.cnext = 0
        self.ceng = 0
        self.wscr_buf = Buf("wscr")
        self.ones = sb("ones", [128, 128], BF16)
        S.op(S.dve, lambda e: e.memset(self.ones.ap[:], 1.0), writes=[self.ones.b])
        self.eps = sb("epsc", [128, 1], F32)
        S.op(S.dve, lambda e: e.memset(self.eps.ap[:], EPS), writes=[self.eps.b])

    def ps(self):
        t = self.psum[self.pnext]
        self.pnext = (self.pnext + 1) % 8
        return t

    def cast_stage(self):
        t = self.cstage[self.cnext]
        self.cnext = (self.cnext + 1) % len(self.cstage)
        return t

    def next_cast_eng(self):
        S = self.S
        e = [S.pool, S.dve, S.act][self.ceng % 3]
        self.ceng += 1
        return e


class FFNRes:
    def __init__(self, R):
        TT = R.TT
        sb = R.sb
        self.sq = sb("f_sq", [128, KC, TT], BF16)
        self.rstd = sb("f_rstd", [128, TT], F32)
        self.h = sb("f_h", [128, KC, TT], BF16)
        self.actb = sb("f_act", [128, FC, TT], BF16)
        self.wi = [sb("f_wi%d" % i, [128, 2, KC, 128], BF16) for i in range(3)]
        self.wo = [sb("f_wo%d" % i, [128, FC, 128], BF16) for i in range(2)]
        self.sil = [sb("f_sil%d" % i, [128, 512], F32) for i in range(2)]
        self.wi_n = 0
        self.wo_n = 0
        self.sil_n = 0


def emit_rmsnorm(S, R, x, gcol, h, sq, rstd, nch, TT, dim):
    for c in range(nch):
        S.op(S.pool, lambda e: e.tensor_tensor(out=sq.ap[:, c, :], in0=x.ap[:, c, :], in1=x.ap[:, c, :], op=ALU.mult),
             reads=[x.b], writes=[sq.b])
    for n in range(TT // 512):
        cs = slice(n * 512, (n + 1) * 512)
        p = R.ps()
        for c in range(nch):
            S.op(S.pe, lambda e: e.matmul(p.ap[:], lhsT=R.ones.ap[:], rhs=sq.ap[:, c, cs], start=(c == 0), stop=(c == nch - 1)),
                 reads=[R.ones.b, sq.b], writes=[p.b])
        S.op(S.act, lambda e: e.activation(out=rstd.ap[:, cs], in_=p.ap[:], func=AF.Sqrt, scale=1.0 / dim, bias=R.eps.ap[:, 0:1]),
             reads=[p.b, R.eps.b], writes=[rstd.b])
    S.op(S.dve, lambda e: e.reciprocal(out=rstd.ap[:], in_=rstd.ap[:]), reads=[rstd.b], writes=[rstd.b])
    for c in range(nch):
        S.op(S.dve, lambda e: e.scalar_tensor_tensor(out=h.ap[:, c, :], in0=x.ap[:, c, :], scalar=gcol.ap[:, c:c + 1], in1=rstd.ap[:],
                                                   op0=ALU.mult, op1=ALU.mult),
             reads=[x.b, gcol.b, rstd.b], writes=[h.b])


def emit_ffn(S, R, F, x, gcol, wi_s, wo_s):
    TT = R.TT
    NG = TT // 512
    emit_rmsnorm(S, R, x, gcol, F.h, F.sq, F.rstd, KC, TT, D)
    for f in range(FC):
        w = F.wi[F.wi_n % 3]; F.wi_n += 1
        S.dma(S.sp, w.ap[:], wi_s[2 * f:2 * f + 2].rearrange("t p k c -> p t k c"), writes=[w.b])
        for n in range(NG):
            cs = slice(n * 512, (n + 1) * 512)
            pg = R.ps(); pu = R.ps()
            for k in range(KC):
                S.op(S.pe, lambda e: e.matmul(pg.ap[:], lhsT=w.ap[:, 0, k, :], rhs=F.h.ap[:, k, cs], start=(k == 0), stop=(k == KC - 1)),
                     reads=[w.b, F.h.b], writes=[pg.b])
            for k in range(KC):
                S.op(S.pe, lambda e: e.matmul(pu.ap[:], lhsT=w.ap[:, 1, k, :], rhs=F.h.ap[:, k, cs], start=(k == 0), stop=(k == KC - 1)),
                     reads=[w.b, F.h.b], writes=[pu.b])
            sl = F.sil[F.sil_n % 2]; F.sil_n += 1
            S.op(S.act, lambda e: e.activation(out=sl.ap[:], in_=pg.ap[:], func=AF.Silu), reads=[pg.b], writes=[sl.b])
            S.op(S.dve, lambda e: e.tensor_tensor(out=F.actb.ap[:, f, cs], in0=sl.ap[:], in1=pu.ap[:], op=ALU.mult),
                 reads=[sl.b, pu.b], writes=[F.actb.b])
    for d in range(KC):
        w = F.wo[F.wo_n % 2]; F.wo_n += 1
        S.dma(S.sp, w.ap[:], wo_s[d], writes=[w.b])
        for n in range(NG):
            cs = slice(n * 512, (n + 1) * 512)
            p = R.ps()
            for fc in range(FC):
                S.op(S.pe, lambda e: e.matmul(p.ap[:], lhsT=w.ap[:, fc, :], rhs=F.actb.ap[:, fc, cs], start=(fc == 0), stop=(fc == FC - 1)),
                     reads=[w.b, F.actb.b], writes=[p.b])
            S.op(S.dve, lambda e: e.scalar_tensor_tensor(out=x.ap[:, d, cs], in0=p.ap[:], scalar=0.5, in1=x.ap[:, d, cs], op0=ALU.mult, op1=ALU.add),
                 reads=[p.b, x.b], writes=[x.b])


B_, SEQ, NL = 2, 16384, 4
NT = 4096
TT = 1024
NTILE = NT // TT
NG = TT // 512
INTOT = 8096
WIN_RUNS = [(0, 23, 0, 1, 128), (2944, 1, 23, 1, 32), (2976, 8, 24, 1, 128), (4000, 32, 32, 1, 128)]
SEG_Q, SEG_K, SEG_BG, SEG_CG, SEG_XS, SEG_QL, SEG_KVL, SEG_KR, SEG_LG, SEG_XB, SEG_GATE = 0, 4, 6, 10, 14, 18, 21, 23, 24, 28, 32


class Tok:
    def __init__(self, R):
        sb = R.sb
        self.tf = [sb("tf%d" % i, [128, 512], F32) for i in range(6)]
        self.tb = [sb("tb%d" % i, [128, 512], BF16) for i in range(8)]
        self.wseg = [sb("wseg%d" % i, [128, 8, 128], BF16) for i in range(4)]
        self.wbr = [sb("wbr%d" % i, [128, 4, 4, 128], BF16) for i in range(2)]
        self.cs2 = [[sb("cs%d_%d" % (j, i), [128, 512], F32) for i in range(4)] for j in range(2)]
        self.n = {"tf": 0, "tb": 0, "wseg": 0, "wbr": 0}

    def get(self, kind):
        lst = getattr(self, kind)
        t = lst[self.n[kind] % len(lst)]
        self.n[kind] += 1
        return t


def load_cols(S, dst, src_vec, nch):
    S.dma(S.sp, dst.ap[:, 0:nch], src_vec.rearrange("(c p) -> p c", p=128), writes=[dst.b], allow_slow_non_contiguous=True)


def headnorm_rope(S, R, K, C, p, M, ones_ap, gcol_ap, perm_ap, cos_ap, sin_ap, dim, out_ap):
    sqb = K.get("tb")
    S.op(S.act, lambda e: e.activation(out=sqb.ap[0:M, :], in_=p.ap[0:M, :], func=AF.Square), reads=[p.b], writes=[sqb.b])
    p2 = R.ps()
    S.op(S.pe, lambda e: e.matmul(p2.ap[0:M, :], lhsT=ones_ap, rhs=sqb.ap[0:M, :], start=True, stop=True),
         reads=[sqb.b, C.b], writes=[p2.b])
    rs = K.get("tf")
    S.op(S.act, lambda e: e.activation(out=rs.ap[0:M, :], in_=p2.ap[0:M, :], func=AF.Sqrt, scale=1.0 / dim, bias=R.eps.ap[0:M, 0:1]),
         reads=[p2.b, R.eps.b], writes=[rs.b])
    S.op(S.dve, lambda e: e.reciprocal(out=rs.ap[0:M, :], in_=rs.ap[0:M, :]), reads=[rs.b], writes=[rs.b])
    qn = K.get("tb")
    S.op(S.dve, lambda e: e.scalar_tensor_tensor(out=qn.ap[0:M, :], in0=p.ap[0:M, :], scalar=gcol_ap, in1=rs.ap[0:M, :], op0=ALU.mult, op1=ALU.mult),
         reads=[p.b, rs.b, C.b], writes=[qn.b])
    p3 = R.ps()
    S.op(S.pe, lambda e: e.matmul(p3.ap[0:M, :], lhsT=perm_ap, rhs=qn.ap[0:M, :], start=True, stop=True),
         reads=[qn.b, C.b], writes=[p3.b])
    t1 = K.get("tf")
    S.op(S.pool, lambda e: e.tensor_tensor(out=t1.ap[0:M, :], in0=qn.ap[0:M, :], in1=cos_ap, op=ALU.mult), reads=[qn.b, K.csb], writes=[t1.b])
    t2 = K.get("tf")
    S.op(S.dve, lambda e: e.tensor_tensor(out=t2.ap[0:M, :], in0=p3.ap[0:M, :], in1=sin_ap, op=ALU.mult), reads=[p3.b, K.csb], writes=[t2.b])
    ob = K.get("tb")
    S.op(S.pool, lambda e: e.tensor_tensor(out=ob.ap[0:M, :], in0=t1.ap[0:M, :], in1=t2.ap[0:M, :], op=ALU.add), reads=[t1.b, t2.b], writes=[ob.b])
    S.dma(S.pool, out_ap, ob.ap[0:M, :], reads=[ob.b])


class Consts:
    pass


def emit_mixprep(S, R, F, K, C, x, W, ti):
    t0 = ti * TT
    hm = F.h
    emit_rmsnorm(S, R, x, C.mixg, hm, F.sq, F.rstd, KC, TT, D)
    S.dma(S.pool, W["x1T"].rearrange("(c p) t -> p c t", p=128)[:, :, t0:t0 + TT], x.ap[:], reads=[x.b])
    qln = [F.actb.ap[:, k, :] for k in range(3)]
    kvn = [F.actb.ap[:, 3 + k, :] for k in range(2)]
    krope = F.actb.ap[0:32, 5, :]
    ab = F.actb.b

    def load_seg(seg):
        w = K.get("wseg")
        S.dma(S.sp, w.ap[:], W["win_s"][seg], writes=[w.b])
        return w

    def proj(w, n, M=128):
        cs = slice(n * 512, (n + 1) * 512)
        p = R.ps()
        for k in range(KC):
            S.op(S.pe, lambda e: e.matmul(p.ap[0:M, :], lhsT=w.ap[:, k, 0:M], rhs=hm.ap[:, k, cs], start=(k == 0), stop=(k == KC - 1)),
                 reads=[w.b, hm.b], writes=[p.b])
        return p

    def store_f32(p, dram_ap, func=None):
        t = K.get("tb")
        if func is None:
            S.op(S.act, lambda e: e.activation(out=t.ap[:], in_=p.ap[:], func=AF.Copy), reads=[p.b], writes=[t.b])
        else:
            S.op(S.act, lambda e: e.activation(out=t.ap[:], in_=p.ap[:], func=func), reads=[p.b], writes=[t.b])
        S.dma(S.pool, dram_ap, t.ap[:], reads=[t.b])

    for (seg0, name, func) in ((SEG_BG, "bgT", None), (SEG_LG, "lgT", AF.Gelu), (SEG_XB, "xbT", None)):
        for c in range(4):
            w = load_seg(seg0 + c)
            for n in range(NG):
                p = proj(w, n)
                store_f32(p, W[name][c * 128:(c + 1) * 128, t0 + n * 512:t0 + (n + 1) * 512], func)
    for c in range(4):
        wc = load_seg(SEG_CG + c)
        wx = load_seg(SEG_XS + c)
        for n in range(NG):
            pc = proj(wc, n)
            px = proj(wx, n)
            t = K.get("tf")
            S.op(S.act, lambda e: e.activation(out=t.ap[:], in_=pc.ap[:], func=AF.Copy), reads=[pc.b], writes=[t.b])
            t2 = K.get("tb")
            S.op(S.dve, lambda e: e.tensor_tensor(out=t2.ap[:], in0=t.ap[:], in1=px.ap[:], op=ALU.mult), reads=[t.b, px.b], writes=[t2.b])
            S.dma(S.pool, W["cxT"][c * 128:(c + 1) * 128, t0 + n * 512:t0 + (n + 1) * 512], t2.ap[:], reads=[t2.b])
    w = load_seg(SEG_KR)
    for n in range(NG):
        p = proj(w, n, M=32)
        S.op(S.act, lambda e: e.activation(out=krope[:, n * 512:(n + 1) * 512], in_=p.ap[0:32, :], func=AF.Copy), reads=[p.b], writes=[ab])
    for (seg0, nch, dst, gc, dim) in ((SEG_QL, 3, qln, C.qag, 384), (SEG_KVL, 2, kvn, C.kvag, 256)):
        ws = [load_seg(seg0 + c) for c in range(nch)]
        for n in range(NG):
            cs = slice(n * 512, (n + 1) * 512)
            ps_ = [proj(ws[c], n) for c in range(nch)]
            p2 = R.ps()
            for c in range(nch):
                sqb = K.get("tb")
                S.op(S.act, lambda e: e.activation(out=sqb.ap[:], in_=ps_[c].ap[:], func=AF.Square), reads=[ps_[c].b], writes=[sqb.b])
                S.op(S.pe, lambda e: e.matmul(p2.ap[:], lhsT=R.ones.ap[:], rhs=sqb.ap[:], start=(c == 0), stop=(c == nch - 1)),
                     reads=[sqb.b, R.ones.b], writes=[p2.b])
            rs = K.get("tf")
            S.op(S.act, lambda e: e.activation(out=rs.ap[:], in_=p2.ap[:], func=AF.Sqrt, scale=1.0 / dim, bias=R.eps.ap[:, 0:1]),
                 reads=[p2.b, R.eps.b], writes=[rs.b])
            S.op(S.dve, lambda e: e.reciprocal(out=rs.ap[:], in_=rs.ap[:]), reads=[rs.b], writes=[rs.b])
            for c in range(nch):
                S.op(S.dve, lambda e: e.scalar_tensor_tensor(out=dst[c][:, cs], in0=ps_[c].ap[:], scalar=gc.ap[:, c:c + 1], in1=rs.ap[:],
                                                           op0=ALU.mult, op1=ALU.mult),
                     reads=[ps_[c].b, rs.b, C.b], writes=[ab])
    def load_cs(n):
        g0 = t0 + n * 512
        S.dma(S.sp, K.cs[0].ap[:], W["cosg"][:, g0:g0 + 512], writes=[K.csb])
        S.dma(S.sp, K.cs[1].ap[:], W["sing"][:, g0:g0 + 512], writes=[K.csb])
        S.dma(S.sp, K.cs[2].ap[0:96, :], W["cosm"][:, g0:g0 + 512], writes=[K.csb])
        S.dma(S.sp, K.cs[3].ap[0:96, :], W["sinm"][:, g0:g0 + 512], writes=[K.csb])

    qg_flat = W["qg"].rearrange("h d t -> (h d) t")
    kg_flat = W["kg"].rearrange("h d t -> (h d) t")
    for n in range(NG):
        gcs = slice(t0 + n * 512, t0 + (n + 1) * 512)
        p = R.psum[7]
        for j in range(4):
            for k in range(KC):
                S.op(S.pe, lambda e: e.matmul(p.ap[:, j * 128:(j + 1) * 128], lhsT=hm.ap[:, k, n * 512 + j * 128:n * 512 + (j + 1) * 128],
                                              rhs=C.wvg.ap[:, k, :], start=(k == 0), stop=(k == KC - 1)),
                     reads=[hm.b, C.b], writes=[p.b])
        vb = K.get("tb")
        S.op(S.act, lambda e: e.activation(out=vb.ap[:], in_=p.ap[:], func=AF.Copy), reads=[p.b], writes=[vb.b])
        S.dma(S.pool, W["vg"][gcs, :].rearrange("(j p) c -> p j c", p=128), vb.ap[:].rearrange("p (j c) -> p j c", c=128), reads=[vb.b])
        for j in range(4):
            p = R.psum[7]
            for k in range(2):
                S.op(S.pe, lambda e: e.matmul(p.ap[:], lhsT=kvn[k][:, n * 512 + j * 128:n * 512 + (j + 1) * 128], rhs=C.wvm.ap[:, k, :],
                                              start=(k == 0), stop=(k == 1)),
                     reads=[ab, C.b], writes=[p.b])
            vb = K.get("tb")
            S.op(S.act, lambda e: e.activation(out=vb.ap[:], in_=p.ap[:], func=AF.Copy), reads=[p.b], writes=[vb.b])
            r0 = t0 + n * 512 + j * 128
            S.dma(S.pool, W["vm"][r0:r0 + 128, :], vb.ap[:], reads=[vb.b])

    items = []
    for n in range(NG):
        for c in range(5):
            items.append(("g", n, c))
        for h in range(8):
            items.append(("q", n, h))
        for h in range(8):
            items.append(("k", n, h))
    ctx = [dict() for _ in items]
    pb_p, pb_2, pb_3 = R.psum[0:3], R.psum[3:5], R.psum[5:7]

    def st_load(i):
        kind, n, c = items[i]
        w = K.get("wseg")
        if kind == "g":
            S.dma(S.sp, w.ap[:], W["win_s"][SEG_Q + c], writes=[w.b])
        elif kind == "q":
            S.dma(S.sp, w.ap[:, 0:3, :], W["wq_s"][c], writes=[w.b])
        else:
            S.dma(S.sp, w.ap[:, 0:2, :], W["wk_s"][c], writes=[w.b])
        ctx[i]["w"] = w
        if (kind, c) == ("g", 0):
            cset = K.cs2[n % 2]
            g0 = t0 + n * 512
            S.dma(S.sp, cset[0].ap[:], W["cosg"][:, g0:g0 + 512], writes=[cset[0].b])
            S.dma(S.sp, cset[1].ap[:], W["sing"][:, g0:g0 + 512], writes=[cset[1].b])
            S.dma(S.sp, cset[2].ap[0:96, :], W["cosm"][:, g0:g0 + 512], writes=[cset[2].b])
            S.dma(S.sp, cset[3].ap[0:96, :], W["sinm"][:, g0:g0 + 512], writes=[cset[3].b])

    def st_proj(i):
        kind, n, c = items[i]
        cs = slice(n * 512, (n + 1) * 512)
        w = ctx[i]["w"]
        p = pb_p[i % 3]
        if kind == "g":
            for k in range(KC):
                S.op(S.pe, lambda e: e.matmul(p.ap[:], lhsT=w.ap[:, k, :], rhs=hm.ap[:, k, cs], start=(k == 0), stop=(k == KC - 1)),
                     reads=[w.b, hm.b], writes=[p.b])
        elif kind == "q":
            for k in range(3):
                S.op(S.pe, lambda e: e.matmul(p.ap[0:96, :], lhsT=w.ap[:, k, 0:96], rhs=qln[k][:, cs], start=(k == 0), stop=(k == 2)),
                     reads=[w.b, ab], writes=[p.b])
        else:
            for k in range(2):
                S.op(S.pe, lambda e: e.matmul(p.ap[0:64, :], lhsT=w.ap[:, k, 0:64], rhs=kvn[k][:, cs], start=(k == 0), stop=(k == 1)),
                     reads=[w.b, ab], writes=[p.b])
            S.op(S.pe, lambda e: e.matmul(p.ap[64:96, :], lhsT=C.ident.ap[0:32, 0:32], rhs=krope[:, cs], start=True, stop=True),
                 reads=[C.b, ab], writes=[p.b])
        ctx[i]["p"] = p

    def params(i):
        kind, n, c = items[i]
        gcs = slice(t0 + n * 512, t0 + (n + 1) * 512)
        cset = K.cs2[n % 2]
        if kind == "g":
            gcol = C.gq.ap[:, 0:1] if c < 4 else C.gk.ap[:, 0:1]
            dst = qg_flat[c * 128:(c + 1) * 128, gcs] if c < 4 else kg_flat[:, gcs]
            return 128, C.bd64.ap[:], gcol, C.permg.ap[:], cset[0], cset[1], 64, dst
        gcol = C.mq.ap[0:96, 0:1] if kind == "q" else C.mk.ap[0:96, 0:1]
        dst = W["qm"][c, :, gcs] if kind == "q" else W["km"][c, :, gcs]
        return 96, R.ones.ap[0:96, 0:96], gcol, C.permm.ap[0:96, 0:96], cset[2], cset[3], 96, dst

    def st_sq(i):
        M = params(i)[0]
        p = ctx[i]["p"]
        sqb = K.get("tb")
        S.op(S.act, lambda e: e.activation(out=sqb.ap[0:M, :], in_=p.ap[0:M, :], func=AF.Square), reads=[p.b], writes=[sqb.b])
        ctx[i]["sqb"] = sqb

    def st_ones(i):
        M, ones_ap = params(i)[0:2]
        sqb = ctx[i]["sqb"]
        p2 = pb_2[i % 2]
        S.op(S.pe, lambda e: e.matmul(p2.ap[0:M, :], lhsT=ones_ap, rhs=sqb.ap[0:M, :], start=True, stop=True), reads=[sqb.b, C.b, R.ones.b], writes=[p2.b])
        ctx[i]["p2"] = p2

    def st_norm(i):
        M, _, gcol, _, _, _, dim, _ = params(i)
        p, p2 = ctx[i]["p"], ctx[i]["p2"]
        rs = K.get("tf")
        S.op(S.act, lambda e: e.activation(out=rs.ap[0:M, :], in_=p2.ap[0:M, :], func=AF.Sqrt, scale=1.0 / dim, bias=R.eps.ap[0:M, 0:1]),
             reads=[p2.b, R.eps.b], writes=[rs.b])
        S.op(S.dve, lambda e: e.reciprocal(out=rs.ap[0:M, :], in_=rs.ap[0:M, :]), reads=[rs.b], writes=[rs.b])
        qn = K.get("tb")
        S.op(S.dve, lambda e: e.scalar_tensor_tensor(out=qn.ap[0:M, :], in0=p.ap[0:M, :], scalar=gcol, in1=rs.ap[0:M, :], op0=ALU.mult, op1=ALU.mult),
             reads=[p.b, rs.b, C.b], writes=[qn.b])
        ctx[i]["qn"] = qn

    def st_perm(i):
        M, _, _, perm_ap = params(i)[0:4]
        qn = ctx[i]["qn"]
        p3 = pb_3[i % 2]
        S.op(S.pe, lambda e: e.matmul(p3.ap[0:M, :], lhsT=perm_ap, rhs=qn.ap[0:M, :], start=True, stop=True), reads=[qn.b, C.b], writes=[p3.b])
        ctx[i]["p3"] = p3

    def st_rope(i):
        M, _, _, _, cosT, sinT, _, dst = params(i)
        qn, p3 = ctx[i]["qn"], ctx[i]["p3"]
        t1 = K.get("tf")
        S.op(S.pool, lambda e: e.tensor_tensor(out=t1.ap[0:M, :], in0=qn.ap[0:M, :], in1=cosT.ap[0:M, :], op=ALU.mult), reads=[qn.b, cosT.b], writes=[t1.b])
        t2 = K.get("tf")
        S.op(S.dve, lambda e: e.tensor_tensor(out=t2.ap[0:M, :], in0=p3.ap[0:M, :], in1=sinT.ap[0:M, :], op=ALU.mult), reads=[p3.b, sinT.b], writes=[t2.b])
        ob = K.get("tb")
        S.op(S.pool, lambda e: e.tensor_tensor(out=ob.ap[0:M, :], in0=t1.ap[0:M, :], in1=t2.ap[0:M, :], op=ALU.add), reads=[t1.b, t2.b], writes=[ob.b])
        S.dma(S.pool, dst, ob.ap[0:M, :], reads=[ob.b])
        ctx[i].clear()

    NI = len(items)
    for s_ in range(-1, NI + 4):
        if 0 <= s_ + 1 < NI:
            st_load(s_ + 1)
        if 0 <= s_ - 1 < NI:
            st_sq(s_ - 1)
        if 0 <= s_ - 2 < NI:
            st_norm(s_ - 2)
        if 0 <= s_ < NI:
            st_proj(s_)
        if 0 <= s_ - 1 < NI:
            st_ones(s_ - 1)
        if 0 <= s_ - 3 < NI:
            st_perm(s_ - 3)
        if 0 <= s_ - 4 < NI:
            st_rope(s_ - 4)


def emit_merge(S, R, F, K, C, x, W, ti):
    t0 = ti * TT
    hm = F.h
    emit_rmsnorm(S, R, x, C.mixg_a, hm, F.sq, F.rstd, KC, TT, D)
    ab = F.actb.b
    merged = F.sq
    for n, name in enumerate(("oaT", "obT", "ocT", "odT")):
        S.dma(S.sp, F.actb.ap[:, 4 * n:4 * n + 4, :], W[name].rearrange("(k p) t -> p k t", p=128)[:, :, t0:t0 + TT],
              reads=([W["o_buf"]] if "o_buf" in W else []), writes=[ab])
    for d in range(KC):
        wb = K.get("wbr")
        S.dma(S.sp, wb.ap[:], W["wbr_s"][:, d].rearrange("n p k c -> p n k c"), writes=[wb.b])
        macc = [None] * NG
        for n in range(4):
            gw = K.get("wseg")
            S.dma(S.sp, gw.ap[:], W["wina_s"][SEG_GATE + n * 8 + d], writes=[gw.b])
            for g in range(NG):
                cs = slice(g * 512, (g + 1) * 512)
                py = R.ps()
                for k in range(4):
                    S.op(S.pe, lambda e: e.matmul(py.ap[:], lhsT=wb.ap[:, n, k, :], rhs=F.actb.ap[:, 4 * n + k, cs], start=(k == 0), stop=(k == 3)),
                         reads=[wb.b, ab], writes=[py.b])
                pg = R.ps()
                for k in range(KC):
                    S.op(S.pe, lambda e: e.matmul(pg.ap[:], lhsT=gw.ap[:, k, :], rhs=hm.ap[:, k, cs], start=(k == 0), stop=(k == KC - 1)),
                         reads=[gw.b, hm.b], writes=[pg.b])
                gs = K.get("tf")
                S.op(S.act, lambda e: e.activation(out=gs.ap[:], in_=pg.ap[:], func=AF.Sigmoid), reads=[pg.b], writes=[gs.b])
                if n == 0:
                    m = K.mt[g]
                    S.op(S.dve, lambda e: e.tensor_tensor(out=m.ap[:], in0=gs.ap[:], in1=py.ap[:], op=ALU.mult), reads=[gs.b, py.b], writes=[m.b])
                else:
                    m = K.mt[g]
                    S.op(S.dve, lambda e: e.tensor_tensor(out=gs.ap[:], in0=gs.ap[:], in1=py.ap[:], op=ALU.mult), reads=[gs.b, py.b], writes=[gs.b])
                    if n < 3:
                        S.op(S.pool, lambda e: e.tensor_tensor(out=m.ap[:], in0=m.ap[:], in1=gs.ap[:], op=ALU.add), reads=[m.b, gs.b], writes=[m.b])
                    else:
                        S.op(S.pool, lambda e: e.tensor_tensor(out=merged.ap[:, d, cs], in0=m.ap[:], in1=gs.ap[:], op=ALU.add),
                             reads=[m.b, gs.b], writes=[merged.b])
    for d in range(KC):
        w = K.get("wseg")
        S.dma(S.sp, w.ap[:], W["wout_s"][d], writes=[w.b])
        for g in range(NG):
            cs = slice(g * 512, (g + 1) * 512)
            p = R.ps()
            for k in range(KC):
                S.op(S.pe, lambda e: e.matmul(p.ap[:], lhsT=w.ap[:, k, :], rhs=merged.ap[:, k, cs], start=(k == 0), stop=(k == KC - 1)),
                     reads=[w.b, merged.b], writes=[p.b])
            S.op(S.dve, lambda e: e.tensor_tensor(out=x.ap[:, d, cs], in0=p.ap[:], in1=x.ap[:, d, cs], op=ALU.add), reads=[p.b, x.b], writes=[x.b])


def _dram_in(nc, name, shape, dt=F32):
    return nc.dram_tensor(name, list(shape), dt, kind="ExternalInput").ap()


def _dram_out(nc, name, shape, dt=F32):
    return nc.dram_tensor(name, list(shape), dt, kind="ExternalOutput").ap()


from concourse.bass import ds

GROUPS = [[0, 1, 2, 3], [4, 5, 6, 7]]
WNAMES = {"ffn1_norm": [D], "ffn1_wi": [D, 2 * DFF], "ffn1_wo": [DFF, D], "mix_norm": [D], "w_in": [D, INTOT],
          "gqa_q_norm": [64], "gqa_k_norm": [64], "mla_qa_norm": [384], "mla_wq_up": [384, 768], "mla_kva_norm": [256],
          "mla_wkv_up": [256, 1024], "mla_q_norm": [96], "mla_k_norm": [96], "w_branch": [4, 512, D], "w_out": [D, D],
          "ffn2_norm": [D], "ffn2_wi": [D, 2 * DFF], "ffn2_wo": [DFF, D]}
CH = 1024
NCHK = SEQ // CH


def build_fused():
    nc = bass.Bass("TRN2", target_bir_lowering=False)
    I = {"xT": _dram_in(nc, "xT", [D, NT]), "cmat": _dram_in(nc, "cmat", [4, 128, 128])}
    for nm, shp in (("cosg", [128, NT]), ("sing", [128, NT]), ("cosm", [96, NT]), ("sinm", [96, NT]),
                    ("scw", [NL, 128, 4]), ("lruw", [NL, 128, 5]), ("lrug", [NL, 4, 128, 128]), ("lrub", [NL, 128, 6])):
        I[nm] = _dram_in(nc, nm, shp)
    for nm, shp in WNAMES.items():
        I[nm] = _dram_in(nc, nm, [NL] + shp)
    xoutT = _dram_out(nc, "xoutT", [D, NT])
    dt_ = lambda name, shape, dt=BF16: nc.dram_tensor(name, list(shape), dt).ap()
    SCR = []
    for l in range(NL):
        SCR.append({"f1wi_s": dt_("f1wi_s%d" % l, [2 * FC, 128, KC, 128]), "f1wo_s": dt_("f1wo_s%d" % l, [KC, 128, FC, 128]),
                    "f2wi_s": dt_("f2wi_s%d" % l, [2 * FC, 128, KC, 128]), "f2wo_s": dt_("f2wo_s%d" % l, [KC, 128, FC, 128]),
                    "win_s": dt_("win_s%d" % l, [64, 128, KC, 128]), "wq_s": dt_("wq_s%d" % l, [8, 128, 3, 128]),
                    "wk_s": dt_("wk_s%d" % l, [8, 128, 2, 128]), "wbr_s": dt_("wbr_s%d" % l, [4, 8, 128, 4, 128]),
                    "wout_s": dt_("wout_s%d" % l, [8, 128, KC, 128])})
    xcur = dt_("xcur", [D, NT], F32)
    qg = dt_("qg", [8, 64, NT]); qm = dt_("qm", [8, 96, NT])
    oaT = dt_("oaT", [512, NT]); ocT = dt_("ocT", [512, NT])
    kg_s = dt_("kg_s", [2, 64, NT]); kgG = dt_("kgG", [4, 128, NT])
    vg_s = dt_("vg_s", [NT, 128]); vgG = dt_("vgG", [4, NT, 128])
    km_s = dt_("km_s", [8, 96, NT]); kmG = dt_("kmG", [8, 4, 96, NT])
    vm_s = dt_("vm_s", [NT, 512]); vmG = dt_("vmG", [4, 4, 1024, 512])
    XS = {nm: dt_(nm + "_s", [512, NT]) for nm in ("bg", "cx", "lg", "xb")}
    XG = {nm: dt_(nm + "G", [4, 4, 128, NT]) for nm in ("bg", "cx", "lg", "xb")}
    ob_s = dt_("ob_s", [4, 128, NT]); od_s = dt_("od_s", [4, 128, NT])
    obG = dt_("obG", [4, 4, 128, NT]); odG = dt_("odG", [4, 4, 128, NT])
    hf_s = dt_("hf_s", [128, SEQ], F32)
    XL = {nm: dt_(nm + "L", [4, 128, NT]) for nm in ("bg", "cx", "lg", "xb")}
    obL = dt_("obL", [512, NT]); odL = dt_("odL", [512, NT])

    with ExitStack() as es:
        S = Sch(nc, es)
        psbig = [es.enter_context(nc.psum_tensor("psb%d" % i, [128, 1024], F32)) for i in range(4)]
        psum = [T(psbig[i // 2][:, (i % 2) * 512:(i % 2 + 1) * 512], "ps%d" % i) for i in range(8)]
        pid = nc.sync.partition_id()
        jr = pid % 4

        def cast_layer(R, l, front, back):
            sc = SCR[l]
            if front:
                yield from cast_weight(S, R, I["ffn1_wi"][l], sc["f1wi_s"], [(0, FC, 0, 2, 128), (DFF, FC, 1, 2, 128)])
                yield from cast_weight(S, R, I["ffn1_wo"][l], sc["f1wo_s"], [(0, KC, 0, 1, 128)])
                yield from cast_weight(S, R, I["w_in"][l], sc["win_s"], WIN_RUNS)
                yield from cast_weight(S, R, I["mla_wq_up"][l], sc["wq_s"], [(h * 96, 1, h, 1, 96) for h in range(8)])
                yield from cast_weight(S, R, I["mla_wkv_up"][l], sc["wk_s"], [(h * 128, 1, h, 1, 64) for h in range(8)])
            if back:
                for n in range(4):
                    yield from cast_weight(S, R, I["w_branch"][l, n], sc["wbr_s"][n], [(0, 8, 0, 1, 128)])
                yield from cast_weight(S, R, I["w_out"][l], sc["wout_s"], [(0, 8, 0, 1, 128)])
                yield from cast_weight(S, R, I["ffn2_wi"][l], sc["f2wi_s"], [(0, FC, 0, 2, 128), (DFF, FC, 1, 2, 128)])
                yield from cast_weight(S, R, I["ffn2_wo"][l], sc["f2wo_s"], [(0, KC, 0, 1, 128)])

        with ExitStack() as pes:
            R = Res(nc, pes, S, TT, psum, tag="pr_")
            for _ in cast_layer(R, 0, True, False):
                pass
            S.barrier()

        def tok_phase(la, lb, tag):
            with ExitStack() as tes:
                R = Res(nc, tes, S, TT, psum, tag=tag, nstage=1)
                F = FFNRes(R)
                K = Tok(R)
                K.csb = Buf("cs")
                K.mt = [R.sb("mt%d" % i, [128, 512], F32) for i in range(NG)]
                C = Consts()
                C.b = Buf("consts")
                sb = R.sb
                x = sb("x", [128, KC, TT], F32)

                def cmat(i, name):
                    t = sb(name, [128, 128], BF16)
                    st, _ = R.cast_stage()
                    S.dma(S.sp, st.ap[:, 0:128], I["cmat"][i], writes=[st.b])
                    S.op(S.dve, lambda e: e.tensor_copy(out=t.ap[:], in_=st.ap[:, 0:128]), reads=[st.b], writes=[C.b])
                    return t
                C.bd64 = cmat(0, "bd64"); C.permg = cmat(1, "permg"); C.permm = cmat(2, "permm"); C.ident = cmat(3, "ident")

                def cols(name, src, nch):
                    t = sb(name, [128, nch], F32)
                    S.dma(S.sp, t.ap[:, 0:nch], src.rearrange("(c p) -> p c", p=128), writes=[C.b], allow_slow_non_contiguous=True)
                    return t

                def col1(name, src, n, rep):
                    t = sb(name, [128, 1], F32)
                    for r in range(rep):
                        S.dma(S.sp, t.ap[r * n:(r + 1) * n, 0:1], src.rearrange("(p o) -> p o", o=1), writes=[C.b], allow_slow_non_contiguous=True)
                    return t
                W = {"cosg": I["cosg"], "sing": I["sing"], "cosm": I["cosm"], "sinm": I["sinm"], "x1T": xcur, "qg": qg, "qm": qm,
                     "kg": kg_s, "vg": vg_s, "km": km_s, "vm": vm_s, "bgT": XS["bg"], "cxT": XS["cx"], "lgT": XS["lg"], "xbT": XS["xb"]}
                if la is not None:
                    C.mixg_a = cols("mixg_a", I["mix_norm"][la], 8)
                    C.f2g = cols("f2g", I["ffn2_norm"][la], 8)
                    olb = Buf("olb")
                    S.dma(S.sp, obL, obG[ds(jr, 1)].rearrange("o r p t -> (o r p) t"), writes=[olb])
                    S.dma(S.sp, odL, odG[ds(jr, 1)].rearrange("o r p t -> (o r p) t"), writes=[olb])
                    W.update({"oaT": oaT, "ocT": ocT, "obT": obL, "odT": odL, "o_buf": olb,
                              "wina_s": SCR[la]["win_s"], "wbr_s": SCR[la]["wbr_s"], "wout_s": SCR[la]["wout_s"]})
                if lb is not None:
                    C.f1g = cols("f1g", I["ffn1_norm"][lb], 8)
                    C.mixg = cols("mixg", I["mix_norm"][lb], 8)
                    C.qag = cols("qag", I["mla_qa_norm"][lb], 3)
                    C.kvag = cols("kvag", I["mla_kva_norm"][lb], 2)
                    C.gq = col1("gq", I["gqa_q_norm"][lb], 64, 2)
                    C.gk = col1("gk", I["gqa_k_norm"][lb], 64, 2)
                    C.mq = col1("mq", I["mla_q_norm"][lb], 96, 1)
                    C.mk = col1("mk", I["mla_k_norm"][lb], 96, 1)
                    C.wvg = sb("wvg", [128, KC, 128], BF16)
                    st, _ = R.cast_stage()
                    S.dma(S.sp, st.ap[:, 0:1024].rearrange("p (k c) -> p k c", c=128),
                          I["w_in"][lb][:, 640:768].rearrange("(k p) c -> p k c", p=128), writes=[st.b])
                    S.op(S.dve, lambda e: e.tensor_copy(out=C.wvg.ap[:].rearrange("p k c -> p (k c)"), in_=st.ap[:, 0:1024]), reads=[st.b], writes=[C.b])
                    C.wvm = sb("wvm", [128, 2, 512], BF16)
                    st, _ = R.cast_stage()
                    for k in range(2):
                        S.dma(S.sp, st.ap[:, k * 512:(k + 1) * 512].rearrange("p (h c) -> p h c", c=64),
                              I["mla_wkv_up"][lb][k * 128:(k + 1) * 128, :].rearrange("p (h c) -> p h c", c=128)[:, :, 64:128], writes=[st.b])
                    S.op(S.dve, lambda e: e.tensor_copy(out=C.wvm.ap[:].rearrange("p k c -> p (k c)"), in_=st.ap[:, 0:1024]), reads=[st.b], writes=[C.b])
                    W.update({"win_s": SCR[lb]["win_s"], "wq_s": SCR[lb]["wq_s"], "wk_s": SCR[lb]["wk_s"]})
                xsrc = I["xT"] if la is None else xcur
                xin = xsrc.rearrange("(c p) t -> p c t", p=128)
                for ti in range(NTILE):
                    t0 = ti * TT
                    S.dma(S.sp, x.ap[:], xin[:, :, t0:t0 + TT], writes=[x.b])
                    if la is not None:
                        emit_merge(S, R, F, K, C, x, W, ti)
                        emit_ffn(S, R, F, x, C.f2g, SCR[la]["f2wi_s"], SCR[la]["f2wo_s"])
                    if lb is not None:
                        emit_ffn(S, R, F, x, C.f1g, SCR[lb]["f1wi_s"], SCR[lb]["f1wo_s"])
                        emit_mixprep(S, R, F, K, C, x, W, ti)
                    else:
                        S.dma(S.pool, xoutT.rearrange("(c p) t -> p c t", p=128)[:, :, t0:t0 + TT], x.ap[:], reads=[x.b])
                S.barrier()

        def p2_phase(l, tag):
            cc0 = S.cccnt
            S.collective(kg_s.rearrange("h d t -> (h d) t"), kgG.rearrange("r p t -> (r p) t"), GROUPS)
            S.collective(vg_s, vgG.rearrange("r t c -> (r t) c"), GROUPS)
            for ck in range(4):
                S.collective(vm_s[ck * 1024:(ck + 1) * 1024, :], vmG[ck].rearrange("r t c -> (r t) c"), GROUPS)
            for h in range(8):
                S.collective(km_s[h], kmG[h].rearrange("r p t -> (r p) t"), GROUPS)
            for nm in ("cx", "bg", "xb", "lg"):
                for jj in range(4):
                    S.collective(XS[nm][jj * 128:(jj + 1) * 128, :], XG[nm][jj].rearrange("r p t -> (r p) t"), GROUPS)
            n_x_cc = S.cccnt
            with ExitStack() as tes:
                sbt = lambda name, shape, dt: T(tes.enter_context(nc.sbuf_tensor(tag + name, shape, dt)), name)
                cb = Buf("consts")
                onesf = sbt("onesf", [128, 64], F32)
                S.op(S.dve, lambda e: e.memset(onesf.ap[:], 1.0), writes=[cb])
                onec = sbt("onec", [128, 1], F32)
                S.op(S.dve, lambda e: e.memset(onec.ap[:], 1.0), writes=[cb])
                scw = sbt("scw_sb", [128, 4], F32)
                lruw = sbt("lruw_sb", [128, 5], F32)
                lrub = sbt("lrub_sb", [128, 6], F32)
                S.dma(S.sp, scw.ap[:], I["scw"][l], writes=[cb])
                S.dma(S.sp, lruw.ap[:], I["lruw"][l], writes=[cb])
                S.dma(S.sp, lrub.ap[:], I["lrub"][l], writes=[cb])
                gst = sbt("gst", [128, 128], F32)
                gmat = [sbt("gmat%d" % i, [128, 128], BF16) for i in range(4)]
                for i in range(4):
                    S.dma(S.sp, gst.ap[:], I["lrug"][l, i], writes=[gst.b])
                    S.op(S.dve, lambda e: e.tensor_copy(out=gmat[i].ap[:], in_=gst.ap[:]), reads=[gst.b], writes=[cb])
                nlam = sbt("nlam", [128, 2], F32)
                for dr in range(2):
                    S.op(S.act, lambda e: e.activation(out=nlam.ap[:, dr:dr + 1], in_=lrub.ap[:, 3 * dr + 2:3 * dr + 3], func=AF.Exp, scale=-1.0),
                         reads=[cb], writes=[nlam.b])
                S.op(S.act, lambda e: e.activation(out=nlam.ap[:], in_=nlam.ap[:], func=AF.Ln, bias=onec.ap[:, 0:1]), reads=[nlam.b, cb], writes=[nlam.b])
                S.op(S.dve, lambda e: e.tensor_scalar(out=nlam.ap[:], in0=nlam.ap[:], scalar1=-8.0, scalar2=None, op0=ALU.mult), reads=[nlam.b], writes=[nlam.b])
                ft = {nm: sbt("l_" + nm, [128, CH + 6], F32) for nm in ("xc", "r", "i", "a", "m", "bv", "h0", "h1", "hf")}
                bt = {nm: sbt("l_" + nm, [128, CH + 6], BF16) for nm in ("xw", "bg", "lg")}
                xcb = sbt("l_xcb", [128, CH], BF16)
                outb = sbt("l_outb", [128, CH], BF16)

                def load_seq(dst, off, G, lo, hi):
                    Gj = G
                    t = lo
                    while t < hi:
                        r = t // NT
                        e_ = min(hi, (r + 1) * NT)
                        a_, b_ = t, e_
                        if b_ - a_ == 1:
                            if a_ - 1 >= r * NT:
                                a_ -= 1
                            else:
                                b_ += 1
                        S.dma(S.sp, dst.ap[:, off + (a_ - lo):off + (b_ - lo)], Gj[r, :, a_ - r * NT:b_ - r * NT], reads=[xlb], writes=[dst.b])
                        t = e_

                xlb = Buf("xlb")

                def conv_lru():
                    S._wait(S.sp, ("cc", S.ccsem, n_x_cc))
                    for nm in ("cx", "bg", "xb", "lg"):
                        S.dma(S.sp, XL[nm].rearrange("r p t -> (r p) t"), XG[nm][ds(jr, 1)].rearrange("o r p t -> (o r p) t"), writes=[xlb])
                    yield
                    for ci in range(NCHK):
                        t0 = ci * CH
                        cw, bgt, acc = bt["xw"], bt["bg"], ft["xc"]
                        lo, hi = max(t0 - 1, 0), min(t0 + CH + 1, SEQ)
                        if ci == 0:
                            S.op(S.pool, lambda e: e.memset(cw.ap[:, 1:2], 0.0), writes=[cw.b])
                        if ci == NCHK - 1:
                            S.op(S.pool, lambda e: e.memset(cw.ap[:, CH + 2:CH + 3], 0.0), writes=[cw.b])
                        load_seq(cw, 1 + lo - (t0 - 1), XL["cx"], lo, hi)
                        load_seq(bgt, 0, XL["bg"], t0, t0 + CH)
                        S.op(S.dve, lambda e: e.tensor_scalar(out=acc.ap[:, 0:CH], in0=cw.ap[:, 1:1 + CH], scalar1=scw.ap[:, 0:1], scalar2=scw.ap[:, 3:4],
                                                              op0=ALU.mult, op1=ALU.add), reads=[cw.b, cb], writes=[acc.b])
                        for k in (1, 2):
                            S.op(S.dve, lambda e: e.scalar_tensor_tensor(out=acc.ap[:, 0:CH], in0=cw.ap[:, 1 + k:1 + k + CH], scalar=scw.ap[:, k:k + 1], in1=acc.ap[:, 0:CH],
                                                                         op0=ALU.mult, op1=ALU.add), reads=[cw.b, cb, acc.b], writes=[acc.b])
                        S.op(S.pool, lambda e: e.tensor_tensor(out=outb.ap[:], in0=acc.ap[:, 0:CH], in1=bgt.ap[:, 0:CH], op=ALU.mult),
                             reads=[acc.b, bgt.b], writes=[outb.b])
                        q = t0 // NT
                        S.dma(S.pool, ob_s[q, :, t0 - q * NT:t0 - q * NT + CH], outb.ap[:], reads=[outb.b])
                        yield
                    hfb = Buf("hf_s")

                    def lru_chunk(ci, dr, hprev, hcur):
                        t0 = ci * CH
                        xw, xc = bt["xw"], ft["xc"]
                        lo, hi = max(t0 - 2, 0), min(t0 + CH + 1, SEQ)
                        if ci == 0:
                            S.op(S.pool, lambda e: e.memset(xw.ap[:, 1:3], 0.0), writes=[xw.b])
                        if ci == NCHK - 1:
                            S.op(S.pool, lambda e: e.memset(xw.ap[:, CH + 3:CH + 4], 0.0), writes=[xw.b])
                        load_seq(xw, 1 + lo - (t0 - 2), XL["xb"], lo, hi)
                        S.op(S.dve, lambda e: e.tensor_scalar(out=xc.ap[:, 0:CH], in0=xw.ap[:, 1:1 + CH], scalar1=lruw.ap[:, 0:1], scalar2=lruw.ap[:, 4:5],
                                                              op0=ALU.mult, op1=ALU.add), reads=[xw.b, cb], writes=[xc.b])
                        for k in (1, 2, 3):
                            S.op(S.dve, lambda e: e.scalar_tensor_tensor(out=xc.ap[:, 0:CH], in0=xw.ap[:, 1 + k:1 + k + CH], scalar=lruw.ap[:, k:k + 1], in1=xc.ap[:, 0:CH],
                                                                         op0=ALU.mult, op1=ALU.add), reads=[xw.b, cb, xc.b], writes=[xc.b])
                        S.op(S.pool, lambda e: e.tensor_copy(out=xcb.ap[:], in_=xc.ap[:, 0:CH]), reads=[xc.b], writes=[xcb.b])
                        yield
                        r, i_, a, m, bv = ft["r"], ft["i"], ft["a"], ft["m"], ft["bv"]
                        for n in range(CH // 512):
                            cs = slice(n * 512, (n + 1) * 512)
                            pr, pi = psum[6], psum[7]
                            S.op(S.pe, lambda e: e.matmul(pr.ap[:], lhsT=gmat[2 * dr].ap[:], rhs=xcb.ap[:, cs], start=True, stop=True), reads=[xcb.b, cb], writes=[pr.b])
                            S.op(S.pe, lambda e: e.matmul(pi.ap[:], lhsT=gmat[2 * dr + 1].ap[:], rhs=xcb.ap[:, cs], start=True, stop=True), reads=[xcb.b, cb], writes=[pi.b])
                            S.op(S.act, lambda e: e.activation(out=r.ap[:, cs], in_=pr.ap[:], func=AF.Sigmoid, bias=lrub.ap[:, 3 * dr:3 * dr + 1]),
                                 reads=[pr.b, cb], writes=[r.b])
                            S.op(S.act, lambda e: e.activation(out=i_.ap[:, cs], in_=pi.ap[:], func=AF.Sigmoid, bias=lrub.ap[:, 3 * dr + 1:3 * dr + 2]),
                                 reads=[pi.b, cb], writes=[i_.b])
                        S.op(S.act, lambda e: e.activation(out=a.ap[:, 0:CH], in_=r.ap[:, 0:CH], func=AF.Exp, scale=nlam.ap[:, dr:dr + 1]),
                             reads=[r.b, nlam.b], writes=[a.b])
                        S.op(S.pool, lambda e: e.tensor_tensor(out=m.ap[:, 0:CH], in0=a.ap[:, 0:CH], in1=a.ap[:, 0:CH], op=ALU.mult), reads=[a.b], writes=[m.b])
                        S.op(S.act, lambda e: e.activation(out=m.ap[:, 0:CH], in_=m.ap[:, 0:CH], func=AF.Sqrt, scale=-1.0, bias=onec.ap[:, 0:1]),
                             reads=[m.b, cb], writes=[m.b])
                        S.op(S.pool, lambda e: e.tensor_tensor(out=bv.ap[:, 0:CH], in0=i_.ap[:, 0:CH], in1=xc.ap[:, 0:CH], op=ALU.mult), reads=[i_.b, xc.b], writes=[bv.b])
                        S.op(S.dve, lambda e: e.tensor_tensor(out=bv.ap[:, 0:CH], in0=bv.ap[:, 0:CH], in1=m.ap[:, 0:CH], op=ALU.mult), reads=[bv.b, m.b], writes=[bv.b])
                        if dr == 0:
                            init = 0.0 if hprev is None else hprev.ap[:, CH - 1:CH]
                            S.op(S.dve, lambda e: e.tensor_tensor_scan(out=hcur.ap[:, 0:CH], data0=a.ap[:, 0:CH], data1=bv.ap[:, 0:CH], initial=init,
                                                                       op0=ALU.mult, op1=ALU.add),
                                 reads=[a.b, bv.b] + ([hprev.b] if hprev is not None else []), writes=[hcur.b])
                        else:
                            init = 0.0 if hprev is None else hprev.ap[:, 0:1]
                            S.op(S.dve, lambda e: e.tensor_tensor_scan(out=hcur.ap[:, 0:CH][:, ::-1], data0=a.ap[:, 0:CH][:, ::-1], data1=bv.ap[:, 0:CH][:, ::-1],
                                                                       initial=init, op0=ALU.mult, op1=ALU.add),
                                 reads=[a.b, bv.b] + ([hprev.b] if hprev is not None else []), writes=[hcur.b])

                    hprev = None
                    for ci in range(NCHK):
                        hcur = ft["h%d" % (ci % 2)]
                        yield from lru_chunk(ci, 0, hprev, hcur)
                        S.dma(S.pool, hf_s[:, ci * CH:(ci + 1) * CH], hcur.ap[:, 0:CH], reads=[hcur.b], writes=[hfb])
                        hprev = hcur
                        yield
                    hprev = None
                    for ci in range(NCHK - 1, -1, -1):
                        t0 = ci * CH
                        hcur = ft["h%d" % (ci % 2)]
                        yield from lru_chunk(ci, 1, hprev, hcur)
                        hf, lg = ft["hf"], bt["lg"]
                        S.dma(S.sp, hf.ap[:, 0:CH], hf_s[:, t0:t0 + CH], reads=[hfb], writes=[hf.b])
                        load_seq(lg, 0, XL["lg"], t0, t0 + CH)
                        S.op(S.pool, lambda e: e.tensor_tensor(out=hf.ap[:, 0:CH], in0=hf.ap[:, 0:CH], in1=hcur.ap[:, 0:CH], op=ALU.add), reads=[hf.b, hcur.b], writes=[hf.b])
                        S.op(S.pool, lambda e: e.tensor_tensor(out=outb.ap[:], in0=hf.ap[:, 0:CH], in1=lg.ap[:, 0:CH], op=ALU.mult), reads=[hf.b, lg.b], writes=[outb.b])
                        q = t0 // NT
                        S.dma(S.pool, od_s[q, :, t0 - q * NT:t0 - q * NT + CH], outb.ap[:], reads=[outb.b])
                        hprev = hcur
                        yield
                    S.dma_barrier(S.pool)
                    for q in range(4):
                        S.collective(ob_s[q], obG[q].rearrange("r p t -> (r p) t"), GROUPS)
                        S.collective(od_s[q], odG[q].rearrange("r p t -> (r p) t"), GROUPS)

                Kt = [sbt("Kt%d" % i, [128, SEQ], BF16) for i in range(2)]
                Vt = [sbt("Vt%d" % i, [128, SEQ // 128, 65], BF16) for i in range(2)]
                Qt = [sbt("Qt%d" % i, [128, NT], BF16) for i in range(2)]
                for kq in Kt + Qt:
                    S.op(S.pool, lambda e: e.memset(kq.ap[64:128, :], 0.0), writes=[kq.b])
                for v in Vt:
                    S.op(S.pool, lambda e: e.memset(v.ap[:, :, 64:65], 1.0), writes=[v.b])
                pt = [sbt("pt%d" % i, [128, 1024], BF16) for i in range(3)]
                spair = [T(psbig[0], "spair0"), T(psbig[1], "spair1")]
                osb = [sbt("osb%d" % i, [128, 264], F32) for i in range(2)]
                identf = sbt("identf", [128, 128], F32)
                S.dma(S.sp, identf.ap[:], I["cmat"][3], writes=[cb])
                obf = [sbt("obf%d" % i, [64, 512], BF16) for i in range(2)]
                sbank = psum[0:4]
                obank = psum[4:6]
                bbank = psum[6]
                cnt = {"q": 0, "o": 0, "kv": 0}
                lru_gen = conv_lru()

                class CastRes:
                    pass
                CR = CastRes()
                cst = [(sbt("cst%d" % i, [128, 1024], F32), sbt("cstb%d" % i, [128, 1024], BF16)) for i in range(2)]
                cn = {"s": 0, "e": 0}

                def _stage():
                    cn["s"] += 1
                    return cst[cn["s"] % 2]

                def _eng():
                    cn["e"] += 1
                    return (S.pool, S.dve)[cn["e"] % 2]
                CR.cast_stage = _stage
                CR.next_cast_eng = _eng
                def _cast_all():
                    yield from cast_layer(CR, l, False, True)
                    if l + 1 < NL:
                        yield from cast_layer(CR, l + 1, True, False)
                cast_gen = _cast_all()

                pending = []

                def flush_fin():
                    while pending:
                        pending.pop(0)()

                def attn_head(Kb, Vb, Qb, d, out_ap, dk):
                    scale = float(d) ** -0.5
                    for qg_ in range(NT // 512):
                        qcs = slice(qg_ * 512, (qg_ + 1) * 512)
                        po = obank[cnt["o"] % 2]
                        ob_ = obf[cnt["o"] % 2]
                        os_ = osb[cnt["o"] % 2]
                        cnt["o"] += 1

                        def s_mm(pi):
                            sp_ = spair[pi % 2]
                            for u in range(2):
                                kb = 2 * pi + u
                                S.op(S.pe, lambda e: e.matmul(sp_.ap[:, u * 512:(u + 1) * 512], lhsT=Kb.ap[0:dk, kb * 128:(kb + 1) * 128], rhs=Qb.ap[0:dk, qcs],
                                                              start=True, stop=True), reads=[Kb.b, Qb.b], writes=[sp_.b])
                        nkb = SEQ // 128
                        npair = nkb // 2
                        s_mm(0); s_mm(1)
                        for pi in range(npair):
                            sp_ = spair[pi % 2]
                            p_ = pt[pi % 3]
                            S.op(S.act, lambda e: e.activation(out=p_.ap[:], in_=sp_.ap[:], func=AF.Exp, scale=scale), reads=[sp_.b], writes=[p_.b])
                            if pi + 2 < npair:
                                s_mm(pi + 2)
                            for u in range(2):
                                kb = 2 * pi + u
                                for j in range(4):
                                    S.op(S.pe, lambda e: e.matmul(po.ap[:, j * 65:(j + 1) * 65], lhsT=p_.ap[:, u * 512 + j * 128:u * 512 + (j + 1) * 128],
                                                                  rhs=Vb.ap[:, kb, 0:65], start=(kb == 0 and j == 0), stop=(kb == nkb - 1), skip_group_check=True),
                                         reads=[Vb.b, p_.b], writes=[po.b])
                            if pi == 4:
                                flush_fin()
                        pov = po.ap[:, 0:260].rearrange("p (j c) -> p j c", c=65)
                        S.op(S.dve, lambda e: e.reciprocal(out=os_.ap[:, 256:260], in_=pov[:, :, 64]), reads=[po.b], writes=[os_.b])
                        for j in range(4):
                            S.op(S.dve, lambda e: e.tensor_scalar(out=os_.ap[:, j * 64:(j + 1) * 64], in0=po.ap[:, j * 65:j * 65 + 64],
                                                                  scalar1=os_.ap[:, 256 + j:257 + j], scalar2=None, op0=ALU.mult),
                                 reads=[po.b, os_.b], writes=[os_.b])

                        def fin(os_=os_, ob_=ob_, out_ap=out_ap, qcs=qcs):
                            for j in range(4):
                                S.op(S.pe, lambda e: e.transpose(out=bbank.ap[0:64, j * 128:(j + 1) * 128], in_=os_.ap[:, j * 64:(j + 1) * 64], identity=identf.ap[:]),
                                     reads=[os_.b, cb], writes=[bbank.b])
                            S.op(S.dve, lambda e: e.tensor_copy(out=ob_.ap[:], in_=bbank.ap[0:64, :]), reads=[bbank.b], writes=[ob_.b])
                            S.dma(S.pool, out_ap[:, qcs], ob_.ap[:], reads=[ob_.b])
                        pending.append(fin)
                        if cnt["o"] > 32:
                            next(lru_gen, None)
                            next(lru_gen, None)
                        for _ in range(3):
                            next(cast_gen, None)

                jobs = []
                for kvh in range(2):
                    for g in range(4):
                        h = kvh * 4 + g
                        jobs.append(("g", kvh, h, g == 0))
                for h in range(8):
                    jobs.append(("m", h, h, True))
                jstate = {}

                def prefetch(i):
                    kind, kvi, h, newkv = jobs[i]
                    if newkv:
                        Kb = Kt[cnt["kv"] % 2]
                        Vb = Vt[cnt["kv"] % 2]
                        cnt["kv"] += 1
                        if kind == "g":
                            S._wait(S.sp, ("cc", S.ccsem, cc0 + 2))
                            for r in range(4):
                                S.dma(S.sp, Kb.ap[0:64, r * NT:(r + 1) * NT], kgG[r, kvi * 64:(kvi + 1) * 64, :], writes=[Kb.b])
                                for hb in range(2):
                                    S.dma(S.sp, Vb.ap[:, r * 32 + hb * 16:r * 32 + hb * 16 + 16, 0:64],
                                          vgG[r, hb * 2048:(hb + 1) * 2048, kvi * 64:(kvi + 1) * 64].rearrange("(b p) c -> p b c", p=128), writes=[Vb.b])
                        else:
                            S._wait(S.sp, ("cc", S.ccsem, cc0 + 7 + h))
                            for r in range(4):
                                S.dma(S.sp, Kb.ap[0:96, r * NT:(r + 1) * NT], kmG[h, r], writes=[Kb.b])
                                for ck in range(4):
                                    S.dma(S.sp, Vb.ap[:, r * 32 + ck * 8:r * 32 + ck * 8 + 8, 0:64],
                                          vmG[ck, r, :, h * 64:(h + 1) * 64].rearrange("(b p) c -> p b c", p=128), writes=[Vb.b])
                        jstate["kv"] = (Kb, Vb)
                    Qb = Qt[cnt["q"] % 2]
                    cnt["q"] += 1
                    d = 64 if kind == "g" else 96
                    S.dma(S.sp, Qb.ap[0:d, :], (qg if kind == "g" else qm)[h], writes=[Qb.b])
                    return jstate["kv"] + (Qb,)

                nxt = prefetch(0)
                for i, (kind, kvi, h, newkv) in enumerate(jobs):
                    Kb, Vb, Qb = nxt
                    if i + 1 < len(jobs):
                        nxt = prefetch(i + 1)
                    if kind == "g":
                        attn_head(Kb, Vb, Qb, 64, oaT[h * 64:(h + 1) * 64, :], 128)
                    else:
                        attn_head(Kb, Vb, Qb, 96, ocT[h * 64:(h + 1) * 64, :], 96)
                flush_fin()
                for _ in lru_gen:
                    pass
                for _ in cast_gen:
                    pass
                S.barrier()

        tok_phase(None, 0, "t0_")
        for l in range(NL):
            p2_phase(l, "p%d_" % l)
            tok_phase(l, l + 1 if l + 1 < NL else None, "t%d_" % (l + 1))
        S.finish()
        print("build_fused ninst", S.ninst, flush=True)
    return nc


def _rope_perm(M, blocks):
    Rm = np.zeros((128, 128), np.float32)
    for (o, n) in blocks:
        hn = n // 2
        for i in range(hn):
            Rm[o + hn + i, o + i] = -1.0
            Rm[o + i, o + hn + i] = 1.0
    return Rm


def host_consts():
    bd64 = np.zeros((128, 128), np.float32)
    bd64[:64, :64] = 1.0
    bd64[64:, 64:] = 1.0
    permg = _rope_perm(128, [(0, 32), (32, 32), (64, 32), (96, 32)])
    permm = _rope_perm(96, [(64, 16), (80, 16)])
    ident = np.eye(128, dtype=np.float32)
    return np.stack([bd64, permg, permm, ident])


def host_rope_tables(tok0):
    t = np.arange(tok0, tok0 + NT)
    row = (t // 64).astype(np.float32)
    col = (t % 64).astype(np.float32)

    def ang(pos, dim):
        inv = (10000.0 ** (-np.arange(0, dim, 2, dtype=np.float32) / dim)).astype(np.float32)
        return (pos[:, None] * inv[None, :]).astype(np.float32)

    def tables(M, blocks):
        cos = np.ones((M, NT), np.float32)
        sin = np.zeros((M, NT), np.float32)
        for (o, n, pos) in blocks:
            a = ang(pos, n)
            c, s = np.cos(a).T, np.sin(a).T
            hn = n // 2
            cos[o:o + hn] = c
            cos[o + hn:o + n] = c
            sin[o:o + hn] = s
            sin[o + hn:o + n] = s
        return cos, sin
    cosg, sing = tables(128, [(0, 32, row), (32, 32, col), (64, 32, row), (96, 32, col)])
    cosm, sinm = tables(96, [(64, 16, row), (80, 16, col)])
    return cosg, sing, cosm, sinm


NCORES = 8
_PROG = []


def _c(a):
    return np.ascontiguousarray(a)


def kernel(**inp):
    inp = {k: np.asarray(v) for k, v in inp.items()}
    x = inp["x"]
    if not _PROG:
        _PROG.append(build_fused())
    cm = host_consts()
    maps = []
    for c in range(NCORES):
        b, j = c // 4, c % 4
        ch = slice(128 * j, 128 * (j + 1))
        m = {"xT": _c(x[b, j * NT:(j + 1) * NT].T), "cmat": cm}
        m["cosg"], m["sing"], m["cosm"], m["sinm"] = host_rope_tables(j * NT)
        for nm in WNAMES:
            m[nm] = _c(inp[nm])
        m["scw"] = _c(np.concatenate([inp["sc_conv_w"][:, :, ch].transpose(0, 2, 1), inp["sc_conv_b"][:, ch][:, :, None]], axis=2))
        m["lruw"] = _c(np.concatenate([inp["lru_conv_w"][:, :, ch].transpose(0, 2, 1), inp["lru_conv_b"][:, ch][:, :, None]], axis=2))
        gm = np.zeros((NL, 4, 128, 128), np.float32)
        for dr in range(2):
            for k, nm in enumerate(("lru_wa", "lru_wx")):
                for blk in range(2):
                    gm[:, 2 * dr + k, 64 * blk:64 * (blk + 1), 64 * blk:64 * (blk + 1)] = inp[nm][:, dr, 2 * j + blk]
        m["lrug"] = gm
        m["lrub"] = _c(np.stack([inp["lru_ba"][:, 0, ch], inp["lru_bx"][:, 0, ch], inp["lru_lambda"][:, 0, ch],
                                 inp["lru_ba"][:, 1, ch], inp["lru_bx"][:, 1, ch], inp["lru_lambda"][:, 1, ch]], axis=2))
        maps.append(m)
    res = run_bass_kernel_spmd(_PROG[0], maps, core_ids=list(range(NCORES))).results
    out = np.empty((B_, SEQ, D), np.float32)
    for c in range(NCORES):
        b, j = c // 4, c % 4
        out[b, j * NT:(j + 1) * NT] = res[c]["xoutT"].T
    return out
```

```python
import numpy as np
from contextlib import ExitStack
import concourse.bass as bass
import concourse.mybir as mybir
from concourse.bass_utils import run_bass_kernel_spmd

F32 = mybir.dt.float32
BF16 = mybir.dt.bfloat16
ALU = mybir.AluOpType
AF = mybir.ActivationFunctionType


class Buf:
    __slots__ = ("name", "w", "r")

    def __init__(self, name=""):
        self.name = name
        self.w = None
        self.r = {}


class Eng:
    def __init__(self, name, h, sem, same_sync=True):
        self.name = name
        self.h = h
        self.sem = sem
        self.cnt = 0
        self.seen = {}
        self.same_sync = same_sync


class Sch:
    def __init__(self, nc, es, n_dma_sems=40):
        self.nc = nc
        self.es = es
        mk = lambda n, h, ss=True: Eng(n, h, es.enter_context(nc.semaphore("p_" + n)), ss)
        self.pe = mk("pe", nc.tensor, False)
        self.act = mk("act", nc.scalar)
        self.dve = mk("dve", nc.vector)
        self.pool = mk("pool", nc.gpsimd)
        self.sp = mk("sp", nc.sync)
        self.dsems = [es.enter_context(nc.semaphore("d%d" % i)) for i in range(n_dma_sems)]
        self.dvals = [0] * n_dma_sems
        self.dnext = 0
        self.ninst = 0
        self.ccsem = None
        self.cccnt = 0

    def _wait(self, eng, ev):
        if ev is None:
            return
        key, sem, val = ev
        if key == eng.name and not eng.same_sync:
            return
        if eng.seen.get(key, 0) >= val:
            return
        eng.h.wait_ge(sem, val)
        eng.seen[key] = val

    def _deps(self, eng, reads, writes):
        for b in reads:
            self._wait(eng, b.w)
        for b in writes:
            self._wait(eng, b.w)
            for ev in b.r.values():
                self._wait(eng, ev)

    def _record(self, ev, reads, writes):
        for b in reads:
            b.r[ev[0]] = ev
        for b in writes:
            b.w = ev
            b.r = {}

    def op(self, eng, fn, reads=(), writes=()):
        self._deps(eng, reads, writes)
        ins = fn(eng.h)
        eng.cnt += 1
        ins.then_inc(eng.sem, 1)
        ev = (eng.name, eng.sem, eng.cnt)
        self._record(ev, reads, writes)
        self.ninst += 1
        return ev

    def dma(self, q, out, in_, reads=(), writes=(), **kw):
        i = self.dnext
        self.dnext = (self.dnext + 1) % len(self.dsems)
        sem = self.dsems[i]
        key = "d%d" % i
        if self.dvals[i] > 0:
            self._wait(q, (key, sem, self.dvals[i]))
        self._deps(q, reads, writes)
        q.h.dma_start(out=out, in_=in_, **kw).then_inc(sem, 16)
        self.dvals[i] += 16
        ev = (key, sem, self.dvals[i])
        self._record(ev, reads, writes)
        self.ninst += 1
        return ev

    def dma_barrier(self, q):
        for i, sem in enumerate(self.dsems):
            if self.dvals[i] > 0:
                self._wait(q, ("d%d" % i, sem, self.dvals[i]))

    def collective(self, src, dst, groups):
        q = self.pool
        if self.ccsem is None:
            self.ccsem = self.es.enter_context(self.nc.semaphore("ccsem"))
            self.cccnt = 0
        self.nc.gpsimd.collective_compute("AllGather", ALU.bypass, replica_groups=groups, ins=[src], outs=[dst]).then_inc(self.ccsem)
        self.cccnt += 1
        self.ninst += 1

    def barrier(self):
        engs = (self.pe, self.act, self.dve, self.pool, self.sp)
        for e in engs:
            for o in engs:
                if o is not e and o.cnt > 0:
                    self._wait(e, (o.name, o.sem, o.cnt))
            self.dma_barrier(e)
            if self.ccsem is not None and self.cccnt > 0:
                self._wait(e, ("cc", self.ccsem, self.cccnt))

    def cc_wait(self, eng):
        if self.ccsem is not None and self.cccnt > 0:
            self._wait(eng, ("cc", self.ccsem, self.cccnt))

    def finish(self):
        for e in (self.pe, self.act, self.dve, self.pool):
            if e.cnt > 0:
                self._wait(self.sp, (e.name, e.sem, e.cnt))
        self.dma_barrier(self.sp)
        self.cc_wait(self.sp)
D = 1024
DFF = 2816
KC = 8
FC = 22
EPS = 1e-6


def copy_on(S, eng, out, in_, reads, writes):
    if eng is S.act:
        return S.op(eng, lambda e: e.activation(out=out, in_=in_, func=AF.Copy), reads=reads, writes=writes)
    return S.op(eng, lambda e: e.tensor_copy(out=out, in_=in_), reads=reads, writes=writes)


def cast_weight(S, R, src, dst, runs):
    K, N = src.shape
    kcn = K // 128
    for kc in range(kcn):
        for (c0, nrun, s0, sst, width) in runs:
            i = 0
            while i < nrun:
                ns = min(8, nrun - i)
                w = ns * 128 if width == 128 else width
                cc = c0 + i * 128
                st, stb = R.cast_stage()
                S.dma(S.sp, st.ap[:, 0:w], src[kc * 128:(kc + 1) * 128, cc:cc + w], writes=[st.b])
                eng = R.next_cast_eng()
                copy_on(S, eng, stb.ap[:, 0:w], st.ap[:, 0:w], [st.b], [stb.b])
                sa = s0 + i * sst
                if width == 128:
                    o = dst[sa:sa + (ns - 1) * sst + 1:sst, :, kc, :].rearrange("s p c -> p s c")
                    i_ = stb.ap[:, 0:w].rearrange("p (s c) -> p s c", c=128)
                else:
                    o = dst[sa, :, kc, 0:w]
                    i_ = stb.ap[:, 0:w]
                S.dma(S.pool, o, i_, reads=[stb.b])
                i += ns
                yield


class T:
    def __init__(self, ap, name=""):
        self.ap = ap
        self.b = Buf(name)


class Res:
    def __init__(self, nc, es, S, TT, psum=None, tag="", nstage=2):
        self.nc, self.S, self.TT = nc, S, TT
        sb = lambda name, shape, dt: T(es.enter_context(nc.sbuf_tensor(tag + name, shape, dt)), name)
        self.sb = sb
        if psum is None:
            psum = [T(es.enter_context(nc.psum_tensor("ps%d" % i, [128, 512], F32)), "ps%d" % i) for i in range(8)]
        self.psum = psum
        self.pnext = 0
        self.cstage = [(sb("cst%d" % i, [128, 1024], F32), sb("cstb%d" % i, [128, 1024], BF16)) for i in range(nstage)]
        self.cnext = 0
        self.ceng = 0
        self.wscr_buf = Buf("wscr")
        self.ones = sb("ones", [128, 128], BF16)
        S.op(S.dve, lambda e: e.memset(self.ones.ap[:], 1.0), writes=[self.ones.b])
        self.eps = sb("epsc", [128, 1], F32)
        S.op(S.dve, lambda e: e.memset(self.eps.ap[:], EPS), writes=[self.eps.b])

    def ps(self):
        t = self.psum[self.pnext]
        self.pnext = (self.pnext + 1) % 8
        return t

    def cast_stage(self):
        t = self.cstage[self.cnext]
        self.cnext = (self.cnext + 1) % len(self.cstage)
        return t

    def next_cast_eng(self):
        S = self.S
        e = [S.pool, S.dve, S.act][self.ceng % 3]
        self.ceng += 1
        return e


class FFNRes:
    def __init__(self, R):
        TT = R.TT
        sb = R.sb
        self.sq = sb("f_sq", [128, KC, TT], BF16)
        self.rstd = sb("f_rstd", [128, TT], F32)
        self.h = sb("f_h", [128, KC, TT], BF16)
        self.actb = sb("f_act", [128, FC, TT], BF16)
        self.wi = [sb("f_wi%d" % i, [128, 2, KC, 128], BF16) for i in range(3)]
        self.wo = [sb("f_wo%d" % i, [128, FC, 128], BF16) for i in range(2)]
        self.sil = [sb("f_sil%d" % i, [128, 512], F32) for i in range(2)]
        self.wi_n = 0
        self.wo_n = 0
        self.sil_n = 0


def emit_rmsnorm(S, R, x, gcol, h, sq, rstd, nch, TT, dim):
    for c in range(nch):
        S.op(S.pool, lambda e: e.tensor_tensor(out=sq.ap[:, c, :], in0=x.ap[:, c, :], in1=x.ap[:, c, :], op=ALU.mult),
             reads=[x.b], writes=[sq.b])
    for n in range(TT // 512):
        cs = slice(n * 512, (n + 1) * 512)
        p = R.ps()
        for c in range(nch):
            S.op(S.pe, lambda e: e.matmul(p.ap[:], lhsT=R.ones.ap[:], rhs=sq.ap[:, c, cs], start=(c == 0), stop=(c == nch - 1)),
                 reads=[R.ones.b, sq.b], writes=[p.b])
        S.op(S.act, lambda e: e.activation(out=rstd.ap[:, cs], in_=p.ap[:], func=AF.Sqrt, scale=1.0 / dim, bias=R.eps.ap[:, 0:1]),
             reads=[p.b, R.eps.b], writes=[rstd.b])
    S.op(S.dve, lambda e: e.reciprocal(out=rstd.ap[:], in_=rstd.ap[:]), reads=[rstd.b], writes=[rstd.b])
    for c in range(nch):
        S.op(S.dve, lambda e: e.scalar_tensor_tensor(out=h.ap[:, c, :], in0=x.ap[:, c, :], scalar=gcol.ap[:, c:c + 1], in1=rstd.ap[:],
                                                   op0=ALU.mult, op1=ALU.mult),
             reads=[x.b, gcol.b, rstd.b], writes=[h.b])


def emit_ffn(S, R, F, x, gcol, wi_s, wo_s):
    TT = R.TT
    NG = TT // 512
    emit_rmsnorm(S, R, x, gcol, F.h, F.sq, F.rstd, KC, TT, D)
    for f in range(FC):
        w = F.wi[F.wi_n % 3]; F.wi_n += 1
        S.dma(S.sp, w.ap[:], wi_s[2 * f:2 * f + 2].rearrange("t p k c -> p t k c"), writes=[w.b])
        for n in range(NG):
            cs = slice(n * 512, (n + 1) * 512)
            pg = R.ps(); pu = R.ps()
            for k in range(KC):
                S.op(S.pe, lambda e: e.matmul(pg.ap[:], lhsT=w.ap[:, 0, k, :], rhs=F.h.ap[:, k, cs], start=(k == 0), stop=(k == KC - 1)),
                     reads=[w.b, F.h.b], writes=[pg.b])
            for k in range(KC):
                S.op(S.pe, lambda e: e.matmul(pu.ap[:], lhsT=w.ap[:, 1, k, :], rhs=F.h.ap[:, k, cs], start=(k == 0), stop=(k == KC - 1)),
                     reads=[w.b, F.h.b], writes=[pu.b])
            sl = F.sil[F.sil_n % 2]; F.sil_n += 1
            S.op(S.act, lambda e: e.activation(out=sl.ap[:], in_=pg.ap[:], func=AF.Silu), reads=[pg.b], writes=[sl.b])
            S.op(S.dve, lambda e: e.tensor_tensor(out=F.actb.ap[:, f, cs], in0=sl.ap[:], in1=pu.ap[:], op=ALU.mult),
                 reads=[sl.b, pu.b], writes=[F.actb.b])
    for d in range(KC):
        w = F.wo[F.wo_n % 2]; F.wo_n += 1
        S.dma(S.sp, w.ap[:], wo_s[d], writes=[w.b])
        for n in range(NG):
            cs = slice(n * 512, (n + 1) * 512)
            p = R.ps()
            for fc in range(FC):
                S.op(S.pe, lambda e: e.matmul(p.ap[:], lhsT=w.ap[:, fc, :], rhs=F.actb.ap[:, fc, cs], start=(fc == 0), stop=(fc == FC - 1)),
                     reads=[w.b, F.actb.b], writes=[p.b])
            S.op(S.dve, lambda e: e.scalar_tensor_tensor(out=x.ap[:, d, cs], in0=p.ap[:], scalar=0.5, in1=x.ap[:, d, cs], op0=ALU.mult, op1=ALU.add),
                 reads=[p.b, x.b], writes=[x.b])


B_, SEQ, NL = 2, 16384, 4
NT = 4096
TT = 1024
NTILE = NT // TT
NG = TT // 512
INTOT = 8096
WIN_RUNS = [(0, 23, 0, 1, 128), (2944, 1, 23, 1, 32), (2976, 8, 24, 1, 128), (4000, 32, 32, 1, 128)]
SEG_Q, SEG_K, SEG_BG, SEG_CG, SEG_XS, SEG_QL, SEG_KVL, SEG_KR, SEG_LG, SEG_XB, SEG_GATE = 0, 4, 6, 10, 14, 18, 21, 23, 24, 28, 32


class Tok:
    def __init__(self, R):
        sb = R.sb
        self.tf = [sb("tf%d" % i, [128, 512], F32) for i in range(6)]
        self.tb = [sb("tb%d" % i, [128, 512], BF16) for i in range(8)]
        self.wseg = [sb("wseg%d" % i, [128, 8, 128], BF16) for i in range(4)]
        self.wbr = [sb("wbr%d" % i, [128, 4, 4, 128], BF16) for i in range(2)]
        self.cs2 = [[sb("cs%d_%d" % (j, i), [128, 512], F32) for i in range(4)] for j in range(2)]
        self.n = {"tf": 0, "tb": 0, "wseg": 0, "wbr": 0}

    def get(self, kind):
        lst = getattr(self, kind)
        t = lst[self.n[kind] % len(lst)]
        self.n[kind] += 1
        return t


def load_cols(S, dst, src_vec, nch):
    S.dma(S.sp, dst.ap[:, 0:nch], src_vec.rearrange("(c p) -> p c", p=128), writes=[dst.b], allow_slow_non_contiguous=True)


def headnorm_rope(S, R, K, C, p, M, ones_ap, gcol_ap, perm_ap, cos_ap, sin_ap, dim, out_ap):
    sqb = K.get("tb")
    S.op(S.act, lambda e: e.activation(out=sqb.ap[0:M, :], in_=p.ap[0:M, :], func=AF.Square), reads=[p.b], writes=[sqb.b])
    p2 = R.ps()
    S.op(S.pe, lambda e: e.matmul(p2.ap[0:M, :], lhsT=ones_ap, rhs=sqb.ap[0:M, :], start=True, stop=True),
         reads=[sqb.b, C.b], writes=[p2.b])
    rs = K.get("tf")
    S.op(S.act, lambda e: e.activation(out=rs.ap[0:M, :], in_=p2.ap[0:M, :], func=AF.Sqrt, scale=1.0 / dim, bias=R.eps.ap[0:M, 0:1]),
         reads=[p2.b, R.eps.b], writes=[rs.b])
    S.op(S.dve, lambda e: e.reciprocal(out=rs.ap[0:M, :], in_=rs.ap[0:M, :]), reads=[rs.b], writes=[rs.b])
    qn = K.get("tb")
    S.op(S.dve, lambda e: e.scalar_tensor_tensor(out=qn.ap[0:M, :], in0=p.ap[0:M, :], scalar=gcol_ap, in1=rs.ap[0:M, :], op0=ALU.mult, op1=ALU.mult),
         reads=[p.b, rs.b, C.b], writes=[qn.b])
    p3 = R.ps()
    S.op(S.pe, lambda e: e.matmul(p3.ap[0:M, :], lhsT=perm_ap, rhs=qn.ap[0:M, :], start=True, stop=True),
         reads=[qn.b, C.b], writes=[p3.b])
    t1 = K.get("tf")
    S.op(S.pool, lambda e: e.tensor_tensor(out=t1.ap[0:M, :], in0=qn.ap[0:M, :], in1=cos_ap, op=ALU.mult), reads=[qn.b, K.csb], writes=[t1.b])
    t2 = K.get("tf")
    S.op(S.dve, lambda e: e.tensor_tensor(out=t2.ap[0:M, :], in0=p3.ap[0:M, :], in1=sin_ap, op=ALU.mult), reads=[p3.b, K.csb], writes=[t2.b])
    ob = K.get("tb")
    S.op(S.pool, lambda e: e.tensor_tensor(out=ob.ap[0:M, :], in0=t1.ap[0:M, :], in1=t2.ap[0:M, :], op=ALU.add), reads=[t1.b, t2.b], writes=[ob.b])
    S.dma(S.pool, out_ap, ob.ap[0:M, :], reads=[ob.b])


class Consts:
    pass


def emit_mixprep(S, R, F, K, C, x, W, ti):
    t0 = ti * TT
    hm = F.h
    emit_rmsnorm(S, R, x, C.mixg, hm, F.sq, F.rstd, KC, TT, D)
    S.dma(S.pool, W["x1T"].rearrange("(c p) t -> p c t", p=128)[:, :, t0:t0 + TT], x.ap[:], reads=[x.b])
    qln = [F.actb.ap[:, k, :] for k in range(3)]
    kvn = [F.actb.ap[:, 3 + k, :] for k in range(2)]
    krope = F.actb.ap[0:32, 5, :]
    ab = F.actb.b

    def load_seg(seg):
        w = K.get("wseg")
        S.dma(S.sp, w.ap[:], W["win_s"][seg], writes=[w.b])
        return w

    def proj(w, n, M=128):
        cs = slice(n * 512, (n + 1) * 512)
        p = R.ps()
        for k in range(KC):
            S.op(S.pe, lambda e: e.matmul(p.ap[0:M, :], lhsT=w.ap[:, k, 0:M], rhs=hm.ap[:, k, cs], start=(k == 0), stop=(k == KC - 1)),
                 reads=[w.b, hm.b], writes=[p.b])
        return p

    def store_f32(p, dram_ap, func=None):
        t = K.get("tb")
        if func is None:
            S.op(S.act, lambda e: e.activation(out=t.ap[:], in_=p.ap[:], func=AF.Copy), reads=[p.b], writes=[t.b])
        else:
            S.op(S.act, lambda e: e.activation(out=t.ap[:], in_=p.ap[:], func=func), reads=[p.b], writes=[t.b])
        S.dma(S.pool, dram_ap, t.ap[:], reads=[t.b])

    for (seg0, name, func) in ((SEG_BG, "bgT", None), (SEG_LG, "lgT", AF.Gelu), (SEG_XB, "xbT", None)):
        for c in range(4):
            w = load_seg(seg0 + c)
            for n in range(NG):
                p = proj(w, n)
                store_f32(p, W[name][c * 128:(c + 1) * 128, t0 + n * 512:t0 + (n + 1) * 512], func)
    for c in range(4):
        wc = load_seg(SEG_CG + c)
        wx = load_seg(SEG_XS + c)
        for n in range(NG):
            pc = proj(wc, n)
            px = proj(wx, n)
            t = K.get("tf")
            S.op(S.act, lambda e: e.activation(out=t.ap[:], in_=pc.ap[:], func=AF.Copy), reads=[pc.b], writes=[t.b])
            t2 = K.get("tb")
            S.op(S.dve, lambda e: e.tensor_tensor(out=t2.ap[:], in0=t.ap[:], in1=px.ap[:], op=ALU.mult), reads=[t.b, px.b], writes=[t2.b])
            S.dma(S.pool, W["cxT"][c * 128:(c + 1) * 128, t0 + n * 512:t0 + (n + 1) * 512], t2.ap[:], reads=[t2.b])
    w = load_seg(SEG_KR)
    for n in range(NG):
        p = proj(w, n, M=32)
        S.op(S.act, lambda e: e.activation(out=krope[:, n * 512:(n + 1) * 512], in_=p.ap[0:32, :], func=AF.Copy), reads=[p.b], writes=[ab])
    for (seg0, nch, dst, gc, dim) in ((SEG_QL, 3, qln, C.qag, 384), (SEG_KVL, 2, kvn, C.kvag, 256)):
        ws = [load_seg(seg0 + c) for c in range(nch)]
        for n in range(NG):
            cs = slice(n * 512, (n + 1) * 512)
            ps_ = [proj(ws[c], n) for c in range(nch)]
            p2 = R.ps()
            for c in range(nch):
                sqb = K.get("tb")
                S.op(S.act, lambda e: e.activation(out=sqb.ap[:], in_=ps_[c].ap[:], func=AF.Square), reads=[ps_[c].b], writes=[sqb.b])
                S.op(S.pe, lambda e: e.matmul(p2.ap[:], lhsT=R.ones.ap[:], rhs=sqb.ap[:], start=(c == 0), stop=(c == nch - 1)),
                     reads=[sqb.b, R.ones.b], writes=[p2.b])
            rs = K.get("tf")
            S.op(S.act, lambda e: e.activation(out=rs.ap[:], in_=p2.ap[:], func=AF.Sqrt, scale=1.0 / dim, bias=R.eps.ap[:, 0:1]),
                 reads=[p2.b, R.eps.b], writes=[rs.b])
            S.op(S.dve, lambda e: e.reciprocal(out=rs.ap[:], in_=rs.ap[:]), reads=[rs.b], writes=[rs.b])
            for c in range(nch):
                S.op(S.dve, lambda e: e.scalar_tensor_tensor(out=dst[c][:, cs], in0=ps_[c].ap[:], scalar=gc.ap[:, c:c + 1], in1=rs.ap[:],
                                                           op0=ALU.mult, op1=ALU.mult),
                     reads=[ps_[c].b, rs.b, C.b], writes=[ab])
    def load_cs(n):
        g0 = t0 + n * 512
        S.dma(S.sp, K.cs[0].ap[:], W["cosg"][:, g0:g0 + 512], writes=[K.csb])
        S.dma(S.sp, K.cs[1].ap[:], W["sing"][:, g0:g0 + 512], writes=[K.csb])
        S.dma(S.sp, K.cs[2].ap[0:96, :], W["cosm"][:, g0:g0 + 512], writes=[K.csb])
        S.dma(S.sp, K.cs[3].ap[0:96, :], W["sinm"][:, g0:g0 + 512], writes=[K.csb])

    qg_flat = W["qg"].rearrange("h d t -> (h d) t")
    kg_flat = W["kg"].rearrange("h d t -> (h d) t")
    for n in range(NG):
        gcs = slice(t0 + n * 512, t0 + (n + 1) * 512)
        p = R.psum[7]
        for j in range(4):
            for k in range(KC):
                S.op(S.pe, lambda e: e.matmul(p.ap[:, j * 128:(j + 1) * 128], lhsT=hm.ap[:, k, n * 512 + j * 128:n * 512 + (j + 1) * 128],
                                              rhs=C.wvg.ap[:, k, :], start=(k == 0), stop=(k == KC - 1)),
                     reads=[hm.b, C.b], writes=[p.b])
        vb = K.get("tb")
        S.op(S.act, lambda e: e.activation(out=vb.ap[:], in_=p.ap[:], func=AF.Copy), reads=[p.b], writes=[vb.b])
        S.dma(S.pool, W["vg"][gcs, :].rearrange("(j p) c -> p j c", p=128), vb.ap[:].rearrange("p (j c) -> p j c", c=128), reads=[vb.b])
        for j in range(4):
            p = R.psum[7]
            for k in range(2):
                S.op(S.pe, lambda e: e.matmul(p.ap[:], lhsT=kvn[k][:, n * 512 + j * 128:n * 512 + (j + 1) * 128], rhs=C.wvm.ap[:, k, :],
                                              start=(k == 0), stop=(k == 1)),
                     reads=[ab, C.b], writes=[p.b])
            vb = K.get("tb")
            S.op(S.act, lambda e: e.activation(out=vb.ap[:], in_=p.ap[:], func=AF.Copy), reads=[p.b], writes=[vb.b])
            r0 = t0 + n * 512 + j * 128
            S.dma(S.pool, W["vm"][r0:r0 + 128, :], vb.ap[:], reads=[vb.b])

    items = []
    for n in range(NG):
        for c in range(5):
            items.append(("g", n, c))
        for h in range(8):
            items.append(("q", n, h))
        for h in range(8):
            items.append(("k", n, h))
    ctx = [dict() for _ in items]
    pb_p, pb_2, pb_3 = R.psum[0:3], R.psum[3:5], R.psum[5:7]

    def st_load(i):
        kind, n, c = items[i]
        w = K.get("wseg")
        if kind == "g":
            S.dma(S.sp, w.ap[:], W["win_s"][SEG_Q + c], writes=[w.b])
        elif kind == "q":
            S.dma(S.sp, w.ap[:, 0:3, :], W["wq_s"][c], writes=[w.b])
        else:
            S.dma(S.sp, w.ap[:, 0:2, :], W["wk_s"][c], writes=[w.b])
        ctx[i]["w"] = w
        if (kind, c) == ("g", 0):
            cset = K.cs2[n % 2]
            g0 = t0 + n * 512
            S.dma(S.sp, cset[0].ap[:], W["cosg"][:, g0:g0 + 512], writes=[cset[0].b])
            S.dma(S.sp, cset[1].ap[:], W["sing"][:, g0:g0 + 512], writes=[cset[1].b])
            S.dma(S.sp, cset[2].ap[0:96, :], W["cosm"][:, g0:g0 + 512], writes=[cset[2].b])
            S.dma(S.sp, cset[3].ap[0:96, :], W["sinm"][:, g0:g0 + 512], writes=[cset[3].b])

    def st_proj(i):
        kind, n, c = items[i]
        cs = slice(n * 512, (n + 1) * 512)
        w = ctx[i]["w"]
        p = pb_p[i % 3]
        if kind == "g":
            for k in range(KC):
                S.op(S.pe, lambda e: e.matmul(p.ap[:], lhsT=w.ap[:, k, :], rhs=hm.ap[:, k, cs], start=(k == 0), stop=(k == KC - 1)),
                     reads=[w.b, hm.b], writes=[p.b])
        elif kind == "q":
            for k in range(3):
                S.op(S.pe, lambda e: e.matmul(p.ap[0:96, :], lhsT=w.ap[:, k, 0:96], rhs=qln[k][:, cs], start=(k == 0), stop=(k == 2)),
                     reads=[w.b, ab], writes=[p.b])
        else:
            for k in range(2):
                S.op(S.pe, lambda e: e.matmul(p.ap[0:64, :], lhsT=w.ap[:, k, 0:64], rhs=kvn[k][:, cs], start=(k == 0), stop=(k == 1)),
                     reads=[w.b, ab], writes=[p.b])
            S.op(S.pe, lambda e: e.matmul(p.ap[64:96, :], lhsT=C.ident.ap[0:32, 0:32], rhs=krope[:, cs], start=True, stop=True),
                 reads=[C.b, ab], writes=[p.b])
        ctx[i]["p"] = p

    def params(i):
        kind, n, c = items[i]
        gcs = slice(t0 + n * 512, t0 + (n + 1) * 512)
        cset = K.cs2[n % 2]
        if kind == "g":
            gcol = C.gq.ap[:, 0:1] if c < 4 else C.gk.ap[:, 0:1]
            dst = qg_flat[c * 128:(c + 1) * 128, gcs] if c < 4 else kg_flat[:, gcs]
            return 128, C.bd64.ap[:], gcol, C.permg.ap[:], cset[0], cset[1], 64, dst
        gcol = C.mq.ap[0:96, 0:1] if kind == "q" else C.mk.ap[0:96, 0:1]
        dst = W["qm"][c, :, gcs] if kind == "q" else W["km"][c, :, gcs]
        return 96, R.ones.ap[0:96, 0:96], gcol, C.permm.ap[0:96, 0:96], cset[2], cset[3], 96, dst

    def st_sq(i):
        M = params(i)[0]
        p = ctx[i]["p"]
        sqb = K.get("tb")
        S.op(S.act, lambda e: e.activation(out=sqb.ap[0:M, :], in_=p.ap[0:M, :], func=AF.Square), reads=[p.b], writes=[sqb.b])
        ctx[i]["sqb"] = sqb

    def st_ones(i):
        M, ones_ap = params(i)[0:2]
        sqb = ctx[i]["sqb"]
        p2 = pb_2[i % 2]
        S.op(S.pe, lambda e: e.matmul(p2.ap[0:M, :], lhsT=ones_ap, rhs=sqb.ap[0:M, :], start=True, stop=True), reads=[sqb.b, C.b, R.ones.b], writes=[p2.b])
        ctx[i]["p2"] = p2

    def st_norm(i):
        M, _, gcol, _, _, _, dim, _ = params(i)
        p, p2 = ctx[i]["p"], ctx[i]["p2"]
        rs = K.get("tf")
        S.op(S.act, lambda e: e.activation(out=rs.ap[0:M, :], in_=p2.ap[0:M, :], func=AF.Sqrt, scale=1.0 / dim, bias=R.eps.ap[0:M, 0:1]),
             reads=[p2.b, R.eps.b], writes=[rs.b])
        S.op(S.dve, lambda e: e.reciprocal(out=rs.ap[0:M, :], in_=rs.ap[0:M, :]), reads=[rs.b], writes=[rs.b])
        qn = K.get("tb")
        S.op(S.dve, lambda e: e.scalar_tensor_tensor(out=qn.ap[0:M, :], in0=p.ap[0:M, :], scalar=gcol, in1=rs.ap[0:M, :], op0=ALU.mult, op1=ALU.mult),
             reads=[p.b, rs.b, C.b], writes=[qn.b])
        ctx[i]["qn"] = qn

    def st_perm(i):
        M, _, _, perm_ap = params(i)[0:4]
        qn = ctx[i]["qn"]
        p3 = pb_3[i % 2]
        S.op(S.pe, lambda e: e.matmul(p3.ap[0:M, :], lhsT=perm_ap, rhs=qn.ap[0:M, :], start=True, stop=True), reads=[qn.b, C.b], writes=[p3.b])
        ctx[i]["p3"] = p3

    def st_rope(i):
        M, _, _, _, cosT, sinT, _, dst = params(i)
        qn, p3 = ctx[i]["qn"], ctx[i]["p3"]
        t1 = K.get("tf")
        S.op(S.pool, lambda e: e.tensor_tensor(out=t1.ap[0:M, :], in0=qn.ap[0:M, :], in1=cosT.ap[0:M, :], op=ALU.mult), reads=[qn.b, cosT.b], writes=[t1.b])
        t2 = K.get("tf")
        S.op(S.dve, lambda e: e.tensor_tensor(out=t2.ap[0:M, :], in0=p3.ap[0:M, :], in1=sinT.ap[0:M, :], op=ALU.mult), reads=[p3.b, sinT.b], writes=[t2.b])
        ob = K.get("tb")
        S.op(S.pool, lambda e: e.tensor_tensor(out=ob.ap[0:M, :], in0=t1.ap[0:M, :], in1=t2.ap[0:M, :], op=ALU.add), reads=[t1.b, t2.b], writes=[ob.b])
        S.dma(S.pool, dst, ob.ap[0:M, :], reads=[ob.b])
        ctx[i].clear()

    NI = len(items)
    for s_ in range(-1, NI + 4):
        if 0 <= s_ + 1 < NI:
            st_load(s_ + 1)
        if 0 <= s_ - 1 < NI:
            st_sq(s_ - 1)
        if 0 <= s_ - 2 < NI:
            st_norm(s_ - 2)
        if 0 <= s_ < NI:
            st_proj(s_)
        if 0 <= s_ - 1 < NI:
            st_ones(s_ - 1)
        if 0 <= s_ - 3 < NI:
            st_perm(s_ - 3)
        if 0 <= s_ - 4 < NI:
            st_rope(s_ - 4)


def emit_merge(S, R, F, K, C, x, W, ti):
    t0 = ti * TT
    hm = F.h
    emit_rmsnorm(S, R, x, C.mixg_a, hm, F.sq, F.rstd, KC, TT, D)
    ab = F.actb.b
    merged = F.sq
    for n, name in enumerate(("oaT", "obT", "ocT", "odT")):
        S.dma(S.sp, F.actb.ap[:, 4 * n:4 * n + 4, :], W[name].rearrange("(k p) t -> p k t", p=128)[:, :, t0:t0 + TT],
              reads=([W["o_buf"]] if "o_buf" in W else []), writes=[ab])
    for d in range(KC):
        wb = K.get("wbr")
        S.dma(S.sp, wb.ap[:], W["wbr_s"][:, d].rearrange("n p k c -> p n k c"), writes=[wb.b])
        macc = [None] * NG
        for n in range(4):
            gw = K.get("wseg")
            S.dma(S.sp, gw.ap[:], W["wina_s"][SEG_GATE + n * 8 + d], writes=[gw.b])
            for g in range(NG):
                cs = slice(g * 512, (g + 1) * 512)
                py = R.ps()
                for k in range(4):
                    S.op(S.pe, lambda e: e.matmul(py.ap[:], lhsT=wb.ap[:, n, k, :], rhs=F.actb.ap[:, 4 * n + k, cs], start=(k == 0), stop=(k == 3)),
                         reads=[wb.b, ab], writes=[py.b])
                pg = R.ps()
                for k in range(KC):
                    S.op(S.pe, lambda e: e.matmul(pg.ap[:], lhsT=gw.ap[:, k, :], rhs=hm.ap[:, k, cs], start=(k == 0), stop=(k == KC - 1)),
                         reads=[gw.b, hm.b], writes=[pg.b])
                gs = K.get("tf")
                S.op(S.act, lambda e: e.activation(out=gs.ap[:], in_=pg.ap[:], func=AF.Sigmoid), reads=[pg.b], writes=[gs.b])
                if n == 0:
                    m = K.mt[g]
                    S.op(S.dve, lambda e: e.tensor_tensor(out=m.ap[:], in0=gs.ap[:], in1=py.ap[:], op=ALU.mult), reads=[gs.b, py.b], writes=[m.b])
                else:
                    m = K.mt[g]
                    S.op(S.dve, lambda e: e.tensor_tensor(out=gs.ap[:], in0=gs.ap[:], in1=py.ap[:], op=ALU.mult), reads=[gs.b, py.b], writes=[gs.b])
                    if n < 3:
                        S.op(S.pool, lambda e: e.tensor_tensor(out=m.ap[:], in0=m.ap[:], in1=gs.ap[:], op=ALU.add), reads=[m.b, gs.b], writes=[m.b])
                    else:
                        S.op(S.pool, lambda e: e.tensor_tensor(out=merged.ap[:, d, cs], in0=m.ap[:], in1=gs.ap[:], op=ALU.add),
                             reads=[m.b, gs.b], writes=[merged.b])
    for d in range(KC):
        w = K.get("wseg")
        S.dma(S.sp, w.ap[:], W["wout_s"][d], writes=[w.b])
        for g in range(NG):
            cs = slice(g * 512, (g + 1) * 512)
            p = R.ps()
            for k in range(KC):
                S.op(S.pe, lambda e: e.matmul(p.ap[:], lhsT=w.ap[:, k, :], rhs=merged.ap[:, k, cs], start=(k == 0), stop=(k == KC - 1)),
                     reads=[w.b, merged.b], writes=[p.b])
            S.op(S.dve, lambda e: e.tensor_tensor(out=x.ap[:, d, cs], in0=p.ap[:], in1=x.ap[:, d, cs], op=ALU.add), reads=[p.b, x.b], writes=[x.b])


def _dram_in(nc, name, shape, dt=F32):
    return nc.dram_tensor(name, list(shape), dt, kind="ExternalInput").ap()


def _dram_out(nc, name, shape, dt=F32):
    return nc.dram_tensor(name, list(shape), dt, kind="ExternalOutput").ap()


from concourse.bass import ds

GROUPS = [[0, 1, 2, 3], [4, 5, 6, 7]]
WNAMES = {"ffn1_norm": [D], "ffn1_wi": [D, 2 * DFF], "ffn1_wo": [DFF, D], "mix_norm": [D], "w_in": [D, INTOT],
          "gqa_q_norm": [64], "gqa_k_norm": [64], "mla_qa_norm": [384], "mla_wq_up": [384, 768], "mla_kva_norm": [256],
          "mla_wkv_up": [256, 1024], "mla_q_norm": [96], "mla_k_norm": [96], "w_branch": [4, 512, D], "w_out": [D, D],
          "ffn2_norm": [D], "ffn2_wi": [D, 2 * DFF], "ffn2_wo": [DFF, D]}
CH = 1024
NCHK = SEQ // CH


def build_fused():
    nc = bass.Bass("TRN2", target_bir_lowering=False)
    I = {"xT": _dram_in(nc, "xT", [D, NT]), "cmat": _dram_in(nc, "cmat", [4, 128, 128])}
    for nm, shp in (("cosg", [128, NT]), ("sing", [128, NT]), ("cosm", [96, NT]), ("sinm", [96, NT]),
                    ("scw", [NL, 128, 4]), ("lruw", [NL, 128, 5]), ("lrug", [NL, 4, 128, 128]), ("lrub", [NL, 128, 6])):
        I[nm] = _dram_in(nc, nm, shp)
    for nm, shp in WNAMES.items():
        I[nm] = _dram_in(nc, nm, [NL] + shp)
    xoutT = _dram_out(nc, "xoutT", [D, NT])
    dt_ = lambda name, shape, dt=BF16: nc.dram_tensor(name, list(shape), dt).ap()
    SCR = []
    for l in range(NL):
        SCR.append({"f1wi_s": dt_("f1wi_s%d" % l, [2 * FC, 128, KC, 128]), "f1wo_s": dt_("f1wo_s%d" % l, [KC, 128, FC, 128]),
                    "f2wi_s": dt_("f2wi_s%d" % l, [2 * FC, 128, KC, 128]), "f2wo_s": dt_("f2wo_s%d" % l, [KC, 128, FC, 128]),
                    "win_s": dt_("win_s%d" % l, [64, 128, KC, 128]), "wq_s": dt_("wq_s%d" % l, [8, 128, 3, 128]),
                    "wk_s": dt_("wk_s%d" % l, [8, 128, 2, 128]), "wbr_s": dt_("wbr_s%d" % l, [4, 8, 128, 4, 128]),
                    "wout_s": dt_("wout_s%d" % l, [8, 128, KC, 128])})
    xcur = dt_("xcur", [D, NT], F32)
    qg = dt_("qg", [8, 64, NT]); qm = dt_("qm", [8, 96, NT])
    oaT = dt_("oaT", [512, NT]); ocT = dt_("ocT", [512, NT])
    kg_s = dt_("kg_s", [2, 64, NT]); kgG = dt_("kgG", [4, 128, NT])
    vg_s = dt_("vg_s", [NT, 128]); vgG = dt_("vgG", [4, NT, 128])
    km_s = dt_("km_s", [8, 96, NT]); kmG = dt_("kmG", [8, 4, 96, NT])
    vm_s = dt_("vm_s", [NT, 512]); vmG = dt_("vmG", [4, 4, 1024, 512])
    XS = {nm: dt_(nm + "_s", [512, NT]) for nm in ("bg", "cx", "lg", "xb")}
    XG = {nm: dt_(nm + "G", [4, 4, 128, NT]) for nm in ("bg", "cx", "lg", "xb")}
    ob_s = dt_("ob_s", [4, 128, NT]); od_s = dt_("od_s", [4, 128, NT])
    obG = dt_("obG", [4, 4, 128, NT]); odG = dt_("odG", [4, 4, 128, NT])
    hf_s = dt_("hf_s", [128, SEQ], F32)
    XL = {nm: dt_(nm + "L", [4, 128, NT]) for nm in ("bg", "cx", "lg", "xb")}
    obL = dt_("obL", [512, NT]); odL = dt_("odL", [512, NT])

    with ExitStack() as es:
        S = Sch(nc, es)
        psum = [T(es.enter_context(nc.psum_tensor("ps%d" % i, [128, 512], F32)), "ps%d" % i) for i in range(8)]
        pid = nc.sync.partition_id()
        jr = pid % 4

        def cast_layer(R, l, front, back):
            sc = SCR[l]
            if front:
                yield from cast_weight(S, R, I["ffn1_wi"][l], sc["f1wi_s"], [(0, FC, 0, 2, 128), (DFF, FC, 1, 2, 128)])
                yield from cast_weight(S, R, I["ffn1_wo"][l], sc["f1wo_s"], [(0, KC, 0, 1, 128)])
                yield from cast_weight(S, R, I["w_in"][l], sc["win_s"], WIN_RUNS)
                yield from cast_weight(S, R, I["mla_wq_up"][l], sc["wq_s"], [(h * 96, 1, h, 1, 96) for h in range(8)])
                yield from cast_weight(S, R, I["mla_wkv_up"][l], sc["wk_s"], [(h * 128, 1, h, 1, 64) for h in range(8)])
            if back:
                for n in range(4):
                    yield from cast_weight(S, R, I["w_branch"][l, n], sc["wbr_s"][n], [(0, 8, 0, 1, 128)])
                yield from cast_weight(S, R, I["w_out"][l], sc["wout_s"], [(0, 8, 0, 1, 128)])
                yield from cast_weight(S, R, I["ffn2_wi"][l], sc["f2wi_s"], [(0, FC, 0, 2, 128), (DFF, FC, 1, 2, 128)])
                yield from cast_weight(S, R, I["ffn2_wo"][l], sc["f2wo_s"], [(0, KC, 0, 1, 128)])

        with ExitStack() as pes:
            R = Res(nc, pes, S, TT, psum, tag="pr_")
            for _ in cast_layer(R, 0, True, False):
                pass
            S.barrier()

        def tok_phase(la, lb, tag):
            with ExitStack() as tes:
                R = Res(nc, tes, S, TT, psum, tag=tag, nstage=1)
                F = FFNRes(R)
                K = Tok(R)
                K.csb = Buf("cs")
                K.mt = [R.sb("mt%d" % i, [128, 512], F32) for i in range(NG)]
                C = Consts()
                C.b = Buf("consts")
                sb = R.sb
                x = sb("x", [128, KC, TT], F32)

                def cmat(i, name):
                    t = sb(name, [128, 128], BF16)
                    st, _ = R.cast_stage()
                    S.dma(S.sp, st.ap[:, 0:128], I["cmat"][i], writes=[st.b])
                    S.op(S.dve, lambda e: e.tensor_copy(out=t.ap[:], in_=st.ap[:, 0:128]), reads=[st.b], writes=[C.b])
                    return t
                C.bd64 = cmat(0, "bd64"); C.permg = cmat(1, "permg"); C.permm = cmat(2, "permm"); C.ident = cmat(3, "ident")

                def cols(name, src, nch):
                    t = sb(name, [128, nch], F32)
                    S.dma(S.sp, t.ap[:, 0:nch], src.rearrange("(c p) -> p c", p=128), writes=[C.b], allow_slow_non_contiguous=True)
                    return t

                def col1(name, src, n, rep):
                    t = sb(name, [128, 1], F32)
                    for r in range(rep):
                        S.dma(S.sp, t.ap[r * n:(r + 1) * n, 0:1], src.rearrange("(p o) -> p o", o=1), writes=[C.b], allow_slow_non_contiguous=True)
                    return t
                W = {"cosg": I["cosg"], "sing": I["sing"], "cosm": I["cosm"], "sinm": I["sinm"], "x1T": xcur, "qg": qg, "qm": qm,
                     "kg": kg_s, "vg": vg_s, "km": km_s, "vm": vm_s, "bgT": XS["bg"], "cxT": XS["cx"], "lgT": XS["lg"], "xbT": XS["xb"]}
                if la is not None:
                    C.mixg_a = cols("mixg_a", I["mix_norm"][la], 8)
                    C.f2g = cols("f2g", I["ffn2_norm"][la], 8)
                    olb = Buf("olb")
                    S.dma(S.sp, obL, obG[ds(jr, 1)].rearrange("o r p t -> (o r p) t"), writes=[olb])
                    S.dma(S.sp, odL, odG[ds(jr, 1)].rearrange("o r p t -> (o r p) t"), writes=[olb])
                    W.update({"oaT": oaT, "ocT": ocT, "obT": obL, "odT": odL, "o_buf": olb,
                              "wina_s": SCR[la]["win_s"], "wbr_s": SCR[la]["wbr_s"], "wout_s": SCR[la]["wout_s"]})
                if lb is not None:
                    C.f1g = cols("f1g", I["ffn1_norm"][lb], 8)
                    C.mixg = cols("mixg", I["mix_norm"][lb], 8)
                    C.qag = cols("qag", I["mla_qa_norm"][lb], 3)
                    C.kvag = cols("kvag", I["mla_kva_norm"][lb], 2)
                    C.gq = col1("gq", I["gqa_q_norm"][lb], 64, 2)
                    C.gk = col1("gk", I["gqa_k_norm"][lb], 64, 2)
                    C.mq = col1("mq", I["mla_q_norm"][lb], 96, 1)
                    C.mk = col1("mk", I["mla_k_norm"][lb], 96, 1)
                    C.wvg = sb("wvg", [128, KC, 128], BF16)
                    st, _ = R.cast_stage()
                    S.dma(S.sp, st.ap[:, 0:1024].rearrange("p (k c) -> p k c", c=128),
                          I["w_in"][lb][:, 640:768].rearrange("(k p) c -> p k c", p=128), writes=[st.b])
                    S.op(S.dve, lambda e: e.tensor_copy(out=C.wvg.ap[:].rearrange("p k c -> p (k c)"), in_=st.ap[:, 0:1024]), reads=[st.b], writes=[C.b])
                    C.wvm = sb("wvm", [128, 2, 512], BF16)
                    st, _ = R.cast_stage()
                    for k in range(2):
                        S.dma(S.sp, st.ap[:, k * 512:(k + 1) * 512].rearrange("p (h c) -> p h c", c=64),
                              I["mla_wkv_up"][lb][k * 128:(k + 1) * 128, :].rearrange("p (h c) -> p h c", c=128)[:, :, 64:128], writes=[st.b])
                    S.op(S.dve, lambda e: e.tensor_copy(out=C.wvm.ap[:].rearrange("p k c -> p (k c)"), in_=st.ap[:, 0:1024]), reads=[st.b], writes=[C.b])
                    W.update({"win_s": SCR[lb]["win_s"], "wq_s": SCR[lb]["wq_s"], "wk_s": SCR[lb]["wk_s"]})
                xsrc = I["xT"] if la is None else xcur
                xin = xsrc.rearrange("(c p) t -> p c t", p=128)
                for ti in range(NTILE):
                    t0 = ti * TT
                    S.dma(S.sp, x.ap[:], xin[:, :, t0:t0 + TT], writes=[x.b])
                    if la is not None:
                        emit_merge(S, R, F, K, C, x, W, ti)
                        emit_ffn(S, R, F, x, C.f2g, SCR[la]["f2wi_s"], SCR[la]["f2wo_s"])
                    if lb is not None:
                        emit_ffn(S, R, F, x, C.f1g, SCR[lb]["f1wi_s"], SCR[lb]["f1wo_s"])
                        emit_mixprep(S, R, F, K, C, x, W, ti)
                    else:
                        S.dma(S.pool, xoutT.rearrange("(c p) t -> p c t", p=128)[:, :, t0:t0 + TT], x.ap[:], reads=[x.b])
                S.barrier()

        def p2_phase(l, tag):
            cc0 = S.cccnt
            S.collective(kg_s.rearrange("h d t -> (h d) t"), kgG.rearrange("r p t -> (r p) t"), GROUPS)
            S.collective(vg_s, vgG.rearrange("r t c -> (r t) c"), GROUPS)
            for ck in range(4):
                S.collective(vm_s[ck * 1024:(ck + 1) * 1024, :], vmG[ck].rearrange("r t c -> (r t) c"), GROUPS)
            for h in range(8):
                S.collective(km_s[h], kmG[h].rearrange("r p t -> (r p) t"), GROUPS)
            for nm in ("cx", "bg", "xb", "lg"):
                for jj in range(4):
                    S.collective(XS[nm][jj * 128:(jj + 1) * 128, :], XG[nm][jj].rearrange("r p t -> (r p) t"), GROUPS)
            n_x_cc = S.cccnt
            with ExitStack() as tes:
                sbt = lambda name, shape, dt: T(tes.enter_context(nc.sbuf_tensor(tag + name, shape, dt)), name)
                cb = Buf("consts")
                onesf = sbt("onesf", [128, 64], F32)
                S.op(S.dve, lambda e: e.memset(onesf.ap[:], 1.0), writes=[cb])
                onec = sbt("onec", [128, 1], F32)
                S.op(S.dve, lambda e: e.memset(onec.ap[:], 1.0), writes=[cb])
                scw = sbt("scw_sb", [128, 4], F32)
                lruw = sbt("lruw_sb", [128, 5], F32)
                lrub = sbt("lrub_sb", [128, 6], F32)
                S.dma(S.sp, scw.ap[:], I["scw"][l], writes=[cb])
                S.dma(S.sp, lruw.ap[:], I["lruw"][l], writes=[cb])
                S.dma(S.sp, lrub.ap[:], I["lrub"][l], writes=[cb])
                gst = sbt("gst", [128, 128], F32)
                gmat = [sbt("gmat%d" % i, [128, 128], BF16) for i in range(4)]
                for i in range(4):
                    S.dma(S.sp, gst.ap[:], I["lrug"][l, i], writes=[gst.b])
                    S.op(S.dve, lambda e: e.tensor_copy(out=gmat[i].ap[:], in_=gst.ap[:]), reads=[gst.b], writes=[cb])
                nlam = sbt("nlam", [128, 2], F32)
                for dr in range(2):
                    S.op(S.act, lambda e: e.activation(out=nlam.ap[:, dr:dr + 1], in_=lrub.ap[:, 3 * dr + 2:3 * dr + 3], func=AF.Exp, scale=-1.0),
                         reads=[cb], writes=[nlam.b])
                S.op(S.act, lambda e: e.activation(out=nlam.ap[:], in_=nlam.ap[:], func=AF.Ln, bias=onec.ap[:, 0:1]), reads=[nlam.b, cb], writes=[nlam.b])
                S.op(S.dve, lambda e: e.tensor_scalar(out=nlam.ap[:], in0=nlam.ap[:], scalar1=-8.0, scalar2=None, op0=ALU.mult), reads=[nlam.b], writes=[nlam.b])
                ft = {nm: sbt("l_" + nm, [128, CH + 6], F32) for nm in ("xc", "r", "i", "a", "m", "bv", "h0", "h1", "hf")}
                bt = {nm: sbt("l_" + nm, [128, CH + 6], BF16) for nm in ("xw", "bg", "lg")}
                xcb = sbt("l_xcb", [128, CH], BF16)
                outb = sbt("l_outb", [128, CH], BF16)

                def load_seq(dst, off, G, lo, hi):
                    Gj = G
                    t = lo
                    while t < hi:
                        r = t // NT
                        e_ = min(hi, (r + 1) * NT)
                        a_, b_ = t, e_
                        if b_ - a_ == 1:
                            if a_ - 1 >= r * NT:
                                a_ -= 1
                            else:
                                b_ += 1
                        S.dma(S.sp, dst.ap[:, off + (a_ - lo):off + (b_ - lo)], Gj[r, :, a_ - r * NT:b_ - r * NT], reads=[xlb], writes=[dst.b])
                        t = e_

                xlb = Buf("xlb")

                def conv_lru():
                    S._wait(S.sp, ("cc", S.ccsem, n_x_cc))
                    for nm in ("cx", "bg", "xb", "lg"):
                        S.dma(S.sp, XL[nm].rearrange("r p t -> (r p) t"), XG[nm][ds(jr, 1)].rearrange("o r p t -> (o r p) t"), writes=[xlb])
                    yield
                    for ci in range(NCHK):
                        t0 = ci * CH
                        cw, bgt, acc = bt["xw"], bt["bg"], ft["xc"]
                        lo, hi = max(t0 - 1, 0), min(t0 + CH + 1, SEQ)
                        if ci == 0:
                            S.op(S.pool, lambda e: e.memset(cw.ap[:, 1:2], 0.0), writes=[cw.b])
                        if ci == NCHK - 1:
                            S.op(S.pool, lambda e: e.memset(cw.ap[:, CH + 2:CH + 3], 0.0), writes=[cw.b])
                        load_seq(cw, 1 + lo - (t0 - 1), XL["cx"], lo, hi)
                        load_seq(bgt, 0, XL["bg"], t0, t0 + CH)
                        S.op(S.dve, lambda e: e.tensor_scalar(out=acc.ap[:, 0:CH], in0=cw.ap[:, 1:1 + CH], scalar1=scw.ap[:, 0:1], scalar2=scw.ap[:, 3:4],
                                                              op0=ALU.mult, op1=ALU.add), reads=[cw.b, cb], writes=[acc.b])
                        for k in (1, 2):
                            S.op(S.dve, lambda e: e.scalar_tensor_tensor(out=acc.ap[:, 0:CH], in0=cw.ap[:, 1 + k:1 + k + CH], scalar=scw.ap[:, k:k + 1], in1=acc.ap[:, 0:CH],
                                                                         op0=ALU.mult, op1=ALU.add), reads=[cw.b, cb, acc.b], writes=[acc.b])
                        S.op(S.pool, lambda e: e.tensor_tensor(out=outb.ap[:], in0=acc.ap[:, 0:CH], in1=bgt.ap[:, 0:CH], op=ALU.mult),
                             reads=[acc.b, bgt.b], writes=[outb.b])
                        q = t0 // NT
                        S.dma(S.pool, ob_s[q, :, t0 - q * NT:t0 - q * NT + CH], outb.ap[:], reads=[outb.b])
                        yield
                    hfb = Buf("hf_s")

                    def lru_chunk(ci, dr, hprev, hcur):
                        t0 = ci * CH
                        xw, xc = bt["xw"], ft["xc"]
                        lo, hi = max(t0 - 2, 0), min(t0 + CH + 1, SEQ)
                        if ci == 0:
                            S.op(S.pool, lambda e: e.memset(xw.ap[:, 1:3], 0.0), writes=[xw.b])
                        if ci == NCHK - 1:
                            S.op(S.pool, lambda e: e.memset(xw.ap[:, CH + 3:CH + 4], 0.0), writes=[xw.b])
                        load_seq(xw, 1 + lo - (t0 - 2), XL["xb"], lo, hi)
                        S.op(S.dve, lambda e: e.tensor_scalar(out=xc.ap[:, 0:CH], in0=xw.ap[:, 1:1 + CH], scalar1=lruw.ap[:, 0:1], scalar2=lruw.ap[:, 4:5],
                                                              op0=ALU.mult, op1=ALU.add), reads=[xw.b, cb], writes=[xc.b])
                        for k in (1, 2, 3):
                            S.op(S.dve, lambda e: e.scalar_tensor_tensor(out=xc.ap[:, 0:CH], in0=xw.ap[:, 1 + k:1 + k + CH], scalar=lruw.ap[:, k:k + 1], in1=xc.ap[:, 0:CH],
                                                                         op0=ALU.mult, op1=ALU.add), reads=[xw.b, cb, xc.b], writes=[xc.b])
                        S.op(S.pool, lambda e: e.tensor_copy(out=xcb.ap[:], in_=xc.ap[:, 0:CH]), reads=[xc.b], writes=[xcb.b])
                        yield
                        r, i_, a, m, bv = ft["r"], ft["i"], ft["a"], ft["m"], ft["bv"]
                        for n in range(CH // 512):
                            cs = slice(n * 512, (n + 1) * 512)
                            pr, pi = psum[6], psum[7]
                            S.op(S.pe, lambda e: e.matmul(pr.ap[:], lhsT=gmat[2 * dr].ap[:], rhs=xcb.ap[:, cs], start=True, stop=True), reads=[xcb.b, cb], writes=[pr.b])
                            S.op(S.pe, lambda e: e.matmul(pi.ap[:], lhsT=gmat[2 * dr + 1].ap[:], rhs=xcb.ap[:, cs], start=True, stop=True), reads=[xcb.b, cb], writes=[pi.b])
                            S.op(S.act, lambda e: e.activation(out=r.ap[:, cs], in_=pr.ap[:], func=AF.Sigmoid, bias=lrub.ap[:, 3 * dr:3 * dr + 1]),
                                 reads=[pr.b, cb], writes=[r.b])
                            S.op(S.act, lambda e: e.activation(out=i_.ap[:, cs], in_=pi.ap[:], func=AF.Sigmoid, bias=lrub.ap[:, 3 * dr + 1:3 * dr + 2]),
                                 reads=[pi.b, cb], writes=[i_.b])
                        S.op(S.act, lambda e: e.activation(out=a.ap[:, 0:CH], in_=r.ap[:, 0:CH], func=AF.Exp, scale=nlam.ap[:, dr:dr + 1]),
                             reads=[r.b, nlam.b], writes=[a.b])
                        S.op(S.pool, lambda e: e.tensor_tensor(out=m.ap[:, 0:CH], in0=a.ap[:, 0:CH], in1=a.ap[:, 0:CH], op=ALU.mult), reads=[a.b], writes=[m.b])
                        S.op(S.act, lambda e: e.activation(out=m.ap[:, 0:CH], in_=m.ap[:, 0:CH], func=AF.Sqrt, scale=-1.0, bias=onec.ap[:, 0:1]),
                             reads=[m.b, cb], writes=[m.b])
                        S.op(S.pool, lambda e: e.tensor_tensor(out=bv.ap[:, 0:CH], in0=i_.ap[:, 0:CH], in1=xc.ap[:, 0:CH], op=ALU.mult), reads=[i_.b, xc.b], writes=[bv.b])
                        S.op(S.dve, lambda e: e.tensor_tensor(out=bv.ap[:, 0:CH], in0=bv.ap[:, 0:CH], in1=m.ap[:, 0:CH], op=ALU.mult), reads=[bv.b, m.b], writes=[bv.b])
                        if dr == 0:
                            init = 0.0 if hprev is None else hprev.ap[:, CH - 1:CH]
                            S.op(S.dve, lambda e: e.tensor_tensor_scan(out=hcur.ap[:, 0:CH], data0=a.ap[:, 0:CH], data1=bv.ap[:, 0:CH], initial=init,
                                                                       op0=ALU.mult, op1=ALU.add),
                                 reads=[a.b, bv.b] + ([hprev.b] if hprev is not None else []), writes=[hcur.b])
                        else:
                            init = 0.0 if hprev is None else hprev.ap[:, 0:1]
                            S.op(S.dve, lambda e: e.tensor_tensor_scan(out=hcur.ap[:, 0:CH][:, ::-1], data0=a.ap[:, 0:CH][:, ::-1], data1=bv.ap[:, 0:CH][:, ::-1],
                                                                       initial=init, op0=ALU.mult, op1=ALU.add),
                                 reads=[a.b, bv.b] + ([hprev.b] if hprev is not None else []), writes=[hcur.b])

                    hprev = None
                    for ci in range(NCHK):
                        hcur = ft["h%d" % (ci % 2)]
                        yield from lru_chunk(ci, 0, hprev, hcur)
                        S.dma(S.pool, hf_s[:, ci * CH:(ci + 1) * CH], hcur.ap[:, 0:CH], reads=[hcur.b], writes=[hfb])
                        hprev = hcur
                        yield
                    hprev = None
                    for ci in range(NCHK - 1, -1, -1):
                        t0 = ci * CH
                        hcur = ft["h%d" % (ci % 2)]
                        yield from lru_chunk(ci, 1, hprev, hcur)
                        hf, lg = ft["hf"], bt["lg"]
                        S.dma(S.sp, hf.ap[:, 0:CH], hf_s[:, t0:t0 + CH], reads=[hfb], writes=[hf.b])
                        load_seq(lg, 0, XL["lg"], t0, t0 + CH)
                        S.op(S.pool, lambda e: e.tensor_tensor(out=hf.ap[:, 0:CH], in0=hf.ap[:, 0:CH], in1=hcur.ap[:, 0:CH], op=ALU.add), reads=[hf.b, hcur.b], writes=[hf.b])
                        S.op(S.pool, lambda e: e.tensor_tensor(out=outb.ap[:], in0=hf.ap[:, 0:CH], in1=lg.ap[:, 0:CH], op=ALU.mult), reads=[hf.b, lg.b], writes=[outb.b])
                        q = t0 // NT
                        S.dma(S.pool, od_s[q, :, t0 - q * NT:t0 - q * NT + CH], outb.ap[:], reads=[outb.b])
                        hprev = hcur
                        yield
                    S.dma_barrier(S.pool)
                    for q in range(4):
                        S.collective(ob_s[q], obG[q].rearrange("r p t -> (r p) t"), GROUPS)
                        S.collective(od_s[q], odG[q].rearrange("r p t -> (r p) t"), GROUPS)

                Kt = [sbt("Kt%d" % i, [128, SEQ], BF16) for i in range(2)]
                Vt = [sbt("Vt%d" % i, [128, SEQ // 128, 65], BF16) for i in range(2)]
                Qt = [sbt("Qt%d" % i, [128, NT], BF16) for i in range(2)]
                for kq in Kt + Qt:
                    S.op(S.pool, lambda e: e.memset(kq.ap[64:128, :], 0.0), writes=(kq.parts if hasattr(kq, "parts") else [kq.b]))
                for kq in Kt:
                    kq.parts = [Buf() for _ in range(4)]
                for v in Vt:
                    v.parts = [Buf() for _ in range(16)]
                    S.op(S.pool, lambda e: e.memset(v.ap[:, :, 64:65], 1.0), writes=v.parts)
                pt = [sbt("pt%d" % i, [128, 512], BF16) for i in range(4)]
                osb = [sbt("osb%d" % i, [128, 264], F32) for i in range(2)]
                identf = sbt("identf", [128, 128], F32)
                S.dma(S.sp, identf.ap[:], I["cmat"][3], writes=[cb])
                obf = [sbt("obf%d" % i, [64, 512], BF16) for i in range(2)]
                sbank = psum[0:4]
                obank = psum[4:6]
                bbank = psum[6]
                cnt = {"q": 0, "o": 0, "kv": 0}
                lru_gen = conv_lru()

                class CastRes:
                    pass
                CR = CastRes()
                cst = [(sbt("cst%d" % i, [128, 1024], F32), sbt("cstb%d" % i, [128, 1024], BF16)) for i in range(2)]
                cn = {"s": 0, "e": 0}

                def _stage():
                    cn["s"] += 1
                    return cst[cn["s"] % 2]

                def _eng():
                    cn["e"] += 1
                    return (S.pool, S.dve)[cn["e"] % 2]
                CR.cast_stage = _stage
                CR.next_cast_eng = _eng
                def _cast_all():
                    yield from cast_layer(CR, l, False, True)
                    if l + 1 < NL:
                        yield from cast_layer(CR, l + 1, True, False)
                cast_gen = _cast_all()

                pending = []

                def flush_fin():
                    while pending:
                        pending.pop(0)()

                def attn_head(Kb, Vb, Qb, d, out_ap, dk):
                    scale = float(d) ** -0.5
                    for qg_ in range(NT // 512):
                        qcs = slice(qg_ * 512, (qg_ + 1) * 512)
                        po = obank[cnt["o"] % 2]
                        ob_ = obf[cnt["o"] % 2]
                        os_ = osb[cnt["o"] % 2]
                        cnt["o"] += 1

                        def s_mm(kb):
                            ps = sbank[kb % 4]
                            S.op(S.pe, lambda e: e.matmul(ps.ap[:], lhsT=Kb.ap[0:dk, kb * 128:(kb + 1) * 128], rhs=Qb.ap[0:dk, qcs], start=True, stop=True),
                                 reads=[Kb.parts[kb // 32], Qb.b], writes=[ps.b])
                        nkb = SEQ // 128
                        s_mm(0); s_mm(1); s_mm(2)
                        for kb in range(nkb):
                            ps = sbank[kb % 4]
                            p_ = pt[kb % 4]
                            S.op(S.act, lambda e: e.activation(out=p_.ap[:], in_=ps.ap[:], func=AF.Exp, scale=scale), reads=[ps.b], writes=[p_.b])
                            if kb + 3 < nkb:
                                s_mm(kb + 3)
                            for j in range(4):
                                S.op(S.pe, lambda e: e.matmul(po.ap[:, j * 65:(j + 1) * 65], lhsT=p_.ap[:, j * 128:(j + 1) * 128], rhs=Vb.ap[:, kb, 0:65],
                                                              start=(kb == 0 and j == 0), stop=(kb == nkb - 1), skip_group_check=True),
                                     reads=[Vb.parts[kb // 8], p_.b], writes=[po.b])
                            if kb == 8:
                                flush_fin()
                        pov = po.ap[:, 0:260].rearrange("p (j c) -> p j c", c=65)
                        S.op(S.dve, lambda e: e.reciprocal(out=os_.ap[:, 256:260], in_=pov[:, :, 64]), reads=[po.b], writes=[os_.b])
                        for j in range(4):
                            S.op(S.dve, lambda e: e.tensor_scalar(out=os_.ap[:, j * 64:(j + 1) * 64], in0=po.ap[:, j * 65:j * 65 + 64],
                                                                  scalar1=os_.ap[:, 256 + j:257 + j], scalar2=None, op0=ALU.mult),
                                 reads=[po.b, os_.b], writes=[os_.b])

                        def fin(os_=os_, ob_=ob_, out_ap=out_ap, qcs=qcs):
                            for j in range(4):
                                S.op(S.pe, lambda e: e.transpose(out=bbank.ap[0:64, j * 128:(j + 1) * 128], in_=os_.ap[:, j * 64:(j + 1) * 64], identity=identf.ap[:]),
                                     reads=[os_.b, cb], writes=[bbank.b])
                            S.op(S.dve, lambda e: e.tensor_copy(out=ob_.ap[:], in_=bbank.ap[0:64, :]), reads=[bbank.b], writes=[ob_.b])
                            S.dma(S.pool, out_ap[:, qcs], ob_.ap[:], reads=[ob_.b])
                        pending.append(fin)
                        v_some(3)
                        if cnt["o"] > 32:
                            next(lru_gen, None)
                            next(lru_gen, None)
                        for _ in range(3):
                            next(cast_gen, None)

                jobs = []
                for kvh in range(2):
                    for g in range(4):
                        h = kvh * 4 + g
                        jobs.append(("g", kvh, h, g == 0))
                for h in range(8):
                    jobs.append(("m", h, h, True))
                jstate = {}
                vloads = []

                def v_some(n):
                    for _ in range(n):
                        if vloads:
                            vloads.pop(0)()

                def prefetch(i):
                    kind, kvi, h, newkv = jobs[i]
                    if newkv:
                        Kb = Kt[cnt["kv"] % 2]
                        Vb = Vt[cnt["kv"] % 2]
                        cnt["kv"] += 1
                        if kind == "g":
                            S._wait(S.sp, ("cc", S.ccsem, cc0 + 2))
                            for r in range(4):
                                S.dma(S.sp, Kb.ap[0:64, r * NT:(r + 1) * NT], kgG[r, kvi * 64:(kvi + 1) * 64, :], writes=[Kb.parts[r]])
                                for hb in range(2):
                                    vloads.append(lambda r=r, hb=hb, Vb=Vb, kvi=kvi: S.dma(
                                        S.sp, Vb.ap[:, r * 32 + hb * 16:r * 32 + hb * 16 + 16, 0:64],
                                        vgG[r, hb * 2048:(hb + 1) * 2048, kvi * 64:(kvi + 1) * 64].rearrange("(b p) c -> p b c", p=128),
                                        writes=[Vb.parts[r * 4 + hb * 2], Vb.parts[r * 4 + hb * 2 + 1]]))
                        else:
                            S._wait(S.sp, ("cc", S.ccsem, cc0 + 7 + h))
                            for r in range(4):
                                S.dma(S.sp, Kb.ap[0:96, r * NT:(r + 1) * NT], kmG[h, r], writes=[Kb.parts[r]])
                                for ck in range(4):
                                    vloads.append(lambda r=r, ck=ck, Vb=Vb, h=h: S.dma(
                                        S.sp, Vb.ap[:, r * 32 + ck * 8:r * 32 + ck * 8 + 8, 0:64],
                                        vmG[ck, r, :, h * 64:(h + 1) * 64].rearrange("(b p) c -> p b c", p=128), writes=[Vb.parts[r * 4 + ck]]))
                        jstate["kv"] = (Kb, Vb)
                    Qb = Qt[cnt["q"] % 2]
                    cnt["q"] += 1
                    d = 64 if kind == "g" else 96
                    S.dma(S.sp, Qb.ap[0:d, :], (qg if kind == "g" else qm)[h], writes=[Qb.b])
                    return jstate["kv"] + (Qb,)

                nxt = prefetch(0)
                v_some(99)
                for i, (kind, kvi, h, newkv) in enumerate(jobs):
                    Kb, Vb, Qb = nxt
                    v_some(99)
                    if i + 1 < len(jobs):
                        nxt = prefetch(i + 1)
                    if kind == "g":
                        attn_head(Kb, Vb, Qb, 64, oaT[h * 64:(h + 1) * 64, :], 128)
                    else:
                        attn_head(Kb, Vb, Qb, 96, ocT[h * 64:(h + 1) * 64, :], 96)
                flush_fin()
                for _ in lru_gen:
                    pass
                for _ in cast_gen:
                    pass
                S.barrier()

        tok_phase(None, 0, "t0_")
        for l in range(NL):
            p2_phase(l, "p%d_" % l)
            tok_phase(l, l + 1 if l + 1 < NL else None, "t%d_" % (l + 1))
        S.finish()
        print("build_fused ninst", S.ninst, flush=True)
    return nc


def _rope_perm(M, blocks):
    Rm = np.zeros((128, 128), np.float32)
    for (o, n) in blocks:
        hn = n // 2
        for i in range(hn):
            Rm[o + hn + i, o + i] = -1.0
            Rm[o + i, o + hn + i] = 1.0
    return Rm


def host_consts():
    bd64 = np.zeros((128, 128), np.float32)
    bd64[:64, :64] = 1.0
    bd64[64:, 64:] = 1.0
    permg = _rope_perm(128, [(0, 32), (32, 32), (64, 32), (96, 32)])
    permm = _rope_perm(96, [(64, 16), (80, 16)])
    ident = np.eye(128, dtype=np.float32)
    return np.stack([bd64, permg, permm, ident])


def host_rope_tables(tok0):
    t = np.arange(tok0, tok0 + NT)
    row = (t // 64).astype(np.float32)
    col = (t % 64).astype(np.float32)

    def ang(pos, dim):
        inv = (10000.0 ** (-np.arange(0, dim, 2, dtype=np.float32) / dim)).astype(np.float32)
        return (pos[:, None] * inv[None, :]).astype(np.float32)

    def tables(M, blocks):
        cos = np.ones((M, NT), np.float32)
        sin = np.zeros((M, NT), np.float32)
        for (o, n, pos) in blocks:
            a = ang(pos, n)
            c, s = np.cos(a).T, np.sin(a).T
            hn = n // 2
            cos[o:o + hn] = c
            cos[o + hn:o + n] = c
            sin[o:o + hn] = s
            sin[o + hn:o + n] = s
        return cos, sin
    cosg, sing = tables(128, [(0, 32, row), (32, 32, col), (64, 32, row), (96, 32, col)])
    cosm, sinm = tables(96, [(64, 16, row), (80, 16, col)])
    return cosg, sing, cosm, sinm


NCORES = 8
_PROG = []


def _c(a):
    return np.ascontiguousarray(a)


def kernel(**inp):
    inp = {k: np.asarray(v) for k, v in inp.items()}
    x = inp["x"]
    if not _PROG:
        _PROG.append(build_fused())
    cm = host_consts()
    maps = []
    for c in range(NCORES):
        b, j = c // 4, c % 4
        ch = slice(128 * j, 128 * (j + 1))
        m = {"xT": _c(x[b, j * NT:(j + 1) * NT].T), "cmat": cm}
        m["cosg"], m["sing"], m["cosm"], m["sinm"] = host_rope_tables(j * NT)
        for nm in WNAMES:
            m[nm] = _c(inp[nm])
        m["scw"] = _c(np.concatenate([inp["sc_conv_w"][:, :, ch].transpose(0, 2, 1), inp["sc_conv_b"][:, ch][:, :, None]], axis=2))
        m["lruw"] = _c(np.concatenate([inp["lru_conv_w"][:, :, ch].transpose(0, 2, 1), inp["lru_conv_b"][:, ch][:, :, None]], axis=2))
        gm = np.zeros((NL, 4, 128, 128), np.float32)
        for dr in range(2):
            for k, nm in enumerate(("lru_wa", "lru_wx")):
                for blk in range(2):
                    gm[:, 2 * dr + k, 64 * blk:64 * (blk + 1), 64 * blk:64 * (blk + 1)] = inp[nm][:, dr, 2 * j + blk]
        m["lrug"] = gm
        m["lrub"] = _c(np.stack([inp["lru_ba"][:, 0, ch], inp["lru_bx"][:, 0, ch], inp["lru_lambda"][:, 0, ch],
                                 inp["lru_ba"][:, 1, ch], inp["lru_bx"][:, 1, ch], inp["lru_lambda"][:, 1, ch]], axis=2))
        maps.append(m)
    res = run_bass_kernel_spmd(_PROG[0], maps, core_ids=list(range(NCORES))).results
    out = np.empty((B_, SEQ, D), np.float32)
    for c in range(NCORES):
        b, j = c // 4, c % 4
        out[b, j * NT:(j + 1) * NT] = res[c]["xoutT"].T
    return out
```

```python
import numpy as np
from contextlib import ExitStack
import concourse.bass as bass
import concourse.mybir as mybir
from concourse.bass_utils import run_bass_kernel_spmd

F32 = mybir.dt.float32
BF16 = mybir.dt.bfloat16
ALU = mybir.AluOpType
AF = mybir.ActivationFunctionType


class Buf:
    __slots__ = ("name", "w", "r")

    def __init__(self, name=""):
        self.name = name
        self.w = None
        self.r = {}


class Eng:
    def __init__(self, name, h, sem, same_sync=True):
        self.name = name
        self.h = h
        self.sem = sem
        self.cnt = 0
        self.seen = {}
        self.same_sync = same_sync


class Sch:
    def __init__(self, nc, es, n_dma_sems=28):
        self.nc = nc
        self.es = es
        mk = lambda n, h, ss=True: Eng(n, h, es.enter_context(nc.semaphore("p_" + n)), ss)
        self.pe = mk("pe", nc.tensor, False)
        self.act = mk("act", nc.scalar)
        self.dve = mk("dve", nc.vector)
        self.pool = mk("pool", nc.gpsimd)
        self.sp = mk("sp", nc.sync)
        self.dpool = {}
        for qn in ("sp", "pool", "act"):
            n = n_dma_sems if qn != "act" else 2
            self.dpool[qn] = {"sems": [es.enter_context(nc.semaphore("d%s%d" % (qn, i))) for i in range(n)], "vals": [0] * n, "next": 0}
        self.ninst = 0
        self.ccsem = None
        self.cccnt = 0

    def _wait(self, eng, ev):
        if ev is None:
            return
        key, sem, val = ev
        if key == eng.name and not eng.same_sync:
            return
        if eng.seen.get(key, 0) >= val:
            return
        eng.h.wait_ge(sem, val)
        eng.seen[key] = val

    def _deps(self, eng, reads, writes):
        for b in reads:
            self._wait(eng, b.w)
        for b in writes:
            self._wait(eng, b.w)
            for ev in b.r.values():
                self._wait(eng, ev)

    def _record(self, ev, reads, writes):
        for b in reads:
            b.r[ev[0]] = ev
        for b in writes:
            b.w = ev
            b.r = {}

    def op(self, eng, fn, reads=(), writes=()):
        self._deps(eng, reads, writes)
        ins = fn(eng.h)
        eng.cnt += 1
        ins.then_inc(eng.sem, 1)
        ev = (eng.name, eng.sem, eng.cnt)
        self._record(ev, reads, writes)
        self.ninst += 1
        return ev

    def dma(self, q, out, in_, reads=(), writes=(), **kw):
        dp = self.dpool[q.name]
        i = dp["next"]
        dp["next"] = (i + 1) % len(dp["sems"])
        sem = dp["sems"][i]
        key = "d%s%d" % (q.name, i)
        if dp["vals"][i] > 0:
            self._wait(q, (key, sem, dp["vals"][i]))
        self._deps(q, reads, writes)
        q.h.dma_start(out=out, in_=in_, **kw).then_inc(sem, 16)
        dp["vals"][i] += 16
        ev = (key, sem, dp["vals"][i])
        self._record(ev, reads, writes)
        self.ninst += 1
        return ev

    def dma_barrier(self, q):
        for qn, dp in self.dpool.items():
            for i, sem in enumerate(dp["sems"]):
                if dp["vals"][i] > 0:
                    self._wait(q, ("d%s%d" % (qn, i), sem, dp["vals"][i]))

    def collective(self, src, dst, groups):
        q = self.pool
        if self.ccsem is None:
            self.ccsem = self.es.enter_context(self.nc.semaphore("ccsem"))
            self.cccnt = 0
        self.nc.gpsimd.collective_compute("AllGather", ALU.bypass, replica_groups=groups, ins=[src], outs=[dst]).then_inc(self.ccsem)
        self.cccnt += 1
        self.ninst += 1

    def barrier(self):
        engs = (self.pe, self.act, self.dve, self.pool, self.sp)
        for e in engs:
            for o in engs:
                if o is not e and o.cnt > 0:
                    self._wait(e, (o.name, o.sem, o.cnt))
            self.dma_barrier(e)
            if self.ccsem is not None and self.cccnt > 0:
                self._wait(e, ("cc", self.ccsem, self.cccnt))

    def cc_wait(self, eng):
        if self.ccsem is not None and self.cccnt > 0:
            self._wait(eng, ("cc", self.ccsem, self.cccnt))

    def finish(self):
        for e in (self.pe, self.act, self.dve, self.pool):
            if e.cnt > 0:
                self._wait(self.sp, (e.name, e.sem, e.cnt))
        self.dma_barrier(self.sp)
        self.cc_wait(self.sp)
D = 1024
DFF = 2816
KC = 8
FC = 22
EPS = 1e-6


def copy_on(S, eng, out, in_, reads, writes):
    if eng is S.act:
        return S.op(eng, lambda e: e.activation(out=out, in_=in_, func=AF.Copy), reads=reads, writes=writes)
    return S.op(eng, lambda e: e.tensor_copy(out=out, in_=in_), reads=reads, writes=writes)


def cast_weight(S, R, src, dst, runs):
    K, N = src.shape
    kcn = K // 128
    for kc in range(kcn):
        for (c0, nrun, s0, sst, width) in runs:
            i = 0
            while i < nrun:
                ns = min(8, nrun - i)
                w = ns * 128 if width == 128 else width
                cc = c0 + i * 128
                st, stb = R.cast_stage()
                S.dma(S.sp, st.ap[:, 0:w], src[kc * 128:(kc + 1) * 128, cc:cc + w], writes=[st.b])
                eng = R.next_cast_eng()
                copy_on(S, eng, stb.ap[:, 0:w], st.ap[:, 0:w], [st.b], [stb.b])
                sa = s0 + i * sst
                if width == 128:
                    o = dst[sa:sa + (ns - 1) * sst + 1:sst, :, kc, :].rearrange("s p c -> p s c")
                    i_ = stb.ap[:, 0:w].rearrange("p (s c) -> p s c", c=128)
                else:
                    o = dst[sa, :, kc, 0:w]
                    i_ = stb.ap[:, 0:w]
                S.dma(S.pool, o, i_, reads=[stb.b])
                i += ns
                yield


class T:
    def __init__(self, ap, name=""):
        self.ap = ap
        self.b = Buf(name)


class Res:
    def __init__(self, nc, es, S, TT, psum=None, tag="", nstage=2):
        self.nc, self.S, self.TT = nc, S, TT
        sb = lambda name, shape, dt: T(es.enter_context(nc.sbuf_tensor(tag + name, shape, dt)), name)
        self.sb = sb
        if psum is None:
            psum = [T(es.enter_context(nc.psum_tensor("ps%d" % i, [128, 512], F32)), "ps%d" % i) for i in range(8)]
        self.psum = psum
        self.pnext = 0
        self.cstage = [(sb("cst%d" % i, [128, 1024], F32), sb("cstb%d" % i, [128, 1024], BF16)) for i in range(nstage)]
        self.cnext = 0
        self.ceng = 0
        self.wscr_buf = Buf("wscr")
        self.ones = sb("ones", [128, 128], BF16)
        S.op(S.dve, lambda e: e.memset(self.ones.ap[:], 1.0), writes=[self.ones.b])
        self.eps = sb("epsc", [128, 1], F32)
        S.op(S.dve, lambda e: e.memset(self.eps.ap[:], EPS), writes=[self.eps.b])

    def ps(self):
        t = self.psum[self.pnext]
        self.pnext = (self.pnext + 1) % 8
        return t

    def cast_stage(self):
        t = self.cstage[self.cnext]
        self.cnext = (self.cnext + 1) % len(self.cstage)
        return t

    def next_cast_eng(self):
        S = self.S
        e = [S.pool, S.dve, S.act][self.ceng % 3]
        self.ceng += 1
        return e


class FFNRes:
    def __init__(self, R):
        TT = R.TT
        sb = R.sb
        self.sq = sb("f_sq", [128, KC, TT], BF16)
        self.rstd = sb("f_rstd", [128, TT], F32)
        self.h = sb("f_h", [128, KC, TT], BF16)
        self.actb = sb("f_act", [128, FC, TT], BF16)
        self.wi = [sb("f_wi%d" % i, [128, 2, KC, 128], BF16) for i in range(3)]
        self.wo = [sb("f_wo%d" % i, [128, FC, 128], BF16) for i in range(2)]
        self.sil = [sb("f_sil%d" % i, [128, 512], F32) for i in range(2)]
        self.wi_n = 0
        self.wo_n = 0
        self.sil_n = 0


def emit_rmsnorm(S, R, x, gcol, h, sq, rstd, nch, TT, dim):
    for c in range(nch):
        S.op(S.pool, lambda e: e.tensor_tensor(out=sq.ap[:, c, :], in0=x.ap[:, c, :], in1=x.ap[:, c, :], op=ALU.mult),
             reads=[x.b], writes=[sq.b])
    for n in range(TT // 512):
        cs = slice(n * 512, (n + 1) * 512)
        p = R.ps()
        for c in range(nch):
            S.op(S.pe, lambda e: e.matmul(p.ap[:], lhsT=R.ones.ap[:], rhs=sq.ap[:, c, cs], start=(c == 0), stop=(c == nch - 1)),
                 reads=[R.ones.b, sq.b], writes=[p.b])
        S.op(S.act, lambda e: e.activation(out=rstd.ap[:, cs], in_=p.ap[:], func=AF.Sqrt, scale=1.0 / dim, bias=R.eps.ap[:, 0:1]),
             reads=[p.b, R.eps.b], writes=[rstd.b])
    S.op(S.dve, lambda e: e.reciprocal(out=rstd.ap[:], in_=rstd.ap[:]), reads=[rstd.b], writes=[rstd.b])
    for c in range(nch):
        S.op(S.dve, lambda e: e.scalar_tensor_tensor(out=h.ap[:, c, :], in0=x.ap[:, c, :], scalar=gcol.ap[:, c:c + 1], in1=rstd.ap[:],
                                                   op0=ALU.mult, op1=ALU.mult),
             reads=[x.b, gcol.b, rstd.b], writes=[h.b])


def emit_ffn(S, R, F, x, gcol, wi_s, wo_s):
    TT = R.TT
    NG = TT // 512
    emit_rmsnorm(S, R, x, gcol, F.h, F.sq, F.rstd, KC, TT, D)
    for f in range(FC):
        w = F.wi[F.wi_n % 3]; F.wi_n += 1
        S.dma(S.sp, w.ap[:], wi_s[2 * f:2 * f + 2].rearrange("t p k c -> p t k c"), writes=[w.b])
        for n in range(NG):
            cs = slice(n * 512, (n + 1) * 512)
            pg = R.ps(); pu = R.ps()
            for k in range(KC):
                S.op(S.pe, lambda e: e.matmul(pg.ap[:], lhsT=w.ap[:, 0, k, :], rhs=F.h.ap[:, k, cs], start=(k == 0), stop=(k == KC - 1)),
                     reads=[w.b, F.h.b], writes=[pg.b])
            for k in range(KC):
                S.op(S.pe, lambda e: e.matmul(pu.ap[:], lhsT=w.ap[:, 1, k, :], rhs=F.h.ap[:, k, cs], start=(k == 0), stop=(k == KC - 1)),
                     reads=[w.b, F.h.b], writes=[pu.b])
            sl = F.sil[F.sil_n % 2]; F.sil_n += 1
            S.op(S.act, lambda e: e.activation(out=sl.ap[:], in_=pg.ap[:], func=AF.Silu), reads=[pg.b], writes=[sl.b])
            S.op(S.dve, lambda e: e.tensor_tensor(out=F.actb.ap[:, f, cs], in0=sl.ap[:], in1=pu.ap[:], op=ALU.mult),
                 reads=[sl.b, pu.b], writes=[F.actb.b])
    for d in range(KC):
        w = F.wo[F.wo_n % 2]; F.wo_n += 1
        S.dma(S.sp, w.ap[:], wo_s[d], writes=[w.b])
        for n in range(NG):
            cs = slice(n * 512, (n + 1) * 512)
            p = R.ps()
            for fc in range(FC):
                S.op(S.pe, lambda e: e.matmul(p.ap[:], lhsT=w.ap[:, fc, :], rhs=F.actb.ap[:, fc, cs], start=(fc == 0), stop=(fc == FC - 1)),
                     reads=[w.b, F.actb.b], writes=[p.b])
            S.op(S.dve, lambda e: e.scalar_tensor_tensor(out=x.ap[:, d, cs], in0=p.ap[:], scalar=0.5, in1=x.ap[:, d, cs], op0=ALU.mult, op1=ALU.add),
                 reads=[p.b, x.b], writes=[x.b])


B_, SEQ, NL = 2, 16384, 4
NT = 4096
TT = 1024
NTILE = NT // TT
NG = TT // 512
INTOT = 8096
WIN_RUNS = [(0, 23, 0, 1, 128), (2944, 1, 23, 1, 32), (2976, 8, 24, 1, 128), (4000, 32, 32, 1, 128)]
SEG_Q, SEG_K, SEG_BG, SEG_CG, SEG_XS, SEG_QL, SEG_KVL, SEG_KR, SEG_LG, SEG_XB, SEG_GATE = 0, 4, 6, 10, 14, 18, 21, 23, 24, 28, 32


class Tok:
    def __init__(self, R):
        sb = R.sb
        self.tf = [sb("tf%d" % i, [128, 512], F32) for i in range(6)]
        self.tb = [sb("tb%d" % i, [128, 512], BF16) for i in range(8)]
        self.wseg = [sb("wseg%d" % i, [128, 8, 128], BF16) for i in range(4)]
        self.wbr = [sb("wbr%d" % i, [128, 4, 4, 128], BF16) for i in range(2)]
        self.cs2 = [[sb("cs%d_%d" % (j, i), [128, 512], F32) for i in range(4)] for j in range(2)]
        self.n = {"tf": 0, "tb": 0, "wseg": 0, "wbr": 0}

    def get(self, kind):
        lst = getattr(self, kind)
        t = lst[self.n[kind] % len(lst)]
        self.n[kind] += 1
        return t


def load_cols(S, dst, src_vec, nch):
    S.dma(S.sp, dst.ap[:, 0:nch], src_vec.rearrange("(c p) -> p c", p=128), writes=[dst.b], allow_slow_non_contiguous=True)


def headnorm_rope(S, R, K, C, p, M, ones_ap, gcol_ap, perm_ap, cos_ap, sin_ap, dim, out_ap):
    sqb = K.get("tb")
    S.op(S.act, lambda e: e.activation(out=sqb.ap[0:M, :], in_=p.ap[0:M, :], func=AF.Square), reads=[p.b], writes=[sqb.b])
    p2 = R.ps()
    S.op(S.pe, lambda e: e.matmul(p2.ap[0:M, :], lhsT=ones_ap, rhs=sqb.ap[0:M, :], start=True, stop=True),
         reads=[sqb.b, C.b], writes=[p2.b])
    rs = K.get("tf")
    S.op(S.act, lambda e: e.activation(out=rs.ap[0:M, :], in_=p2.ap[0:M, :], func=AF.Sqrt, scale=1.0 / dim, bias=R.eps.ap[0:M, 0:1]),
         reads=[p2.b, R.eps.b], writes=[rs.b])
    S.op(S.dve, lambda e: e.reciprocal(out=rs.ap[0:M, :], in_=rs.ap[0:M, :]), reads=[rs.b], writes=[rs.b])
    qn = K.get("tb")
    S.op(S.dve, lambda e: e.scalar_tensor_tensor(out=qn.ap[0:M, :], in0=p.ap[0:M, :], scalar=gcol_ap, in1=rs.ap[0:M, :], op0=ALU.mult, op1=ALU.mult),
         reads=[p.b, rs.b, C.b], writes=[qn.b])
    p3 = R.ps()
    S.op(S.pe, lambda e: e.matmul(p3.ap[0:M, :], lhsT=perm_ap, rhs=qn.ap[0:M, :], start=True, stop=True),
         reads=[qn.b, C.b], writes=[p3.b])
    t1 = K.get("tf")
    S.op(S.pool, lambda e: e.tensor_tensor(out=t1.ap[0:M, :], in0=qn.ap[0:M, :], in1=cos_ap, op=ALU.mult), reads=[qn.b, K.csb], writes=[t1.b])
    t2 = K.get("tf")
    S.op(S.dve, lambda e: e.tensor_tensor(out=t2.ap[0:M, :], in0=p3.ap[0:M, :], in1=sin_ap, op=ALU.mult), reads=[p3.b, K.csb], writes=[t2.b])
    ob = K.get("tb")
    S.op(S.pool, lambda e: e.tensor_tensor(out=ob.ap[0:M, :], in0=t1.ap[0:M, :], in1=t2.ap[0:M, :], op=ALU.add), reads=[t1.b, t2.b], writes=[ob.b])
    S.dma(S.pool, out_ap, ob.ap[0:M, :], reads=[ob.b])


class Consts:
    pass


def emit_mixprep(S, R, F, K, C, x, W, ti):
    t0 = ti * TT
    hm = F.h
    emit_rmsnorm(S, R, x, C.mixg, hm, F.sq, F.rstd, KC, TT, D)
    S.dma(S.pool, W["x1T"].rearrange("(c p) t -> p c t", p=128)[:, :, t0:t0 + TT], x.ap[:], reads=[x.b])
    qln = [F.actb.ap[:, k, :] for k in range(3)]
    kvn = [F.actb.ap[:, 3 + k, :] for k in range(2)]
    krope = F.actb.ap[0:32, 5, :]
    ab = F.actb.b

    def load_seg(seg):
        w = K.get("wseg")
        S.dma(S.sp, w.ap[:], W["win_s"][seg], writes=[w.b])
        return w

    def proj(w, n, M=128):
        cs = slice(n * 512, (n + 1) * 512)
        p = R.ps()
        for k in range(KC):
            S.op(S.pe, lambda e: e.matmul(p.ap[0:M, :], lhsT=w.ap[:, k, 0:M], rhs=hm.ap[:, k, cs], start=(k == 0), stop=(k == KC - 1)),
                 reads=[w.b, hm.b], writes=[p.b])
        return p

    def store_f32(p, dram_ap, func=None):
        t = K.get("tb")
        if func is None:
            S.op(S.act, lambda e: e.activation(out=t.ap[:], in_=p.ap[:], func=AF.Copy), reads=[p.b], writes=[t.b])
        else:
            S.op(S.act, lambda e: e.activation(out=t.ap[:], in_=p.ap[:], func=func), reads=[p.b], writes=[t.b])
        S.dma(S.pool, dram_ap, t.ap[:], reads=[t.b])

    for (seg0, name, func) in ((SEG_BG, "bgT", None), (SEG_LG, "lgT", AF.Gelu), (SEG_XB, "xbT", None)):
        for c in range(4):
            w = load_seg(seg0 + c)
            for n in range(NG):
                p = proj(w, n)
                store_f32(p, W[name][c * 128:(c + 1) * 128, t0 + n * 512:t0 + (n + 1) * 512], func)
    for c in range(4):
        wc = load_seg(SEG_CG + c)
        wx = load_seg(SEG_XS + c)
        for n in range(NG):
            pc = proj(wc, n)
            px = proj(wx, n)
            t = K.get("tf")
            S.op(S.act, lambda e: e.activation(out=t.ap[:], in_=pc.ap[:], func=AF.Copy), reads=[pc.b], writes=[t.b])
            t2 = K.get("tb")
            S.op(S.dve, lambda e: e.tensor_tensor(out=t2.ap[:], in0=t.ap[:], in1=px.ap[:], op=ALU.mult), reads=[t.b, px.b], writes=[t2.b])
            S.dma(S.pool, W["cxT"][c * 128:(c + 1) * 128, t0 + n * 512:t0 + (n + 1) * 512], t2.ap[:], reads=[t2.b])
    w = load_seg(SEG_KR)
    for n in range(NG):
        p = proj(w, n, M=32)
        S.op(S.act, lambda e: e.activation(out=krope[:, n * 512:(n + 1) * 512], in_=p.ap[0:32, :], func=AF.Copy), reads=[p.b], writes=[ab])
    for (seg0, nch, dst, gc, dim) in ((SEG_QL, 3, qln, C.qag, 384), (SEG_KVL, 2, kvn, C.kvag, 256)):
        ws = [load_seg(seg0 + c) for c in range(nch)]
        for n in range(NG):
            cs = slice(n * 512, (n + 1) * 512)
            ps_ = [proj(ws[c], n) for c in range(nch)]
            p2 = R.ps()
            for c in range(nch):
                sqb = K.get("tb")
                S.op(S.act, lambda e: e.activation(out=sqb.ap[:], in_=ps_[c].ap[:], func=AF.Square), reads=[ps_[c].b], writes=[sqb.b])
                S.op(S.pe, lambda e: e.matmul(p2.ap[:], lhsT=R.ones.ap[:], rhs=sqb.ap[:], start=(c == 0), stop=(c == nch - 1)),
                     reads=[sqb.b, R.ones.b], writes=[p2.b])
            rs = K.get("tf")
            S.op(S.act, lambda e: e.activation(out=rs.ap[:], in_=p2.ap[:], func=AF.Sqrt, scale=1.0 / dim, bias=R.eps.ap[:, 0:1]),
                 reads=[p2.b, R.eps.b], writes=[rs.b])
            S.op(S.dve, lambda e: e.reciprocal(out=rs.ap[:], in_=rs.ap[:]), reads=[rs.b], writes=[rs.b])
            for c in range(nch):
                S.op(S.dve, lambda e: e.scalar_tensor_tensor(out=dst[c][:, cs], in0=ps_[c].ap[:], scalar=gc.ap[:, c:c + 1], in1=rs.ap[:],
                                                           op0=ALU.mult, op1=ALU.mult),
                     reads=[ps_[c].b, rs.b, C.b], writes=[ab])
    def load_cs(n):
        g0 = t0 + n * 512
        S.dma(S.sp, K.cs[0].ap[:], W["cosg"][:, g0:g0 + 512], writes=[K.csb])
        S.dma(S.sp, K.cs[1].ap[:], W["sing"][:, g0:g0 + 512], writes=[K.csb])
        S.dma(S.sp, K.cs[2].ap[0:96, :], W["cosm"][:, g0:g0 + 512], writes=[K.csb])
        S.dma(S.sp, K.cs[3].ap[0:96, :], W["sinm"][:, g0:g0 + 512], writes=[K.csb])

    qg_flat = W["qg"].rearrange("h d t -> (h d) t")
    kg_flat = W["kg"].rearrange("h d t -> (h d) t")
    for n in range(NG):
        gcs = slice(t0 + n * 512, t0 + (n + 1) * 512)
        p = R.psum[7]
        for j in range(4):
            for k in range(KC):
                S.op(S.pe, lambda e: e.matmul(p.ap[:, j * 128:(j + 1) * 128], lhsT=hm.ap[:, k, n * 512 + j * 128:n * 512 + (j + 1) * 128],
                                              rhs=C.wvg.ap[:, k, :], start=(k == 0), stop=(k == KC - 1)),
                     reads=[hm.b, C.b], writes=[p.b])
        vb = K.get("tb")
        S.op(S.act, lambda e: e.activation(out=vb.ap[:], in_=p.ap[:], func=AF.Copy), reads=[p.b], writes=[vb.b])
        S.dma(S.pool, W["vg"][gcs, :].rearrange("(j p) c -> p j c", p=128), vb.ap[:].rearrange("p (j c) -> p j c", c=128), reads=[vb.b])
        for j in range(4):
            p = R.psum[7]
            for k in range(2):
                S.op(S.pe, lambda e: e.matmul(p.ap[:], lhsT=kvn[k][:, n * 512 + j * 128:n * 512 + (j + 1) * 128], rhs=C.wvm.ap[:, k, :],
                                              start=(k == 0), stop=(k == 1)),
                     reads=[ab, C.b], writes=[p.b])
            vb = K.get("tb")
            S.op(S.act, lambda e: e.activation(out=vb.ap[:], in_=p.ap[:], func=AF.Copy), reads=[p.b], writes=[vb.b])
            r0 = t0 + n * 512 + j * 128
            S.dma(S.pool, W["vm"][r0:r0 + 128, :], vb.ap[:], reads=[vb.b])

    items = []
    for n in range(NG):
        for c in range(5):
            items.append(("g", n, c))
        for h in range(8):
            items.append(("q", n, h))
        for h in range(8):
            items.append(("k", n, h))
    ctx = [dict() for _ in items]
    pb_p, pb_2, pb_3 = R.psum[0:3], R.psum[3:5], R.psum[5:7]

    def st_load(i):
        kind, n, c = items[i]
        w = K.get("wseg")
        if kind == "g":
            S.dma(S.sp, w.ap[:], W["win_s"][SEG_Q + c], writes=[w.b])
        elif kind == "q":
            S.dma(S.sp, w.ap[:, 0:3, :], W["wq_s"][c], writes=[w.b])
        else:
            S.dma(S.sp, w.ap[:, 0:2, :], W["wk_s"][c], writes=[w.b])
        ctx[i]["w"] = w
        if (kind, c) == ("g", 0):
            cset = K.cs2[n % 2]
            g0 = t0 + n * 512
            S.dma(S.sp, cset[0].ap[:], W["cosg"][:, g0:g0 + 512], writes=[cset[0].b])
            S.dma(S.sp, cset[1].ap[:], W["sing"][:, g0:g0 + 512], writes=[cset[1].b])
            S.dma(S.sp, cset[2].ap[0:96, :], W["cosm"][:, g0:g0 + 512], writes=[cset[2].b])
            S.dma(S.sp, cset[3].ap[0:96, :], W["sinm"][:, g0:g0 + 512], writes=[cset[3].b])

    def st_proj(i):
        kind, n, c = items[i]
        cs = slice(n * 512, (n + 1) * 512)
        w = ctx[i]["w"]
        p = pb_p[i % 3]
        if kind == "g":
            for k in range(KC):
                S.op(S.pe, lambda e: e.matmul(p.ap[:], lhsT=w.ap[:, k, :], rhs=hm.ap[:, k, cs], start=(k == 0), stop=(k == KC - 1)),
                     reads=[w.b, hm.b], writes=[p.b])
        elif kind == "q":
            for k in range(3):
                S.op(S.pe, lambda e: e.matmul(p.ap[0:96, :], lhsT=w.ap[:, k, 0:96], rhs=qln[k][:, cs], start=(k == 0), stop=(k == 2)),
                     reads=[w.b, ab], writes=[p.b])
        else:
            for k in range(2):
                S.op(S.pe, lambda e: e.matmul(p.ap[0:64, :], lhsT=w.ap[:, k, 0:64], rhs=kvn[k][:, cs], start=(k == 0), stop=(k == 1)),
                     reads=[w.b, ab], writes=[p.b])
            S.op(S.pe, lambda e: e.matmul(p.ap[64:96, :], lhsT=C.ident.ap[0:32, 0:32], rhs=krope[:, cs], start=True, stop=True),
                 reads=[C.b, ab], writes=[p.b])
        ctx[i]["p"] = p

    def params(i):
        kind, n, c = items[i]
        gcs = slice(t0 + n * 512, t0 + (n + 1) * 512)
        cset = K.cs2[n % 2]
        if kind == "g":
            gcol = C.gq.ap[:, 0:1] if c < 4 else C.gk.ap[:, 0:1]
            dst = qg_flat[c * 128:(c + 1) * 128, gcs] if c < 4 else kg_flat[:, gcs]
            return 128, C.bd64.ap[:], gcol, C.permg.ap[:], cset[0], cset[1], 64, dst
        gcol = C.mq.ap[0:96, 0:1] if kind == "q" else C.mk.ap[0:96, 0:1]
        dst = W["qm"][c, :, gcs] if kind == "q" else W["km"][c, :, gcs]
        return 96, R.ones.ap[0:96, 0:96], gcol, C.permm.ap[0:96, 0:96], cset[2], cset[3], 96, dst

    def st_sq(i):
        M = params(i)[0]
        p = ctx[i]["p"]
        sqb = K.get("tb")
        S.op(S.act, lambda e: e.activation(out=sqb.ap[0:M, :], in_=p.ap[0:M, :], func=AF.Square), reads=[p.b], writes=[sqb.b])
        ctx[i]["sqb"] = sqb

    def st_ones(i):
        M, ones_ap = params(i)[0:2]
        sqb = ctx[i]["sqb"]
        p2 = pb_2[i % 2]
        S.op(S.pe, lambda e: e.matmul(p2.ap[0:M, :], lhsT=ones_ap, rhs=sqb.ap[0:M, :], start=True, stop=True), reads=[sqb.b, C.b, R.ones.b], writes=[p2.b])
        ctx[i]["p2"] = p2

    def st_norm(i):
        M, _, gcol, _, _, _, dim, _ = params(i)
        p, p2 = ctx[i]["p"], ctx[i]["p2"]
        rs = K.get("tf")
        S.op(S.act, lambda e: e.activation(out=rs.ap[0:M, :], in_=p2.ap[0:M, :], func=AF.Sqrt, scale=1.0 / dim, bias=R.eps.ap[0:M, 0:1]),
             reads=[p2.b, R.eps.b], writes=[rs.b])
        S.op(S.dve, lambda e: e.reciprocal(out=rs.ap[0:M, :], in_=rs.ap[0:M, :]), reads=[rs.b], writes=[rs.b])
        qn = K.get("tb")
        S.op(S.dve, lambda e: e.scalar_tensor_tensor(out=qn.ap[0:M, :], in0=p.ap[0:M, :], scalar=gcol, in1=rs.ap[0:M, :], op0=ALU.mult, op1=ALU.mult),
             reads=[p.b, rs.b, C.b], writes=[qn.b])
        ctx[i]["qn"] = qn

    def st_perm(i):
        M, _, _, perm_ap = params(i)[0:4]
        qn = ctx[i]["qn"]
        p3 = pb_3[i % 2]
        S.op(S.pe, lambda e: e.matmul(p3.ap[0:M, :], lhsT=perm_ap, rhs=qn.ap[0:M, :], start=True, stop=True), reads=[qn.b, C.b], writes=[p3.b])
        ctx[i]["p3"] = p3

    def st_rope(i):
        M, _, _, _, cosT, sinT, _, dst = params(i)
        qn, p3 = ctx[i]["qn"], ctx[i]["p3"]
        t1 = K.get("tf")
        S.op(S.pool, lambda e: e.tensor_tensor(out=t1.ap[0:M, :], in0=qn.ap[0:M, :], in1=cosT.ap[0:M, :], op=ALU.mult), reads=[qn.b, cosT.b], writes=[t1.b])
        t2 = K.get("tf")
        S.op(S.dve, lambda e: e.tensor_tensor(out=t2.ap[0:M, :], in0=p3.ap[0:M, :], in1=sinT.ap[0:M, :], op=ALU.mult), reads=[p3.b, sinT.b], writes=[t2.b])
        ob = K.get("tb")
        S.op(S.pool, lambda e: e.tensor_tensor(out=ob.ap[0:M, :], in0=t1.ap[0:M, :], in1=t2.ap[0:M, :], op=ALU.add), reads=[t1.b, t2.b], writes=[ob.b])
        S.dma(S.pool, dst, ob.ap[0:M, :], reads=[ob.b])
        ctx[i].clear()

    NI = len(items)
    for s_ in range(-1, NI + 4):
        if 0 <= s_ + 1 < NI:
            st_load(s_ + 1)
        if 0 <= s_ - 1 < NI:
            st_sq(s_ - 1)
        if 0 <= s_ - 2 < NI:
            st_norm(s_ - 2)
        if 0 <= s_ < NI:
            st_proj(s_)
        if 0 <= s_ - 1 < NI:
            st_ones(s_ - 1)
        if 0 <= s_ - 3 < NI:
            st_perm(s_ - 3)
        if 0 <= s_ - 4 < NI:
            st_rope(s_ - 4)


def emit_merge(S, R, F, K, C, x, W, ti):
    t0 = ti * TT
    hm = F.h
    emit_rmsnorm(S, R, x, C.mixg_a, hm, F.sq, F.rstd, KC, TT, D)
    ab = F.actb.b
    merged = F.sq
    for n, name in enumerate(("oaT", "obT", "ocT", "odT")):
        S.dma(S.sp, F.actb.ap[:, 4 * n:4 * n + 4, :], W[name].rearrange("(k p) t -> p k t", p=128)[:, :, t0:t0 + TT],
              reads=([W["o_buf"]] if "o_buf" in W else []), writes=[ab])
    for d in range(KC):
        wb = K.get("wbr")
        S.dma(S.sp, wb.ap[:], W["wbr_s"][:, d].rearrange("n p k c -> p n k c"), writes=[wb.b])
        macc = [None] * NG
        for n in range(4):
            gw = K.get("wseg")
            S.dma(S.sp, gw.ap[:], W["wina_s"][SEG_GATE + n * 8 + d], writes=[gw.b])
            for g in range(NG):
                cs = slice(g * 512, (g + 1) * 512)
                py = R.ps()
                for k in range(4):
                    S.op(S.pe, lambda e: e.matmul(py.ap[:], lhsT=wb.ap[:, n, k, :], rhs=F.actb.ap[:, 4 * n + k, cs], start=(k == 0), stop=(k == 3)),
                         reads=[wb.b, ab], writes=[py.b])
                pg = R.ps()
                for k in range(KC):
                    S.op(S.pe, lambda e: e.matmul(pg.ap[:], lhsT=gw.ap[:, k, :], rhs=hm.ap[:, k, cs], start=(k == 0), stop=(k == KC - 1)),
                         reads=[gw.b, hm.b], writes=[pg.b])
                gs = K.get("tf")
                S.op(S.act, lambda e: e.activation(out=gs.ap[:], in_=pg.ap[:], func=AF.Sigmoid), reads=[pg.b], writes=[gs.b])
                if n == 0:
                    m = K.mt[g]
                    S.op(S.dve, lambda e: e.tensor_tensor(out=m.ap[:], in0=gs.ap[:], in1=py.ap[:], op=ALU.mult), reads=[gs.b, py.b], writes=[m.b])
                else:
                    m = K.mt[g]
                    S.op(S.dve, lambda e: e.tensor_tensor(out=gs.ap[:], in0=gs.ap[:], in1=py.ap[:], op=ALU.mult), reads=[gs.b, py.b], writes=[gs.b])
                    if n < 3:
                        S.op(S.pool, lambda e: e.tensor_tensor(out=m.ap[:], in0=m.ap[:], in1=gs.ap[:], op=ALU.add), reads=[m.b, gs.b], writes=[m.b])
                    else:
                        S.op(S.pool, lambda e: e.tensor_tensor(out=merged.ap[:, d, cs], in0=m.ap[:], in1=gs.ap[:], op=ALU.add),
                             reads=[m.b, gs.b], writes=[merged.b])
    for d in range(KC):
        w = K.get("wseg")
        S.dma(S.sp, w.ap[:], W["wout_s"][d], writes=[w.b])
        for g in range(NG):
            cs = slice(g * 512, (g + 1) * 512)
            p = R.ps()
            for k in range(KC):
                S.op(S.pe, lambda e: e.matmul(p.ap[:], lhsT=w.ap[:, k, :], rhs=merged.ap[:, k, cs], start=(k == 0), stop=(k == KC - 1)),
                     reads=[w.b, merged.b], writes=[p.b])
            S.op(S.dve, lambda e: e.tensor_tensor(out=x.ap[:, d, cs], in0=p.ap[:], in1=x.ap[:, d, cs], op=ALU.add), reads=[p.b, x.b], writes=[x.b])


def _dram_in(nc, name, shape, dt=F32):
    return nc.dram_tensor(name, list(shape), dt, kind="ExternalInput").ap()


def _dram_out(nc, name, shape, dt=F32):
    return nc.dram_tensor(name, list(shape), dt, kind="ExternalOutput").ap()


from concourse.bass import ds

GROUPS = [[0, 1, 2, 3], [4, 5, 6, 7]]
WNAMES = {"ffn1_norm": [D], "ffn1_wi": [D, 2 * DFF], "ffn1_wo": [DFF, D], "mix_norm": [D], "w_in": [D, INTOT],
          "gqa_q_norm": [64], "gqa_k_norm": [64], "mla_qa_norm": [384], "mla_wq_up": [384, 768], "mla_kva_norm": [256],
          "mla_wkv_up": [256, 1024], "mla_q_norm": [96], "mla_k_norm": [96], "w_branch": [4, 512, D], "w_out": [D, D],
          "ffn2_norm": [D], "ffn2_wi": [D, 2 * DFF], "ffn2_wo": [DFF, D]}
CH = 1024
NCHK = SEQ // CH


def build_fused():
    nc = bass.Bass("TRN2", target_bir_lowering=False)
    I = {"xT": _dram_in(nc, "xT", [D, NT]), "cmat": _dram_in(nc, "cmat", [4, 128, 128])}
    for nm, shp in (("cosg", [128, NT]), ("sing", [128, NT]), ("cosm", [96, NT]), ("sinm", [96, NT]),
                    ("scw", [NL, 128, 4]), ("lruw", [NL, 128, 5]), ("lrug", [NL, 4, 128, 128]), ("lrub", [NL, 128, 6])):
        I[nm] = _dram_in(nc, nm, shp)
    for nm, shp in WNAMES.items():
        I[nm] = _dram_in(nc, nm, [NL] + shp)
    xoutT = _dram_out(nc, "xoutT", [D, NT])
    dt_ = lambda name, shape, dt=BF16: nc.dram_tensor(name, list(shape), dt).ap()
    SCR = []
    for l in range(NL):
        SCR.append({"f1wi_s": dt_("f1wi_s%d" % l, [2 * FC, 128, KC, 128]), "f1wo_s": dt_("f1wo_s%d" % l, [KC, 128, FC, 128]),
                    "f2wi_s": dt_("f2wi_s%d" % l, [2 * FC, 128, KC, 128]), "f2wo_s": dt_("f2wo_s%d" % l, [KC, 128, FC, 128]),
                    "win_s": dt_("win_s%d" % l, [64, 128, KC, 128]), "wq_s": dt_("wq_s%d" % l, [8, 128, 3, 128]),
                    "wk_s": dt_("wk_s%d" % l, [8, 128, 2, 128]), "wbr_s": dt_("wbr_s%d" % l, [4, 8, 128, 4, 128]),
                    "wout_s": dt_("wout_s%d" % l, [8, 128, KC, 128])})
    xcur = dt_("xcur", [D, NT], F32)
    qg = dt_("qg", [8, 64, NT]); qm = dt_("qm", [8, 96, NT])
    oaT = dt_("oaT", [512, NT]); ocT = dt_("ocT", [512, NT])
    kg_s = dt_("kg_s", [2, 64, NT]); kgG = dt_("kgG", [4, 128, NT])
    vg_s = dt_("vg_s", [NT, 128]); vgG = dt_("vgG", [4, NT, 128])
    km_s = dt_("km_s", [8, 96, NT]); kmG = dt_("kmG", [8, 4, 96, NT])
    vm_s = dt_("vm_s", [NT, 512]); vmG = dt_("vmG", [4, 4, 1024, 512])
    XS = {nm: dt_(nm + "_s", [512, NT]) for nm in ("bg", "cx", "lg", "xb")}
    XG = {nm: dt_(nm + "G", [4, 4, 128, NT]) for nm in ("bg", "cx", "lg", "xb")}
    ob_s = dt_("ob_s", [4, 128, NT]); od_s = dt_("od_s", [4, 128, NT])
    obG = dt_("obG", [4, 4, 128, NT]); odG = dt_("odG", [4, 4, 128, NT])
    hf_s = dt_("hf_s", [128, SEQ], F32)
    XL = {nm: dt_(nm + "L", [4, 128, NT]) for nm in ("bg", "cx", "lg", "xb")}
    obL = dt_("obL", [512, NT]); odL = dt_("odL", [512, NT])

    with ExitStack() as es:
        S = Sch(nc, es)
        psum = [T(es.enter_context(nc.psum_tensor("ps%d" % i, [128, 512], F32)), "ps%d" % i) for i in range(8)]
        pid = nc.sync.partition_id()
        jr = pid % 4

        def cast_layer(R, l, front, back):
            sc = SCR[l]
            if front:
                yield from cast_weight(S, R, I["ffn1_wi"][l], sc["f1wi_s"], [(0, FC, 0, 2, 128), (DFF, FC, 1, 2, 128)])
                yield from cast_weight(S, R, I["ffn1_wo"][l], sc["f1wo_s"], [(0, KC, 0, 1, 128)])
                yield from cast_weight(S, R, I["w_in"][l], sc["win_s"], WIN_RUNS)
                yield from cast_weight(S, R, I["mla_wq_up"][l], sc["wq_s"], [(h * 96, 1, h, 1, 96) for h in range(8)])
                yield from cast_weight(S, R, I["mla_wkv_up"][l], sc["wk_s"], [(h * 128, 1, h, 1, 64) for h in range(8)])
            if back:
                for n in range(4):
                    yield from cast_weight(S, R, I["w_branch"][l, n], sc["wbr_s"][n], [(0, 8, 0, 1, 128)])
                yield from cast_weight(S, R, I["w_out"][l], sc["wout_s"], [(0, 8, 0, 1, 128)])
                yield from cast_weight(S, R, I["ffn2_wi"][l], sc["f2wi_s"], [(0, FC, 0, 2, 128), (DFF, FC, 1, 2, 128)])
                yield from cast_weight(S, R, I["ffn2_wo"][l], sc["f2wo_s"], [(0, KC, 0, 1, 128)])

        with ExitStack() as pes:
            R = Res(nc, pes, S, TT, psum, tag="pr_")
            for _ in cast_layer(R, 0, True, False):
                pass
            S.barrier()

        def tok_phase(la, lb, tag):
            with ExitStack() as tes:
                R = Res(nc, tes, S, TT, psum, tag=tag, nstage=1)
                F = FFNRes(R)
                K = Tok(R)
                K.csb = Buf("cs")
                K.mt = [R.sb("mt%d" % i, [128, 512], F32) for i in range(NG)]
                C = Consts()
                C.b = Buf("consts")
                sb = R.sb
                x = sb("x", [128, KC, TT], F32)

                def cmat(i, name):
                    t = sb(name, [128, 128], BF16)
                    st, _ = R.cast_stage()
                    S.dma(S.sp, st.ap[:, 0:128], I["cmat"][i], writes=[st.b])
                    S.op(S.dve, lambda e: e.tensor_copy(out=t.ap[:], in_=st.ap[:, 0:128]), reads=[st.b], writes=[C.b])
                    return t
                C.bd64 = cmat(0, "bd64"); C.permg = cmat(1, "permg"); C.permm = cmat(2, "permm"); C.ident = cmat(3, "ident")

                def cols(name, src, nch):
                    t = sb(name, [128, nch], F32)
                    S.dma(S.sp, t.ap[:, 0:nch], src.rearrange("(c p) -> p c", p=128), writes=[C.b], allow_slow_non_contiguous=True)
                    return t

                def col1(name, src, n, rep):
                    t = sb(name, [128, 1], F32)
                    for r in range(rep):
                        S.dma(S.sp, t.ap[r * n:(r + 1) * n, 0:1], src.rearrange("(p o) -> p o", o=1), writes=[C.b], allow_slow_non_contiguous=True)
                    return t
                W = {"cosg": I["cosg"], "sing": I["sing"], "cosm": I["cosm"], "sinm": I["sinm"], "x1T": xcur, "qg": qg, "qm": qm,
                     "kg": kg_s, "vg": vg_s, "km": km_s, "vm": vm_s, "bgT": XS["bg"], "cxT": XS["cx"], "lgT": XS["lg"], "xbT": XS["xb"]}
                if la is not None:
                    C.mixg_a = cols("mixg_a", I["mix_norm"][la], 8)
                    C.f2g = cols("f2g", I["ffn2_norm"][la], 8)
                    olb = Buf("olb")
                    S.dma(S.sp, obL, obG[ds(jr, 1)].rearrange("o r p t -> (o r p) t"), writes=[olb])
                    S.dma(S.sp, odL, odG[ds(jr, 1)].rearrange("o r p t -> (o r p) t"), writes=[olb])
                    W.update({"oaT": oaT, "ocT": ocT, "obT": obL, "odT": odL, "o_buf": olb,
                              "wina_s": SCR[la]["win_s"], "wbr_s": SCR[la]["wbr_s"], "wout_s": SCR[la]["wout_s"]})
                if lb is not None:
                    C.f1g = cols("f1g", I["ffn1_norm"][lb], 8)
                    C.mixg = cols("mixg", I["mix_norm"][lb], 8)
                    C.qag = cols("qag", I["mla_qa_norm"][lb], 3)
                    C.kvag = cols("kvag", I["mla_kva_norm"][lb], 2)
                    C.gq = col1("gq", I["gqa_q_norm"][lb], 64, 2)
                    C.gk = col1("gk", I["gqa_k_norm"][lb], 64, 2)
                    C.mq = col1("mq", I["mla_q_norm"][lb], 96, 1)
                    C.mk = col1("mk", I["mla_k_norm"][lb], 96, 1)
                    C.wvg = sb("wvg", [128, KC, 128], BF16)
                    st, _ = R.cast_stage()
                    S.dma(S.sp, st.ap[:, 0:1024].rearrange("p (k c) -> p k c", c=128),
                          I["w_in"][lb][:, 640:768].rearrange("(k p) c -> p k c", p=128), writes=[st.b])
                    S.op(S.dve, lambda e: e.tensor_copy(out=C.wvg.ap[:].rearrange("p k c -> p (k c)"), in_=st.ap[:, 0:1024]), reads=[st.b], writes=[C.b])
                    C.wvm = sb("wvm", [128, 2, 512], BF16)
                    st, _ = R.cast_stage()
                    for k in range(2):
                        S.dma(S.sp, st.ap[:, k * 512:(k + 1) * 512].rearrange("p (h c) -> p h c", c=64),
                              I["mla_wkv_up"][lb][k * 128:(k + 1) * 128, :].rearrange("p (h c) -> p h c", c=128)[:, :, 64:128], writes=[st.b])
                    S.op(S.dve, lambda e: e.tensor_copy(out=C.wvm.ap[:].rearrange("p k c -> p (k c)"), in_=st.ap[:, 0:1024]), reads=[st.b], writes=[C.b])
                    W.update({"win_s": SCR[lb]["win_s"], "wq_s": SCR[lb]["wq_s"], "wk_s": SCR[lb]["wk_s"]})
                xsrc = I["xT"] if la is None else xcur
                xin = xsrc.rearrange("(c p) t -> p c t", p=128)
                for ti in range(NTILE):
                    t0 = ti * TT
                    S.dma(S.sp, x.ap[:], xin[:, :, t0:t0 + TT], writes=[x.b])
                    if la is not None:
                        emit_merge(S, R, F, K, C, x, W, ti)
                        emit_ffn(S, R, F, x, C.f2g, SCR[la]["f2wi_s"], SCR[la]["f2wo_s"])
                    if lb is not None:
                        emit_ffn(S, R, F, x, C.f1g, SCR[lb]["f1wi_s"], SCR[lb]["f1wo_s"])
                        emit_mixprep(S, R, F, K, C, x, W, ti)
                    else:
                        S.dma(S.pool, xoutT.rearrange("(c p) t -> p c t", p=128)[:, :, t0:t0 + TT], x.ap[:], reads=[x.b])
                S.barrier()

        def p2_phase(l, tag):
            cc0 = S.cccnt
            S.collective(kg_s.rearrange("h d t -> (h d) t"), kgG.rearrange("r p t -> (r p) t"), GROUPS)
            S.collective(vg_s, vgG.rearrange("r t c -> (r t) c"), GROUPS)
            for ck in range(4):
                S.collective(vm_s[ck * 1024:(ck + 1) * 1024, :], vmG[ck].rearrange("r t c -> (r t) c"), GROUPS)
            for h in range(8):
                S.collective(km_s[h], kmG[h].rearrange("r p t -> (r p) t"), GROUPS)
            for nm in ("cx", "bg", "xb", "lg"):
                for jj in range(4):
                    S.collective(XS[nm][jj * 128:(jj + 1) * 128, :], XG[nm][jj].rearrange("r p t -> (r p) t"), GROUPS)
            n_x_cc = S.cccnt
            with ExitStack() as tes:
                sbt = lambda name, shape, dt: T(tes.enter_context(nc.sbuf_tensor(tag + name, shape, dt)), name)
                cb = Buf("consts")
                onesf = sbt("onesf", [128, 64], F32)
                S.op(S.dve, lambda e: e.memset(onesf.ap[:], 1.0), writes=[cb])
                onec = sbt("onec", [128, 1], F32)
                S.op(S.dve, lambda e: e.memset(onec.ap[:], 1.0), writes=[cb])
                scw = sbt("scw_sb", [128, 4], F32)
                lruw = sbt("lruw_sb", [128, 5], F32)
                lrub = sbt("lrub_sb", [128, 6], F32)
                S.dma(S.sp, scw.ap[:], I["scw"][l], writes=[cb])
                S.dma(S.sp, lruw.ap[:], I["lruw"][l], writes=[cb])
                S.dma(S.sp, lrub.ap[:], I["lrub"][l], writes=[cb])
                gst = sbt("gst", [128, 128], F32)
                gmat = [sbt("gmat%d" % i, [128, 128], BF16) for i in range(4)]
                for i in range(4):
                    S.dma(S.sp, gst.ap[:], I["lrug"][l, i], writes=[gst.b])
                    S.op(S.dve, lambda e: e.tensor_copy(out=gmat[i].ap[:], in_=gst.ap[:]), reads=[gst.b], writes=[cb])
                nlam = sbt("nlam", [128, 2], F32)
                for dr in range(2):
                    S.op(S.act, lambda e: e.activation(out=nlam.ap[:, dr:dr + 1], in_=lrub.ap[:, 3 * dr + 2:3 * dr + 3], func=AF.Exp, scale=-1.0),
                         reads=[cb], writes=[nlam.b])
                S.op(S.act, lambda e: e.activation(out=nlam.ap[:], in_=nlam.ap[:], func=AF.Ln, bias=onec.ap[:, 0:1]), reads=[nlam.b, cb], writes=[nlam.b])
                S.op(S.dve, lambda e: e.tensor_scalar(out=nlam.ap[:], in0=nlam.ap[:], scalar1=-8.0, scalar2=None, op0=ALU.mult), reads=[nlam.b], writes=[nlam.b])
                ft = {nm: sbt("l_" + nm, [128, CH + 6], F32) for nm in ("xc", "r", "i", "a", "m", "bv", "h0", "h1", "hf")}
                bt = {nm: sbt("l_" + nm, [128, CH + 6], BF16) for nm in ("xw", "bg", "lg")}
                xcb = sbt("l_xcb", [128, CH], BF16)
                outb = sbt("l_outb", [128, CH], BF16)

                def load_seq(dst, off, G, lo, hi):
                    Gj = G
                    t = lo
                    while t < hi:
                        r = t // NT
                        e_ = min(hi, (r + 1) * NT)
                        a_, b_ = t, e_
                        if b_ - a_ == 1:
                            if a_ - 1 >= r * NT:
                                a_ -= 1
                            else:
                                b_ += 1
                        S.dma(S.sp, dst.ap[:, off + (a_ - lo):off + (b_ - lo)], Gj[r, :, a_ - r * NT:b_ - r * NT], reads=[xlb], writes=[dst.b])
                        t = e_

                xlb = Buf("xlb")

                def conv_lru():
                    S._wait(S.sp, ("cc", S.ccsem, n_x_cc))
                    for nm in ("cx", "bg", "xb", "lg"):
                        S.dma(S.sp, XL[nm].rearrange("r p t -> (r p) t"), XG[nm][ds(jr, 1)].rearrange("o r p t -> (o r p) t"), writes=[xlb])
                    yield
                    for ci in range(NCHK):
                        t0 = ci * CH
                        cw, bgt, acc = bt["xw"], bt["bg"], ft["xc"]
                        lo, hi = max(t0 - 1, 0), min(t0 + CH + 1, SEQ)
                        if ci == 0:
                            S.op(S.pool, lambda e: e.memset(cw.ap[:, 1:2], 0.0), writes=[cw.b])
                        if ci == NCHK - 1:
                            S.op(S.pool, lambda e: e.memset(cw.ap[:, CH + 2:CH + 3], 0.0), writes=[cw.b])
                        load_seq(cw, 1 + lo - (t0 - 1), XL["cx"], lo, hi)
                        load_seq(bgt, 0, XL["bg"], t0, t0 + CH)
                        S.op(S.dve, lambda e: e.tensor_scalar(out=acc.ap[:, 0:CH], in0=cw.ap[:, 1:1 + CH], scalar1=scw.ap[:, 0:1], scalar2=scw.ap[:, 3:4],
                                                              op0=ALU.mult, op1=ALU.add), reads=[cw.b, cb], writes=[acc.b])
                        for k in (1, 2):
                            S.op(S.dve, lambda e: e.scalar_tensor_tensor(out=acc.ap[:, 0:CH], in0=cw.ap[:, 1 + k:1 + k + CH], scalar=scw.ap[:, k:k + 1], in1=acc.ap[:, 0:CH],
                                                                         op0=ALU.mult, op1=ALU.add), reads=[cw.b, cb, acc.b], writes=[acc.b])
                        S.op(S.pool, lambda e: e.tensor_tensor(out=outb.ap[:], in0=acc.ap[:, 0:CH], in1=bgt.ap[:, 0:CH], op=ALU.mult),
                             reads=[acc.b, bgt.b], writes=[outb.b])
                        q = t0 // NT
                        S.dma(S.pool, ob_s[q, :, t0 - q * NT:t0 - q * NT + CH], outb.ap[:], reads=[outb.b])
                        yield
                    hfb = Buf("hf_s")

                    def lru_chunk(ci, dr, hprev, hcur):
                        t0 = ci * CH
                        xw, xc = bt["xw"], ft["xc"]
                        lo, hi = max(t0 - 2, 0), min(t0 + CH + 1, SEQ)
                        if ci == 0:
                            S.op(S.pool, lambda e: e.memset(xw.ap[:, 1:3], 0.0), writes=[xw.b])
                        if ci == NCHK - 1:
                            S.op(S.pool, lambda e: e.memset(xw.ap[:, CH + 3:CH + 4], 0.0), writes=[xw.b])
                        load_seq(xw, 1 + lo - (t0 - 2), XL["xb"], lo, hi)
                        S.op(S.dve, lambda e: e.tensor_scalar(out=xc.ap[:, 0:CH], in0=xw.ap[:, 1:1 + CH], scalar1=lruw.ap[:, 0:1], scalar2=lruw.ap[:, 4:5],
                                                              op0=ALU.mult, op1=ALU.add), reads=[xw.b, cb], writes=[xc.b])
                        for k in (1, 2, 3):
                            S.op(S.dve, lambda e: e.scalar_tensor_tensor(out=xc.ap[:, 0:CH], in0=xw.ap[:, 1 + k:1 + k + CH], scalar=lruw.ap[:, k:k + 1], in1=xc.ap[:, 0:CH],
                                                                         op0=ALU.mult, op1=ALU.add), reads=[xw.b, cb, xc.b], writes=[xc.b])
                        S.op(S.pool, lambda e: e.tensor_copy(out=xcb.ap[:], in_=xc.ap[:, 0:CH]), reads=[xc.b], writes=[xcb.b])
                        yield
                        r, i_, a, m, bv = ft["r"], ft["i"], ft["a"], ft["m"], ft["bv"]
                        for n in range(CH // 512):
                            cs = slice(n * 512, (n + 1) * 512)
                            pr, pi = psum[6], psum[7]
                            S.op(S.pe, lambda e: e.matmul(pr.ap[:], lhsT=gmat[2 * dr].ap[:], rhs=xcb.ap[:, cs], start=True, stop=True), reads=[xcb.b, cb], writes=[pr.b])
                            S.op(S.pe, lambda e: e.matmul(pi.ap[:], lhsT=gmat[2 * dr + 1].ap[:], rhs=xcb.ap[:, cs], start=True, stop=True), reads=[xcb.b, cb], writes=[pi.b])
                            S.op(S.act, lambda e: e.activation(out=r.ap[:, cs], in_=pr.ap[:], func=AF.Sigmoid, bias=lrub.ap[:, 3 * dr:3 * dr + 1]),
                                 reads=[pr.b, cb], writes=[r.b])
                            S.op(S.act, lambda e: e.activation(out=i_.ap[:, cs], in_=pi.ap[:], func=AF.Sigmoid, bias=lrub.ap[:, 3 * dr + 1:3 * dr + 2]),
                                 reads=[pi.b, cb], writes=[i_.b])
                        S.op(S.act, lambda e: e.activation(out=a.ap[:, 0:CH], in_=r.ap[:, 0:CH], func=AF.Exp, scale=nlam.ap[:, dr:dr + 1]),
                             reads=[r.b, nlam.b], writes=[a.b])
                        S.op(S.pool, lambda e: e.tensor_tensor(out=m.ap[:, 0:CH], in0=a.ap[:, 0:CH], in1=a.ap[:, 0:CH], op=ALU.mult), reads=[a.b], writes=[m.b])
                        S.op(S.act, lambda e: e.activation(out=m.ap[:, 0:CH], in_=m.ap[:, 0:CH], func=AF.Sqrt, scale=-1.0, bias=onec.ap[:, 0:1]),
                             reads=[m.b, cb], writes=[m.b])
                        S.op(S.pool, lambda e: e.tensor_tensor(out=bv.ap[:, 0:CH], in0=i_.ap[:, 0:CH], in1=xc.ap[:, 0:CH], op=ALU.mult), reads=[i_.b, xc.b], writes=[bv.b])
                        S.op(S.dve, lambda e: e.tensor_tensor(out=bv.ap[:, 0:CH], in0=bv.ap[:, 0:CH], in1=m.ap[:, 0:CH], op=ALU.mult), reads=[bv.b, m.b], writes=[bv.b])
                        if dr == 0:
                            init = 0.0 if hprev is None else hprev.ap[:, CH - 1:CH]
                            S.op(S.dve, lambda e: e.tensor_tensor_scan(out=hcur.ap[:, 0:CH], data0=a.ap[:, 0:CH], data1=bv.ap[:, 0:CH], initial=init,
                                                                       op0=ALU.mult, op1=ALU.add),
                                 reads=[a.b, bv.b] + ([hprev.b] if hprev is not None else []), writes=[hcur.b])
                        else:
                            init = 0.0 if hprev is None else hprev.ap[:, 0:1]
                            S.op(S.dve, lambda e: e.tensor_tensor_scan(out=hcur.ap[:, 0:CH][:, ::-1], data0=a.ap[:, 0:CH][:, ::-1], data1=bv.ap[:, 0:CH][:, ::-1],
                                                                       initial=init, op0=ALU.mult, op1=ALU.add),
                                 reads=[a.b, bv.b] + ([hprev.b] if hprev is not None else []), writes=[hcur.b])

                    hprev = None
                    for ci in range(NCHK):
                        hcur = ft["h%d" % (ci % 2)]
                        yield from lru_chunk(ci, 0, hprev, hcur)
                        S.dma(S.pool, hf_s[:, ci * CH:(ci + 1) * CH], hcur.ap[:, 0:CH], reads=[hcur.b], writes=[hfb])
                        hprev = hcur
                        yield
                    hprev = None
                    for ci in range(NCHK - 1, -1, -1):
                        t0 = ci * CH
                        hcur = ft["h%d" % (ci % 2)]
                        yield from lru_chunk(ci, 1, hprev, hcur)
                        hf, lg = ft["hf"], bt["lg"]
                        S.dma(S.sp, hf.ap[:, 0:CH], hf_s[:, t0:t0 + CH], reads=[hfb], writes=[hf.b])
                        load_seq(lg, 0, XL["lg"], t0, t0 + CH)
                        S.op(S.pool, lambda e: e.tensor_tensor(out=hf.ap[:, 0:CH], in0=hf.ap[:, 0:CH], in1=hcur.ap[:, 0:CH], op=ALU.add), reads=[hf.b, hcur.b], writes=[hf.b])
                        S.op(S.pool, lambda e: e.tensor_tensor(out=outb.ap[:], in0=hf.ap[:, 0:CH], in1=lg.ap[:, 0:CH], op=ALU.mult), reads=[hf.b, lg.b], writes=[outb.b])
                        q = t0 // NT
                        S.dma(S.pool, od_s[q, :, t0 - q * NT:t0 - q * NT + CH], outb.ap[:], reads=[outb.b])
                        hprev = hcur
                        yield
                    S.dma_barrier(S.pool)
                    for q in range(4):
                        S.collective(ob_s[q], obG[q].rearrange("r p t -> (r p) t"), GROUPS)
                        S.collective(od_s[q], odG[q].rearrange("r p t -> (r p) t"), GROUPS)

                Kt = [sbt("Kt%d" % i, [128, SEQ], BF16) for i in range(2)]
                Vt = [sbt("Vt%d" % i, [128, SEQ // 128, 65], BF16) for i in range(2)]
                Qt = [sbt("Qt%d" % i, [128, NT], BF16) for i in range(2)]
                for kq in Kt + Qt:
                    S.op(S.pool, lambda e: e.memset(kq.ap[64:128, :], 0.0), writes=(kq.parts if hasattr(kq, "parts") else [kq.b]))
                for kq in Kt:
                    kq.parts = [Buf() for _ in range(4)]
                for v in Vt:
                    v.parts = [Buf() for _ in range(16)]
                    S.op(S.pool, lambda e: e.memset(v.ap[:, :, 64:65], 1.0), writes=v.parts)
                pt = [sbt("pt%d" % i, [128, 512], BF16) for i in range(4)]
                osb = [sbt("osb%d" % i, [128, 264], F32) for i in range(2)]
                identf = sbt("identf", [128, 128], F32)
                S.dma(S.sp, identf.ap[:], I["cmat"][3], writes=[cb])
                obf = [sbt("obf%d" % i, [64, 512], BF16) for i in range(2)]
                sbank = psum[0:4]
                obank = psum[4:6]
                bbank = psum[6]
                cnt = {"q": 0, "o": 0, "kv": 0}
                lru_gen = conv_lru()

                class CastRes:
                    pass
                CR = CastRes()
                cst = [(sbt("cst%d" % i, [128, 1024], F32), sbt("cstb%d" % i, [128, 1024], BF16)) for i in range(2)]
                cn = {"s": 0, "e": 0}

                def _stage():
                    cn["s"] += 1
                    return cst[cn["s"] % 2]

                def _eng():
                    cn["e"] += 1
                    return (S.pool, S.dve)[cn["e"] % 2]
                CR.cast_stage = _stage
                CR.next_cast_eng = _eng
                def _cast_all():
                    yield from cast_layer(CR, l, False, True)
                    if l + 1 < NL:
                        yield from cast_layer(CR, l + 1, True, False)
                cast_gen = _cast_all()

                pending = []

                def flush_fin():
                    while pending:
                        pending.pop(0)()

                def attn_head(Kb, Vb, Qb, d, out_ap, dk):
                    scale = float(d) ** -0.5
                    for qg_ in range(NT // 512):
                        qcs = slice(qg_ * 512, (qg_ + 1) * 512)
                        po = obank[cnt["o"] % 2]
                        ob_ = obf[cnt["o"] % 2]
                        os_ = osb[cnt["o"] % 2]
                        cnt["o"] += 1

                        def s_mm(kb):
                            ps = sbank[kb % 4]
                            S.op(S.pe, lambda e: e.matmul(ps.ap[:], lhsT=Kb.ap[0:dk, kb * 128:(kb + 1) * 128], rhs=Qb.ap[0:dk, qcs], start=True, stop=True),
                                 reads=[Kb.parts[kb // 32], Qb.b], writes=[ps.b])
                        nkb = SEQ // 128
                        s_mm(0); s_mm(1); s_mm(2)
                        for kb in range(nkb):
                            ps = sbank[kb % 4]
                            p_ = pt[kb % 4]
                            S.op(S.act, lambda e: e.activation(out=p_.ap[:], in_=ps.ap[:], func=AF.Exp, scale=scale), reads=[ps.b], writes=[p_.b])
                            if kb + 3 < nkb:
                                s_mm(kb + 3)
                            for j in range(4):
                                S.op(S.pe, lambda e: e.matmul(po.ap[:, j * 65:(j + 1) * 65], lhsT=p_.ap[:, j * 128:(j + 1) * 128], rhs=Vb.ap[:, kb, 0:65],
                                                              start=(kb == 0 and j == 0), stop=(kb == nkb - 1), skip_group_check=True),
                                     reads=[Vb.parts[kb // 8], p_.b], writes=[po.b])
                            if kb == 8:
                                flush_fin()
                        pov = po.ap[:, 0:260].rearrange("p (j c) -> p j c", c=65)
                        S.op(S.dve, lambda e: e.reciprocal(out=os_.ap[:, 256:260], in_=pov[:, :, 64]), reads=[po.b], writes=[os_.b])
                        for j in range(4):
                            S.op(S.dve, lambda e: e.tensor_scalar(out=os_.ap[:, j * 64:(j + 1) * 64], in0=po.ap[:, j * 65:j * 65 + 64],
                                                                  scalar1=os_.ap[:, 256 + j:257 + j], scalar2=None, op0=ALU.mult),
                                 reads=[po.b, os_.b], writes=[os_.b])

                        def fin(os_=os_, ob_=ob_, out_ap=out_ap, qcs=qcs):
                            for j in range(4):
                                S.op(S.pe, lambda e: e.transpose(out=bbank.ap[0:64, j * 128:(j + 1) * 128], in_=os_.ap[:, j * 64:(j + 1) * 64], identity=identf.ap[:]),
                                     reads=[os_.b, cb], writes=[bbank.b])
                            S.op(S.dve, lambda e: e.tensor_copy(out=ob_.ap[:], in_=bbank.ap[0:64, :]), reads=[bbank.b], writes=[ob_.b])
                            S.dma(S.pool, out_ap[:, qcs], ob_.ap[:], reads=[ob_.b])
                        pending.append(fin)
                        v_some(3)
                        if cnt["o"] > 32:
                            next(lru_gen, None)
                            next(lru_gen, None)
                        for _ in range(3):
                            next(cast_gen, None)

                jobs = []
                for kvh in range(2):
                    for g in range(4):
                        h = kvh * 4 + g
                        jobs.append(("g", kvh, h, g == 0))
                for h in range(8):
                    jobs.append(("m", h, h, True))
                jstate = {}
                vloads = []

                def v_some(n):
                    for _ in range(n):
                        if vloads:
                            vloads.pop(0)()

                def prefetch(i):
                    kind, kvi, h, newkv = jobs[i]
                    if newkv:
                        Kb = Kt[cnt["kv"] % 2]
                        Vb = Vt[cnt["kv"] % 2]
                        cnt["kv"] += 1
                        if kind == "g":
                            S._wait(S.sp, ("cc", S.ccsem, cc0 + 2))
                            for r in range(4):
                                S.dma(S.sp, Kb.ap[0:64, r * NT:(r + 1) * NT], kgG[r, kvi * 64:(kvi + 1) * 64, :], writes=[Kb.parts[r]])
                                for hb in range(2):
                                    vloads.append(lambda r=r, hb=hb, Vb=Vb, kvi=kvi: S.dma(
                                        S.sp, Vb.ap[:, r * 32 + hb * 16:r * 32 + hb * 16 + 16, 0:64],
                                        vgG[r, hb * 2048:(hb + 1) * 2048, kvi * 64:(kvi + 1) * 64].rearrange("(b p) c -> p b c", p=128),
                                        writes=[Vb.parts[r * 4 + hb * 2], Vb.parts[r * 4 + hb * 2 + 1]]))
                        else:
                            S._wait(S.sp, ("cc", S.ccsem, cc0 + 7 + h))
                            for r in range(4):
                                S.dma(S.sp, Kb.ap[0:96, r * NT:(r + 1) * NT], kmG[h, r], writes=[Kb.parts[r]])
                                for ck in range(4):
                                    vloads.append(lambda r=r, ck=ck, Vb=Vb, h=h: S.dma(
                                        S.sp, Vb.ap[:, r * 32 + ck * 8:r * 32 + ck * 8 + 8, 0:64],
                                        vmG[ck, r, :, h * 64:(h + 1) * 64].rearrange("(b p) c -> p b c", p=128), writes=[Vb.parts[r * 4 + ck]]))
                        jstate["kv"] = (Kb, Vb)
                    Qb = Qt[cnt["q"] % 2]
                    cnt["q"] += 1
                    d = 64 if kind == "g" else 96
                    S.dma(S.sp, Qb.ap[0:d, :], (qg if kind == "g" else qm)[h], writes=[Qb.b])
                    return jstate["kv"] + (Qb,)

                nxt = prefetch(0)
                v_some(99)
                for i, (kind, kvi, h, newkv) in enumerate(jobs):
                    Kb, Vb, Qb = nxt
                    v_some(99)
                    if i + 1 < len(jobs):
                        nxt = prefetch(i + 1)
                    if kind == "g":
                        attn_head(Kb, Vb, Qb, 64, oaT[h * 64:(h + 1) * 64, :], 128)
                    else:
                        attn_head(Kb, Vb, Qb, 96, ocT[h * 64:(h + 1) * 64, :], 96)
                flush_fin()
                for _ in lru_gen:
                    pass
                for _ in cast_gen:
                    pass
                S.barrier()

        tok_phase(None, 0, "t0_")
        for l in range(NL):
            p2_phase(l, "p%d_" % l)
            tok_phase(l, l + 1 if l + 1 < NL else None, "t%d_" % (l + 1))
        S.finish()
        print("build_fused ninst", S.ninst, flush=True)
    return nc


def _rope_perm(M, blocks):
    Rm = np.zeros((128, 128), np.float32)
    for (o, n) in blocks:
        hn = n // 2
        for i in range(hn):
            Rm[o + hn + i, o + i] = -1.0
            Rm[o + i, o + hn + i] = 1.0
    return Rm


def host_consts():
    bd64 = np.zeros((128, 128), np.float32)
    bd64[:64, :64] = 1.0
    bd64[64:, 64:] = 1.0
    permg = _rope_perm(128, [(0, 32), (32, 32), (64, 32), (96, 32)])
    permm = _rope_perm(96, [(64, 16), (80, 16)])
    ident = np.eye(128, dtype=np.float32)
    return np.stack([bd64, permg, permm, ident])


def host_rope_tables(tok0):
    t = np.arange(tok0, tok0 + NT)
    row = (t // 64).astype(np.float32)
    col = (t % 64).astype(np.float32)

    def ang(pos, dim):
        inv = (10000.0 ** (-np.arange(0, dim, 2, dtype=np.float32) / dim)).astype(np.float32)
        return (pos[:, None] * inv[None, :]).astype(np.float32)

    def tables(M, blocks):
        cos = np.ones((M, NT), np.float32)
        sin = np.zeros((M, NT), np.float32)
        for (o, n, pos) in blocks:
            a = ang(pos, n)
            c, s = np.cos(a).T, np.sin(a).T
            hn = n // 2
            cos[o:o + hn] = c
            cos[o + hn:o + n] = c
            sin[o:o + hn] = s
            sin[o + hn:o + n] = s
        return cos, sin
    cosg, sing = tables(128, [(0, 32, row), (32, 32, col), (64, 32, row), (96, 32, col)])
    cosm, sinm = tables(96, [(64, 16, row), (80, 16, col)])
    return cosg, sing, cosm, sinm


NCORES = 8
_PROG = []


def _c(a):
    return np.ascontiguousarray(a)


def kernel(**inp):
    inp = {k: np.asarray(v) for k, v in inp.items()}
    x = inp["x"]
    if not _PROG:
        _PROG.append(build_fused())
    cm = host_consts()
    maps = []
    for c in range(NCORES):
        b, j = c // 4, c % 4
        ch = slice(128 * j, 128 * (j + 1))
        m = {"xT": _c(x[b, j * NT:(j + 1) * NT].T), "cmat": cm}
        m["cosg"], m["sing"], m["cosm"], m["sinm"] = host_rope_tables(j * NT)
        for nm in WNAMES:
            m[nm] = _c(inp[nm])
        m["scw"] = _c(np.concatenate([inp["sc_conv_w"][:, :, ch].transpose(0, 2, 1), inp["sc_conv_b"][:, ch][:, :, None]], axis=2))
        m["lruw"] = _c(np.concatenate([inp["lru_conv_w"][:, :, ch].transpose(0, 2, 1), inp["lru_conv_b"][:, ch][:, :, None]], axis=2))
        gm = np.zeros((NL, 4, 128, 128), np.float32)
        for dr in range(2):
            for k, nm in enumerate(("lru_wa", "lru_wx")):
                for blk in range(2):
                    gm[:, 2 * dr + k, 64 * blk:64 * (blk + 1), 64 * blk:64 * (blk + 1)] = inp[nm][:, dr, 2 * j + blk]
        m["lrug"] = gm
        m["lrub"] = _c(np.stack([inp["lru_ba"][:, 0, ch], inp["lru_bx"][:, 0, ch], inp["lru_lambda"][:, 0, ch],
                                 inp["lru_ba"][:, 1, ch], inp["lru_bx"][:, 1, ch], inp["lru_lambda"][:, 1, ch]], axis=2))
        maps.append(m)
    res = run_bass_kernel_spmd(_PROG[0], maps, core_ids=list(range(NCORES))).results
    out = np.empty((B_, SEQ, D), np.float32)
    for c in range(NCORES):
        b, j = c // 4, c % 4
        out[b, j * NT:(j + 1) * NT] = res[c]["xoutT"].T
    return out
```
